# Optimizing a Trainium2 kernel written in Bass

```python
import math
import jax
import jax.numpy as jnp
from jax import lax
import numpy as np

D_MODEL = 2048
BATCH = 2
SEQ = 4096
DEPTH = 2

N_A_LAYERS = max(1, DEPTH // 2)
N_B_LAYERS = DEPTH - N_A_LAYERS
S5_GROUP_CH = 16
S5_GROUPS = D_MODEL // S5_GROUP_CH
S5_STATE = 64
HEAD_DIM = 128
N_HEADS = D_MODEL // HEAD_DIM
N_KV_HEADS = 4
DILATED_PATTERNS = ((128, 1), (512, 4), (2048, 16))
N_GROUPS = len(DILATED_PATTERNS)
FFN_HIDDEN = ((8 * D_MODEL // 3 + 255) // 256) * 256
EPS = 1e-6
NEG_INF = -1e30

kernel_name = "yoco_s5_dilated_attention_hybrid"


def rms_norm(x, g):
    xf = x.astype(jnp.float32)
    r = lax.rsqrt(jnp.mean(xf * xf, axis=-1, keepdims=True) + EPS)
    return (xf * r * g.astype(jnp.float32)).astype(x.dtype)


def swiglu(h, w_in, w_out):
    gate, up = jnp.split(h @ w_in, 2, axis=-1)
    return (jax.nn.silu(gate) * up) @ w_out


def s5_mixer(u, lam_re, lam_im, log_dt, b_re, b_im, c_re, c_im, d_skip, w_glu):
    bsz, seq, dm = u.shape
    f32 = jnp.float32
    ug = u.astype(f32).reshape(bsz, seq, S5_GROUPS, S5_GROUP_CH)
    lr = lam_re.astype(f32)
    li = lam_im.astype(f32)
    dt = jnp.exp(log_dt.astype(f32))[:, None]
    mag = jnp.exp(lr * dt)
    ang = li * dt
    lb_re = mag * jnp.cos(ang)
    lb_im = mag * jnp.sin(ang)
    nr = lb_re - 1.0
    den = lr * lr + li * li
    f_re = (nr * lr + lb_im * li) / den
    f_im = (lb_im * lr - nr * li) / den
    br = b_re.astype(f32)
    bi = b_im.astype(f32)
    bb_re = f_re[..., None] * br - f_im[..., None] * bi
    bb_im = f_re[..., None] * bi + f_im[..., None] * br
    bu_re = jnp.einsum('bsgc,gpc->bsgp', ug, bb_re)
    bu_im = jnp.einsum('bsgc,gpc->bsgp', ug, bb_im)
    a_re = jnp.broadcast_to(lb_re, bu_re.shape)
    a_im = jnp.broadcast_to(lb_im, bu_im.shape)

    def combine(e1, e2):
        a1r, a1i, b1r, b1i = e1
        a2r, a2i, b2r, b2i = e2
        return (a2r * a1r - a2i * a1i,
                a2r * a1i + a2i * a1r,
                a2r * b1r - a2i * b1i + b2r,
                a2r * b1i + a2i * b1r + b2i)

    _, _, xs_re, xs_im = lax.associative_scan(combine, (a_re, a_im, bu_re, bu_im), axis=1)
    y = (jnp.einsum('bsgp,gcp->bsgc', xs_re, c_re.astype(f32))
         - jnp.einsum('bsgp,gcp->bsgc', xs_im, c_im.astype(f32)))
    y = y.reshape(bsz, seq, dm) + d_skip.astype(f32) * u.astype(f32)
    z = jax.nn.gelu(y).astype(u.dtype)
    val, gate = jnp.split(z @ w_glu, 2, axis=-1)
    return val * jax.nn.sigmoid(gate)


def dilated_window_attention(q, k, v, window, dilation):
    bsz, seq, nh, hd = q.shape
    nkv = k.shape[2]
    rep = nh // nkv
    n = seq // dilation
    blk = window // dilation
    nb = -(-n // blk)
    pad = nb * blk - n

    def to_residue(t):
        h = t.shape[2]
        t = t.reshape(bsz, n, dilation, h, hd).transpose(0, 2, 1, 3, 4)
        t = t.reshape(bsz * dilation, n, h, hd)
        t = jnp.pad(t, ((0, 0), (0, pad), (0, 0), (0, 0)))
        return t.reshape(bsz * dilation, nb, blk, h, hd)

    def with_prev(t):
        prev = jnp.pad(t, ((0, 0), (1, 0), (0, 0), (0, 0), (0, 0)))[:, :-1]
        return jnp.concatenate([prev, t], axis=2)

    qr = to_residue(q).reshape(bsz * dilation, nb, blk, nkv, rep, hd)
    kb = with_prev(to_residue(k))
    vb = with_prev(to_residue(v))
    scores = jnp.einsum('znqkge,znske->znkgqs', qr, kb).astype(jnp.float32) * (hd ** -0.5)
    qi = jnp.arange(blk)[:, None]
    si = jnp.arange(2 * blk)[None, :]
    dist = qi + blk - si
    band = (dist >= 0) & (dist <= blk)
    valid = band[None] & ((jnp.arange(nb)[:, None, None] > 0) | (si[None] >= blk))
    scores = jnp.where(valid[None, :, None, None], scores, NEG_INF)
    m = jnp.max(scores, axis=-1, keepdims=True)
    p = jnp.exp(scores - m)
    l = jnp.sum(p, axis=-1, keepdims=True)
    out = jnp.einsum('znkgqs,znske->znqkge', (p / l).astype(v.dtype), vb)
    lse = (m + jnp.log(l))[..., 0]
    out = out.reshape(bsz * dilation, nb * blk, nh, hd)[:, :n]
    out = out.reshape(bsz, dilation, n, nh, hd).transpose(0, 2, 1, 3, 4).reshape(bsz, seq, nh, hd)
    lse = lse.transpose(0, 1, 4, 2, 3).reshape(bsz * dilation, nb * blk, nh)[:, :n]
    lse = lse.reshape(bsz, dilation, n, nh).transpose(0, 2, 1, 3).reshape(bsz, seq, nh)
    return out, lse


def dilated_mixer(h, k, v, w_q, w_o):
    bsz, seq, _ = h.shape
    q = (h @ w_q).reshape(bsz, seq, N_GROUPS, N_HEADS, HEAD_DIM)
    outs = []
    lses = []
    for g, (window, dilation) in enumerate(DILATED_PATTERNS):
        o, l = dilated_window_attention(q[:, :, g], k, v, window, dilation)
        outs.append(o)
        lses.append(l)
    wts = jax.nn.softmax(jnp.stack(lses, axis=0), axis=0)
    o = jnp.sum(wts[..., None] * jnp.stack(outs, axis=0).astype(jnp.float32), axis=0)
    return o.astype(h.dtype).reshape(bsz, seq, N_HEADS * HEAD_DIM) @ w_o


def setup_inputs(seed: int = 0) -> dict:
    key = jax.random.key(seed)
    ks = jax.random.split(key, 24)
    f32 = jnp.float32

    def nrm(k, shape, scale):
        return jax.random.normal(k, shape, f32) * scale

    na, nbl, D, F = N_A_LAYERS, N_B_LAYERS, D_MODEL, FFN_HIDDEN
    G, P, C = S5_GROUPS, S5_STATE, S5_GROUP_CH
    x = nrm(ks[0], (BATCH, SEQ, D), 1.0)
    n_idx = jnp.arange(P, dtype=f32)
    s5_lam_re = -0.5 + nrm(ks[1], (na, G, P), 0.01)
    s5_lam_im = math.pi * n_idx + nrm(ks[2], (na, G, P), 0.01)
    s5_log_dt = jax.random.uniform(ks[3], (na, G), f32, math.log(1e-3), math.log(1e-1))
    s5_b_re = nrm(ks[4], (na, G, P, C), (2 * C) ** -0.5)
    s5_b_im = nrm(ks[5], (na, G, P, C), (2 * C) ** -0.5)
    s5_c_re = nrm(ks[6], (na, G, C, P), (2 * P) ** -0.5)
    s5_c_im = nrm(ks[7], (na, G, C, P), (2 * P) ** -0.5)
    s5_d = nrm(ks[8], (na, D), 1.0)
    s5_w_glu = nrm(ks[9], (na, D, 2 * D), D ** -0.5)
    a_norm_mix = 1.0 + nrm(ks[10], (na, D), 0.02)
    ffn_norm = 1.0 + nrm(ks[11], (DEPTH, D), 0.02)
    ffn_w_in = nrm(ks[12], (DEPTH, D, 2 * F), D ** -0.5)
    ffn_w_out = nrm(ks[13], (DEPTH, F, D), F ** -0.5)
    b_norm_mix = 1.0 + nrm(ks[14], (nbl, D), 0.02)
    attn_w_q = nrm(ks[15], (nbl, D, N_GROUPS * N_HEADS * HEAD_DIM), D ** -0.5)
    attn_w_o = nrm(ks[16], (nbl, N_HEADS * HEAD_DIM, D), (N_HEADS * HEAD_DIM) ** -0.5)
    kv_norm = 1.0 + nrm(ks[17], (D,), 0.02)
    w_kv = nrm(ks[18], (D, 2 * N_KV_HEADS * HEAD_DIM), D ** -0.5)
    final_norm = 1.0 + nrm(ks[19], (D,), 0.02)
    return {"x": x, "s5_lam_re": s5_lam_re, "s5_lam_im": s5_lam_im, "s5_log_dt": s5_log_dt,
            "s5_b_re": s5_b_re, "s5_b_im": s5_b_im, "s5_c_re": s5_c_re, "s5_c_im": s5_c_im,
            "s5_d": s5_d, "s5_w_glu": s5_w_glu, "a_norm_mix": a_norm_mix,
            "ffn_norm": ffn_norm, "ffn_w_in": ffn_w_in, "ffn_w_out": ffn_w_out,
            "b_norm_mix": b_norm_mix, "attn_w_q": attn_w_q, "attn_w_o": attn_w_o,
            "kv_norm": kv_norm, "w_kv": w_kv, "final_norm": final_norm}


def reference(x, s5_lam_re, s5_lam_im, s5_log_dt, s5_b_re, s5_b_im, s5_c_re, s5_c_im,
              s5_d, s5_w_glu, a_norm_mix, ffn_norm, ffn_w_in, ffn_w_out,
              b_norm_mix, attn_w_q, attn_w_o, kv_norm, w_kv, final_norm):
    bsz, seq, _ = x.shape
    k = None
    v = None
    for layer in range(DEPTH):
        if layer < N_A_LAYERS:
            i = layer
            h = rms_norm(x, a_norm_mix[i])
            x = x + s5_mixer(h, s5_lam_re[i], s5_lam_im[i], s5_log_dt[i], s5_b_re[i], s5_b_im[i],
                             s5_c_re[i], s5_c_im[i], s5_d[i], s5_w_glu[i])
        else:
            j = layer - N_A_LAYERS
            if j == 0:
                kv = (rms_norm(x, kv_norm) @ w_kv).reshape(bsz, seq, 2, N_KV_HEADS, HEAD_DIM)
                k = kv[:, :, 0]
                v = kv[:, :, 1]
            h = rms_norm(x, b_norm_mix[j])
            x = x + dilated_mixer(h, k, v, attn_w_q[j], attn_w_o[j])
        x = x + swiglu(rms_norm(x, ffn_norm[layer]), ffn_w_in[layer], ffn_w_out[layer])
    return rms_norm(x, final_norm)
```

```python
import math
from contextlib import ExitStack
import numpy as np
import concourse.bass as bass
import concourse.mybir as mybir
from concourse.bass_utils import run_bass_kernel_spmd

F32 = mybir.dt.float32
BF16 = mybir.dt.bfloat16
AF = mybir.ActivationFunctionType
ALU = mybir.AluOpType

D = 2048
NCH = 16
T = 1024
NTB = 2
FF = 5632
NF = 44
EPS = 1e-6
TBS = 64
NPAIR = 64


class Sched:
    def __init__(self, nc, st):
        self.nc = nc
        self.eng = {"pe": nc.tensor, "act": nc.scalar, "dve": nc.vector, "pool": nc.gpsimd, "sp": nc.sync}
        self.sem = {}
        self.cnt = {}
        for e in ["pe", "act", "dve", "pool"]:
            self.sem["c_" + e] = st.enter_context(nc.semaphore("c_" + e))
            self.cnt["c_" + e] = 0
        for q in ["sp", "act", "pool"]:
            self.sem["d_" + q] = st.enter_context(nc.semaphore("d_" + q))
            self.cnt["d_" + q] = 0
        self.seen = {e: {} for e in self.eng}
        self.last_w = {}
        self.readers = {}
        self.stopped = False
        _CUR[0] = self

    def _wait(self, e, deps):
        need = {}
        for (c, v) in deps:
            if c == "c_" + e and e == "pe":
                continue
            if v > need.get(c, 0):
                need[c] = v
        for c, v in need.items():
            if self.seen[e].get(c, 0) >= v:
                continue
            self.eng[e].wait_ge(self.sem[c], v)
            self.seen[e][c] = v

    def _deps(self, r, w):
        deps = []
        for k in r:
            if k in self.last_w:
                deps.append(self.last_w[k])
        for k in w:
            if k in self.last_w:
                deps.append(self.last_w[k])
            deps.extend(self.readers.get(k, []))
        return deps

    def _record(self, tok, r, w):
        for k in w:
            self.last_w[k] = tok
            self.readers[k] = []
        for k in r:
            self.readers.setdefault(k, []).append(tok)

    def op(self, e, fn, r=(), w=()):
        if self.stopped:
            return
        self._wait(e, self._deps(r, w))
        ins = fn()
        c = "c_" + e
        self.cnt[c] += 1
        ins.then_inc(self.sem[c], 1)
        self._record((c, self.cnt[c]), r, w)

    def group(self, e, fns, r=(), w=()):
        if self.stopped:
            return
        self._wait(e, self._deps(r, w))
        ins = None
        for f in fns:
            ins = f()
        c = "c_" + e
        self.cnt[c] += 1
        ins.then_inc(self.sem[c], 1)
        self._record((c, self.cnt[c]), r, w)

    def dma(self, q, out, in_, r=(), w=(), **kw):
        if self.stopped:
            return
        self._wait(q, self._deps(r, w))
        ins = self.eng[q].dma_start(out=out, in_=in_, **kw)
        c = "d_" + q
        self.cnt[c] += 16
        ins.then_inc(self.sem[c], 16)
        self._record((c, self.cnt[c]), r, w)

    def barrier(self):
        if self.stopped:
            return
        for e in self.eng:
            deps = [(c, v) for c, v in self.cnt.items() if v > 0]
            self._wait(e, deps)
        self.last_w = {}
        self.readers = {}

    def finish(self):
        deps = [(c, v) for c, v in self.cnt.items() if v > 0]
        self._wait("sp", deps)


def _dram_in(nc, name, shape, dt=F32):
    return nc.dram_tensor(name, list(shape), dt, kind="ExternalInput").ap()


def _dram_out(nc, name, shape, dt=F32):
    return nc.dram_tensor(name, list(shape), dt, kind="ExternalOutput").ap()


class Ctx:
    pass


class _Stop(Exception):
    pass


import os
STOP = int(os.environ.get("L0_STOP", "0"))


_CUR = [None]


def ck(k):
    if STOP == k:
        _CUR[0].stopped = True


def rmsnorm(S, nc, C, gain, hkey="h", dst=None):
    for tb in range(NTB):
        sl = slice(tb * 512, (tb + 1) * 512)
        ss = C.ps[6 + tb]
        fns = []
        for c in range(NCH):
            sq = C.sq[c % 2]
            S.op("act", lambda c=c, sq=sq: nc.scalar.activation(sq[:], C.x[:, c, sl], AF.Square),
                 r=["x"], w=[("sq", c % 2)])
            S.op("pe", lambda c=c, sq=sq: nc.tensor.matmul(ss[:], C.ones[:], sq[:], start=(c == 0), stop=(c == NCH - 1)),
                 r=[("sq", c % 2), "ones"], w=[("ps", 6 + tb)])
        S.op("act", lambda: nc.scalar.activation(C.rstd[:], ss[:], AF.Sqrt, bias=C.epsc[:, 0:1], scale=1.0 / D),
             r=[("ps", 6 + tb)], w=["rstd"])
        S.op("dve", lambda: nc.vector.reciprocal(C.rstd[:], C.rstd[:]), r=["rstd"], w=["rstd"])
        for c in range(NCH):
            S.op("dve", lambda c=c: nc.vector.scalar_tensor_tensor(
                (C.h if dst is None else dst)[:, c, sl], C.x[:, c, sl], gain[:, c:c + 1], C.rstd[:], ALU.mult, ALU.mult),
                r=["x", "rstd", "gains"], w=[hkey])


def load_cast(S, nc, C, src_ap, dst, dkey, shape3=None):
    i = C.stg_i
    C.stg_i += 1
    s = i % len(C.stg)
    stg = C.stg[s]
    sv = stg[:] if shape3 is None else stg[:].rearrange("p (k n) -> p k n", k=shape3[0])
    S.dma("sp", sv, src_ap, w=[("stg", s)])
    dv = dst
    S.op("act", lambda: nc.scalar.copy(dv, sv), r=[("stg", s)], w=[dkey])


def linear_fm(S, nc, C, w_ap, n_out_chunks, col0, src, skey, consume):
    for oc in range(n_out_chunks):
        wb = C.wbuf[C.wb_i % len(C.wbuf)]
        wk = ("wb", C.wb_i % len(C.wbuf))
        C.wb_i += 1
        c0 = col0 + oc * 128
        load_cast(S, nc, C, w_ap[:, c0:c0 + 128].rearrange("(k p) n -> p k n", p=128),
                  wb[:], wk, shape3=(NCH, 128))
        for tb in range(NTB):
            pi = C.lin_ps[C.ps_i % len(C.lin_ps)]
            C.ps_i += 1
            pst = C.ps[pi]
            sl = slice(tb * 512, (tb + 1) * 512)
            S.group("pe", [lambda k=k: nc.tensor.matmul(pst[:], wb[:, k, :], src[:, k, sl],
                                                        start=(k == 0), stop=(k == NCH - 1))
                           for k in range(NCH)],
                    r=[wk, skey], w=[("ps", pi)])
            consume(oc, tb, pst, ("ps", pi))


def ffn(S, nc, C, w_in, w_out, wo, hid, sg):
    V = nc.vector
    GS = 2
    for g in range(NF // GS):
        for fi in range(GS):
            f = g * GS + fi
            hb = hid[f % 4]
            hk = ("hid", f % 4)
            wob = wo[f % 4]
            load_cast(S, nc, C, w_out[f * 128:(f + 1) * 128, :], wob[:], ("wo", f % 4))

            def ffn_consume(oc, tb, pst, pk, hb=hb, hk=hk):
                sl = slice(tb * 512, (tb + 1) * 512)
                if oc == 0:
                    S.op("act", lambda: nc.scalar.activation(sg[tb][:], pst[:], AF.Silu), r=[pk], w=[("sg", tb)])
                else:
                    S.op("dve", lambda: V.tensor_tensor(hb[:, sl], sg[tb][:], pst[:], ALU.mult),
                         r=[pk, ("sg", tb)], w=[hk])
            linear_fm(S, nc, C, w_in, 1, f * 128, C.h, "h",
                      lambda oc, tb, pst, pk: ffn_consume(0, tb, pst, pk))
            linear_fm(S, nc, C, w_in, 1, FF + f * 128, C.h, "h",
                      lambda oc, tb, pst, pk: ffn_consume(1, tb, pst, pk))
        for c in range(NCH):
            for tb in range(NTB):
                sl = slice(tb * 512, (tb + 1) * 512)
                pi = C.out_ps[C.ps_i % 2]
                C.ps_i += 1
                fns = []
                rk = []
                for fi in range(GS):
                    f = g * GS + fi
                    rk += [("wo", f % 4), ("hid", f % 4)]
                    fns.append(lambda f=f, fi=fi, pi=pi, c=c, sl=sl: nc.tensor.matmul(
                        C.ps[pi][:], wo[f % 4][:, c * 128:(c + 1) * 128], hid[f % 4][:, sl],
                        start=(fi == 0), stop=(fi == GS - 1)))
                S.group("pe", fns, r=rk, w=[("ps", pi)])
                S.op("dve", lambda pi=pi, c=c, sl=sl: V.tensor_tensor(C.x[:, c, sl], C.x[:, c, sl], C.ps[pi][:], ALU.add),
                     r=[("ps", pi), "x"], w=["x"])


def build_l0(state_only):
    nc = bass.Bass("TRN2", target_bir_lowering=False)
    xT = _dram_in(nc, "xT", [D, T])
    lam_re = _dram_in(nc, "lam_re", [128, 64])
    lam_im = _dram_in(nc, "lam_im", [128, 64])
    log_dt = _dram_in(nc, "log_dt", [128, 1])
    b_re = _dram_in(nc, "b_re", [128, 64, 16])
    b_im = _dram_in(nc, "b_im", [128, 64, 16])
    c_re = _dram_in(nc, "c_re", [2048, 64])
    c_im = _dram_in(nc, "c_im", [2048, 64])
    gains = _dram_in(nc, "gains", [128, 4, NCH])
    xin = _dram_in(nc, "xin", [128, 2, NPAIR])
    ident_d = _dram_in(nc, "ident", [128, 128])
    mask_d = _dram_in(nc, "mask_sm", [128, 128])
    xend = _dram_out(nc, "xend", [128, 2, NPAIR])
    if not state_only:
        w_glu = _dram_in(nc, "w_glu", [D, 2 * D])
        w_in = _dram_in(nc, "w_in", [D, 2 * FF])
        w_out = _dram_in(nc, "w_out", [FF, D])
        w_kv = _dram_in(nc, "w_kv", [D, 1024])
        x1T = _dram_out(nc, "x1T", [D, T])
        kT = _dram_out(nc, "kT", [512, T], BF16)
        vT = _dram_out(nc, "vT", [512, T], BF16)

    with ExitStack() as st:
        E = st.enter_context
        S = Sched(nc, st)
        C = Ctx()
        C.x = E(nc.sbuf_tensor("x", [128, NCH, T], F32))
        C.h = E(nc.sbuf_tensor("h", [128, NCH, T], BF16))
        C.ones = E(nc.sbuf_tensor("ones", [128, 128], BF16))
        C.ident = E(nc.sbuf_tensor("identS", [128, 128], F32))
        C.mask = E(nc.sbuf_tensor("maskS", [128, 128], F32))
        C.gn = E(nc.sbuf_tensor("gn", [128, 4, NCH], F32))
        C.epsc = E(nc.sbuf_tensor("epsc", [128, 1], F32))
        C.rstd = E(nc.sbuf_tensor("rstd", [128, 512], F32))
        C.sq = [E(nc.sbuf_tensor(f"sq{i}", [128, 512], BF16)) for i in range(2)]
        C.ps = [E(nc.psum_tensor(f"ps{i}", [128, 512], F32)) for i in range(8)]
        C.Xc = E(nc.sbuf_tensor("Xc", [128, 2, NPAIR], F32))
        st01 = ExitStack()
        E01 = st01.enter_context
        C.A1 = E01(nc.sbuf_tensor("A1", [128, 2, NPAIR], F32))
        C.A2 = E01(nc.sbuf_tensor("A2", [128, 2, NPAIR], F32))
        C.lbu = E01(nc.sbuf_tensor("lbu", [128, NCH, 2, 128], BF16))
        C.cz = E01(nc.sbuf_tensor("cz", [128, NCH, 2, 128], BF16))
        C.dg = E01(nc.sbuf_tensor("dg", [128, NCH, 128], BF16))

        try:
            S.op("dve", lambda: nc.vector.memset(C.ones[:], 1.0), w=["ones"])
            S.op("dve", lambda: nc.vector.memset(C.epsc[:], EPS), w=["epsc"])
            S.dma("sp", C.ident[:], ident_d, w=["ident"])
            S.dma("sp", C.mask[:], mask_d, w=["mask"])
            S.dma("sp", C.gn[:], gains, w=["gains"])
            S.dma("sp", C.Xc[:], xin, w=["Xc"])
            for c in range(NCH):
                S.dma("sp", C.x[:, c, :], xT[c * 128:(c + 1) * 128, :], w=["x"])

            ck(1)
            with ExitStack() as st0:
                E0 = st0.enter_context
                lr = E0(nc.sbuf_tensor("lr", [128, 64], F32))
                li = E0(nc.sbuf_tensor("li", [128, 64], F32))
                ldt = E0(nc.sbuf_tensor("ldt", [128, 1], F32))
                dt = E0(nc.sbuf_tensor("dt", [128, 1], F32))
                mag = E0(nc.sbuf_tensor("mag", [128, 64], F32))
                ang = E0(nc.sbuf_tensor("ang", [128, 64], F32))
                angc = E0(nc.sbuf_tensor("angc", [128, 64], F32))
                tm = E0(nc.sbuf_tensor("tm", [128, 64], F32))
                tm2 = E0(nc.sbuf_tensor("tm2", [128, 64], F32))
                nr = E0(nc.sbuf_tensor("nr", [128, 64], F32))
                rden = E0(nc.sbuf_tensor("rden", [128, 64], F32))
                T4 = E0(nc.sbuf_tensor("T4", [128, 4, 2, 64], F32))
                S4 = E0(nc.sbuf_tensor("S4", [128, 4, 128], F32))
                Bz = E0(nc.sbuf_tensor("Bz", [128, 2, 128, 16], F32))
                Bb = E0(nc.sbuf_tensor("Bb", [128, 2, 128, 16], F32))
                tb1 = E0(nc.sbuf_tensor("tb1", [128, 128, 16], F32))
                Cin = E0(nc.sbuf_tensor("Cin", [128, 2, NCH, 128], F32))
                dgf = E0(nc.sbuf_tensor("dgf", [128, 128], F32))

                S.dma("sp", lr[:], lam_re, w=["lr"])
                S.dma("sp", li[:], lam_im, w=["li"])
                S.dma("sp", ldt[:], log_dt, w=["ldt"])
                for ri, src in enumerate([b_re, b_im]):
                    for hh in range(2):
                        for gh in range(2):
                            S.dma("sp", Bz[hh * 64:(hh + 1) * 64, ri, gh * 64:(gh + 1) * 64, :],
                                  src[gh * 64:(gh + 1) * 64].rearrange("g p c -> p g c"), w=["Bz"])
                for ri, src in enumerate([c_re, c_im]):
                    for dup in range(2):
                        S.dma("sp", Cin[:, ri, :, dup * 64:(dup + 1) * 64],
                              src.rearrange("(k q) p -> q k p", q=128), w=["Cin"])

                ck(2)
                V = nc.vector
                S.op("act", lambda: nc.scalar.activation(dt[:], ldt[:], AF.Exp), r=["ldt"], w=["dt"])
                S.op("act", lambda: nc.scalar.activation(mag[:], lr[:], AF.Exp, scale=dt[:, 0:1]), r=["lr", "dt"], w=["mag"])
                S.op("dve", lambda: V.tensor_scalar(ang[:], li[:], dt[:, 0:1], None, ALU.mult), r=["li", "dt"], w=["ang"])
                S.op("dve", lambda: V.tensor_scalar(angc[:], ang[:], math.pi / 2, None, ALU.add), r=["ang"], w=["angc"])
                for a_, key in ((ang, "ang"), (angc, "angc")):
                    for _ in range(4):
                        S.op("dve", lambda a_=a_: V.tensor_scalar(tm[:], a_[:], math.pi, 2 * math.pi, ALU.is_gt, ALU.mult),
                             r=[key], w=["tm"])
                        S.op("dve", lambda a_=a_: V.tensor_tensor(a_[:], a_[:], tm[:], ALU.subtract), r=["tm", key], w=[key])
                S.op("act", lambda: nc.scalar.activation(ang[:], ang[:], AF.Sin), r=["ang"], w=["ang"])
                S.op("act", lambda: nc.scalar.activation(angc[:], angc[:], AF.Sin), r=["angc"], w=["angc"])
                for dup in range(2):
                    S.op("dve", lambda dup=dup: V.tensor_tensor(T4[:, 0, dup, :], mag[:], angc[:], ALU.mult),
                         r=["mag", "angc"], w=["T4"])
                    S.op("dve", lambda dup=dup: V.tensor_tensor(T4[:, 1, dup, :], mag[:], ang[:], ALU.mult),
                         r=["mag", "ang"], w=["T4"])
                a_n = T4[:, 0, 0, :]
                b_n = T4[:, 1, 0, :]
                S.op("dve", lambda: V.tensor_scalar(nr[:], a_n, -1.0, None, ALU.add), r=["T4"], w=["nr"])
                S.op("dve", lambda: V.tensor_tensor(tm[:], lr[:], lr[:], ALU.mult), r=["lr"], w=["tm"])
                S.op("dve", lambda: V.tensor_tensor(tm2[:], li[:], li[:], ALU.mult), r=["li"], w=["tm2"])
                S.op("dve", lambda: V.tensor_tensor(tm[:], tm[:], tm2[:], ALU.add), r=["tm", "tm2"], w=["tm"])
                S.op("dve", lambda: V.reciprocal(rden[:], tm[:]), r=["tm"], w=["rden"])
                S.op("dve", lambda: V.tensor_tensor(tm[:], nr[:], lr[:], ALU.mult), r=["nr", "lr"], w=["tm"])
                S.op("dve", lambda: V.tensor_tensor(tm2[:], b_n, li[:], ALU.mult), r=["T4", "li"], w=["tm2"])
                S.op("dve", lambda: V.tensor_tensor(tm[:], tm[:], tm2[:], ALU.add), r=["tm", "tm2"], w=["tm"])
                for dup in range(2):
                    S.op("dve", lambda dup=dup: V.tensor_tensor(T4[:, 2, dup, :], tm[:], rden[:], ALU.mult),
                         r=["tm", "rden"], w=["T4"])
                S.op("dve", lambda: V.tensor_tensor(tm[:], b_n, lr[:], ALU.mult), r=["T4", "lr"], w=["tm"])
                S.op("dve", lambda: V.tensor_tensor(tm2[:], nr[:], li[:], ALU.mult), r=["nr", "li"], w=["tm2"])
                S.op("dve", lambda: V.tensor_tensor(tm[:], tm[:], tm2[:], ALU.subtract), r=["tm", "tm2"], w=["tm"])
                for dup in range(2):
                    S.op("dve", lambda dup=dup: V.tensor_tensor(T4[:, 3, dup, :], tm[:], rden[:], ALU.mult),
                         r=["tm", "rden"], w=["T4"])
                ck(3)
                for q in range(4):
                    S.op("pe", lambda q=q: nc.tensor.transpose(C.ps[0][:, 0:128], T4[:, q, :, :].rearrange("g a p -> g (a p)"), C.ident[:]),
                         r=["T4", "ident"], w=[("ps", 0)])
                    S.op("dve", lambda q=q: V.tensor_copy(S4[:, q, :], C.ps[0][:, 0:128]), r=[("ps", 0)], w=["S4"])
                for hh in range(2):
                    ps_ = slice(hh * 64, (hh + 1) * 64)
                    for ri in range(2):
                        S.op("dve", lambda hh=hh, ps_=ps_, ri=ri: V.tensor_copy(C.A1[ps_, ri, :], S4[ps_, 0, hh::2]),
                             r=["S4"], w=["A1"])
                    S.op("dve", lambda hh=hh, ps_=ps_: V.tensor_scalar(C.A2[ps_, 0, :], S4[ps_, 1, hh::2], -1.0, None, ALU.mult),
                         r=["S4"], w=["A2"])
                    S.op("dve", lambda hh=hh, ps_=ps_: V.tensor_copy(C.A2[ps_, 1, :], S4[ps_, 1, hh::2]),
                         r=["S4"], w=["A2"])
                ck(4)
                fre = S4[:, 2, :].unsqueeze(2).broadcast_to([128, 128, 16])
                fim = S4[:, 3, :].unsqueeze(2).broadcast_to([128, 128, 16])
                S.op("dve", lambda: V.tensor_tensor(Bb[:, 0], Bz[:, 0], fre, ALU.mult), r=["Bz", "S4"], w=["Bb"])
                S.op("dve", lambda: V.tensor_tensor(tb1[:], Bz[:, 1], fim, ALU.mult), r=["Bz", "S4"], w=["tb1"])
                S.op("dve", lambda: V.tensor_tensor(Bb[:, 0], Bb[:, 0], tb1[:], ALU.subtract), r=["tb1", "Bb"], w=["Bb"])
                S.op("dve", lambda: V.tensor_tensor(Bb[:, 1], Bz[:, 1], fre, ALU.mult), r=["Bz", "S4"], w=["Bb"])
                S.op("dve", lambda: V.tensor_tensor(tb1[:], Bz[:, 0], fim, ALU.mult), r=["Bz", "S4"], w=["tb1"])
                S.op("dve", lambda: V.tensor_tensor(Bb[:, 1], Bb[:, 1], tb1[:], ALU.add), r=["tb1", "Bb"], w=["Bb"])
                mk = C.mask[:].rearrange("p (g c) -> p g c", c=16)
                for ri in range(2):
                    for ch in range(NCH):
                        S.op("dve", lambda ri=ri, ch=ch: V.tensor_tensor(
                            Bb[:, ri, ch * 8:(ch + 1) * 8, :], Bb[:, ri, ch * 8:(ch + 1) * 8, :], mk, ALU.mult),
                            r=["Bb", "mask"], w=["Bb"])
                ck(5)
                for ri in range(2):
                    for ch in range(NCH):
                        pi = (ri * NCH + ch) % 4
                        S.op("pe", lambda ri=ri, ch=ch, pi=pi: nc.tensor.transpose(
                            C.ps[pi][:, 0:128], Bb[:, ri, ch * 8:(ch + 1) * 8, :].rearrange("p g c -> p (g c)"), C.ident[:]),
                            r=["Bb", "ident"], w=[("ps", pi)])
                        S.op("act", lambda ri=ri, ch=ch, pi=pi: nc.scalar.copy(C.lbu[:, ch, ri, :], C.ps[pi][:, 0:128]),
                             r=[("ps", pi)], w=["lbu"])
                for ri in range(2):
                    for ch in range(NCH):
                        pi = (ri * NCH + ch) % 4
                        S.op("pe", lambda ri=ri, ch=ch, pi=pi: nc.tensor.transpose(
                            C.ps[pi][:, 0:128], Cin[:, ri, ch, :], C.ident[:]),
                            r=["Cin", "ident"], w=[("ps", pi)])
                        if ri == 0:
                            S.op("dve", lambda ri=ri, ch=ch, pi=pi: V.tensor_tensor(
                                C.cz[:, ch, ri, :], C.ps[pi][:, 0:128], C.mask[:], ALU.mult),
                                r=[("ps", pi), "mask"], w=["cz"])
                        else:
                            S.op("dve", lambda ri=ri, ch=ch, pi=pi: V.scalar_tensor_tensor(
                                C.cz[:, ch, ri, :], C.ps[pi][:, 0:128], -1.0, C.mask[:], ALU.mult, ALU.mult),
                                r=[("ps", pi), "mask"], w=["cz"])
                for ch in range(NCH):
                    S.op("dve", lambda ch=ch: V.tensor_scalar(C.dg[:, ch, :], C.ident[:], C.gn[:, 3, ch:ch + 1], None, ALU.mult),
                         r=["ident", "gains"], w=["dg"])
                S.barrier()

            ck(6)
            rmsnorm(S, nc, C, C.gn[:, 0, :])

            ck(7)
            with ExitStack() as st1:
                E1 = st1.enter_context
                Bf = E1(nc.sbuf_tensor("Bf", [128, 2, NPAIR, TBS], F32))
                t1 = E1(nc.sbuf_tensor("t1", [128, 2, NPAIR], F32))
                t2 = E1(nc.sbuf_tensor("t2", [128, 2, NPAIR], F32))
                Xb = E1(nc.sbuf_tensor("Xb", [128, 2, NPAIR, TBS], BF16))
                ysb = E1(nc.sbuf_tensor("ysb", [128, 512], F32))
                y2 = E1(nc.sbuf_tensor("y2", [128, 512], F32))
                y3 = E1(nc.sbuf_tensor("y3", [128, 512], F32))
                V = nc.vector
                nblk = int(os.environ.get("L0_NBLK", T // TBS))
                for blk in range(nblk):
                    tsl = slice(blk * TBS, (blk + 1) * TBS)
                    for ri in range(2):
                        for half in range(2):
                            fns = []
                            for cc in range(8):
                                ch = half * 8 + cc
                                for j in range(4):
                                    fns.append(lambda cc=cc, ch=ch, j=j, ri=ri: nc.tensor.matmul(
                                        C.ps[j][:, cc * TBS:(cc + 1) * TBS],
                                        C.lbu[32 * j:32 * j + 32, ch, ri, :], C.h[32 * j:32 * j + 32, ch, tsl],
                                        start=True, stop=True, tile_position=(32 * j, 0)))
                            S.group("pe", fns, r=["lbu", "h"], w=[("ps", 0), ("ps", 1), ("ps", 2), ("ps", 3)])
                            for j in range(4):
                                S.op("act", lambda j=j, ri=ri, half=half: nc.scalar.copy(
                                    Bf[:, ri, half * 32 + j:half * 32 + 32:4, :],
                                    C.ps[j][:].rearrange("p (a t) -> p a t", t=TBS)),
                                    r=[("ps", j)], w=["Bf"])
                    for t in range(TBS):
                        prev = C.Xc[:] if t == 0 else Bf[:, :, :, t - 1]
                        cur = Bf[:, :, :, t]
                        S.op("dve", lambda prev=prev: V.tensor_tensor(t1[:], C.A1[:], prev, ALU.mult),
                             r=["Bf", "Xc", "A1"], w=["t1"])
                        for ri in range(2):
                            pv = C.Xc[:, 1 - ri, :] if t == 0 else Bf[:, 1 - ri, :, t - 1]
                            S.op("dve", lambda pv=pv, ri=ri: V.tensor_tensor(t2[:, ri, :], C.A2[:, ri, :], pv, ALU.mult),
                                 r=["Bf", "Xc", "A2"], w=["t2"])
                        S.op("dve", lambda cur=cur: V.tensor_tensor(cur, cur, t1[:], ALU.add), r=["t1", "Bf"], w=["Bf"])
                        S.op("dve", lambda cur=cur: V.tensor_tensor(cur, cur, t2[:], ALU.add), r=["t2", "Bf"], w=["Bf"])
                    S.op("dve", lambda: V.tensor_copy(C.Xc[:], Bf[:, :, :, TBS - 1]), r=["Bf"], w=["Xc"])
                    if state_only:
                        continue
                    S.op("act", lambda: nc.scalar.copy(Xb[:], Bf[:]), r=["Bf"], w=["Xb"])
                    for half in range(2):
                        pi = 4 + half
                        fns = []
                        for cc in range(8):
                            ch = half * 8 + cc
                            osl = slice(cc * TBS, (cc + 1) * TBS)
                            fns.append(lambda ch=ch, osl=osl, pi=pi: nc.tensor.matmul(
                                C.ps[pi][:, osl], C.dg[:, ch, :], C.h[:, ch, tsl], start=(ch % 8 == 0), stop=False))
                            for j in range(4):
                                pair = ch * 4 + j
                                for ri in range(2):
                                    fns.append(lambda ch=ch, osl=osl, pi=pi, j=j, ri=ri, pair=pair: nc.tensor.matmul(
                                        C.ps[pi][32 * j:32 * j + 32, osl], C.cz[:, ch, ri, 32 * j:32 * j + 32],
                                        Xb[:, ri, pair, :], start=False, stop=(ri == 1 and j == 3),
                                        tile_position=(0, 32 * j), skip_group_check=True))
                        S.group("pe", fns, r=["dg", "h", "cz", "Xb"], w=[("ps", pi)])
                        yp = C.ps[pi]
                        S.op("act", lambda yp=yp: nc.scalar.copy(ysb[:], yp[:]), r=[("ps", pi)], w=["ysb"])
                        S.op("dve", lambda: V.tensor_tensor(y2[:], ysb[:], ysb[:], ALU.mult), r=["ysb"], w=["y2"])
                        S.op("dve", lambda: V.tensor_scalar(y2[:], y2[:], 0.044715, 1.0, ALU.mult, ALU.add), r=["y2"], w=["y2"])
                        S.op("dve", lambda: V.tensor_tensor(y2[:], y2[:], ysb[:], ALU.mult), r=["y2", "ysb"], w=["y2"])
                        S.op("act", lambda: nc.scalar.activation(y3[:], y2[:], AF.Sigmoid, scale=1.5957691216057308),
                             r=["y2"], w=["y3"])
                        S.op("dve", lambda half=half: V.tensor_tensor(
                            C.h[:, half * 8:(half + 1) * 8, tsl], ysb[:].rearrange("p (a t) -> p a t", t=TBS),
                            y3[:].rearrange("p (a t) -> p a t", t=TBS), ALU.mult),
                            r=["ysb", "y3", "h"], w=["h"])
                S.dma("sp", xend, C.Xc[:], r=["Xc"], w=["xend_d"])
                S.barrier()
            st01.close()

            if not state_only:
                with ExitStack() as st2:
                    E2 = st2.enter_context
                    C.stg = [E2(nc.sbuf_tensor(f"stg{i}", [128, 2048], F32)) for i in range(3)]
                    C.stg_i = 0
                    C.wbuf = [E2(nc.sbuf_tensor(f"wb{i}", [128, NCH, 128], BF16)) for i in range(4)]
                    C.wb_i = 0
                    C.ps_i = 0
                    C.lin_ps = [0, 1, 2, 3]
                    C.out_ps = [4, 5]
                    wo = [E2(nc.sbuf_tensor(f"wo{i}", [128, 2048], BF16)) for i in range(4)]
                    hid = [E2(nc.sbuf_tensor(f"hid{i}", [128, T], BF16)) for i in range(4)]
                    sg = [E2(nc.sbuf_tensor(f"sg{i}", [128, 512], F32)) for i in range(2)]
                    val = [E2(nc.sbuf_tensor(f"val{i}", [128, 512], F32)) for i in range(2)]
                    V = nc.vector
                    st_ = {}

                    def glu_consume(oc, tb, pst, pk):
                        sl = slice(tb * 512, (tb + 1) * 512)
                        if oc % 2 == 0:
                            i = tb
                            S.op("act", lambda: nc.scalar.copy(val[i][:], pst[:]), r=[pk], w=[("val", i)])
                        else:
                            i = tb
                            c = oc // 2
                            S.op("act", lambda: nc.scalar.activation(sg[i][:], pst[:], AF.Sigmoid), r=[pk], w=[("sg", i)])
                            S.op("dve", lambda: V.tensor_tensor(sg[i][:], sg[i][:], val[i][:], ALU.mult),
                                 r=[("sg", i), ("val", i)], w=[("sg", i)])
                            S.op("dve", lambda: V.tensor_tensor(C.x[:, c, sl], C.x[:, c, sl], sg[i][:], ALU.add),
                                 r=[("sg", i), "x"], w=["x"])

                    for c in range(NCH):
                        for which in range(2):
                            linear_fm(S, nc, C, w_glu, 1, which * D + c * 128, C.h, "h",
                                      lambda oc, tb, pst, pk, which=which, c=c: glu_consume(2 * c + which, tb, pst, pk))
                    rmsnorm(S, nc, C, C.gn[:, 1, :])
                    ffn(S, nc, C, w_in, w_out, wo, hid, sg)
                    for c in range(NCH):
                        S.dma("sp", x1T[c * 128:(c + 1) * 128, :], C.x[:, c, :], r=["x"], w=["x1_d"])
                    rmsnorm(S, nc, C, C.gn[:, 2, :])
                    kvb = [E2(nc.sbuf_tensor(f"kvb{i}", [128, 512], BF16)) for i in range(2)]

                    def kv_consume(oc, tb, pst, pk):
                        sl = slice(tb * 512, (tb + 1) * 512)
                        i = (oc * 2 + tb) % 2
                        S.op("act", lambda: nc.scalar.copy(kvb[i][:], pst[:]), r=[pk], w=[("kvb", i)])
                        dst = kT if oc < 4 else vT
                        o4 = oc % 4
                        S.dma("sp", dst[o4 * 128:(o4 + 1) * 128, sl], kvb[i][:], r=[("kvb", i)], w=["kv_d"])
                    linear_fm(S, nc, C, w_kv, 8, 0, C.h, "h", kv_consume)
                    S.barrier()
        except _Stop:
            pass
        S.finish()
    return nc


def _key_sets():
    ks = []
    for kb in range(-1, 8):
        k0 = 128 * kb
        qs, qe = max(0, k0), min(T, k0 + 256)
        ks.append((k0 + 2048, 1, 128, qs, 1, qe - qs, qs - k0, 1 if kb < 0 else 0, 0))
    for r in range(4):
        for kc in (-128, 0, 128):
            qs, qe = max(0, kc), min(256, kc + 256)
            ks.append((r + 4 * kc + 2048, 4, 128, r + 4 * qs, 4, qe - qs, qs - kc, 1 if kc < 0 else 0, 1))
    for r in range(16):
        ks.append((r, 16, 128, r, 16, 64, 128, 2, 2))
        ks.append((r + 2048, 16, 64, r, 16, 64, 0, 0, 2))
    return ks


def build_l1():
    nc = bass.Bass("TRN2", target_bir_lowering=False)
    xT = _dram_in(nc, "xT", [D, T])
    kTa = _dram_in(nc, "kTa", [512, 3072], BF16)
    Va = _dram_in(nc, "Va", [3072, 512], BF16)
    gains = _dram_in(nc, "gains", [128, 4, NCH])
    bandm = _dram_in(nc, "bandm", [128, 256], BF16)
    kbias = _dram_in(nc, "kbias", [128, 4])
    w_q = _dram_in(nc, "w_q", [D, 3 * D])
    w_o = _dram_in(nc, "w_o", [D, D])
    w_in = _dram_in(nc, "w_in", [D, 2 * FF])
    w_out = _dram_in(nc, "w_out", [FF, D])
    outT = _dram_out(nc, "outT", [D, T])
    KS = _key_sets()
    with ExitStack() as st:
        E = st.enter_context
        S = Sched(nc, st)
        C = Ctx()
        V = nc.vector
        C.x = E(nc.sbuf_tensor("x", [128, NCH, T], F32))
        C.h = E(nc.sbuf_tensor("h", [128, NCH, T], BF16))
        C.ones = E(nc.sbuf_tensor("ones", [128, 128], BF16))
        C.gn = E(nc.sbuf_tensor("gn", [128, 4, NCH], F32))
        C.epsc = E(nc.sbuf_tensor("epsc", [128, 1], F32))
        C.rstd = E(nc.sbuf_tensor("rstd", [128, 512], F32))
        C.sq = [E(nc.sbuf_tensor(f"sq{i}", [128, 512], BF16)) for i in range(2)]
        C.ps = [E(nc.psum_tensor(f"ps{i}", [128, 512], F32)) for i in range(8)]
        S.op("dve", lambda: V.memset(C.ones[:], 1.0), w=["ones"])
        S.op("dve", lambda: V.memset(C.epsc[:], EPS), w=["epsc"])
        S.dma("sp", C.gn[:], gains, w=["gains"])
        for c in range(NCH):
            S.dma("sp", C.x[:, c, :], xT[c * 128:(c + 1) * 128, :], w=["x"])
        rmsnorm(S, nc, C, C.gn[:, 0, :])
        with ExitStack() as sa:
            EA = sa.enter_context
            attn = EA(nc.sbuf_tensor("attn", [128, NCH, T], BF16))
            with ExitStack() as sb:
                EB = sb.enter_context
                C.stg = [EB(nc.sbuf_tensor(f"stg{i}", [128, 2048], F32)) for i in range(3)]
                C.stg_i = 0
                C.wbuf = [EB(nc.sbuf_tensor(f"wb{i}", [128, NCH, 128], BF16)) for i in range(3)]
                C.wb_i = 0
                C.ps_i = 0
                C.lin_ps = [6, 7]
                mask = EB(nc.sbuf_tensor("bandS", [128, 256], BF16))
                kb_ = EB(nc.sbuf_tensor("kbS", [128, 4], F32))
                kts = EB(nc.sbuf_tensor("kts", [128, 3072], BF16))
                Vt = EB(nc.sbuf_tensor("Vt", [128, len(KS), 128], BF16))
                qg = [EB(nc.sbuf_tensor(f"qg{i}", [128, T], BF16)) for i in range(3)]
                pe_ = [EB(nc.sbuf_tensor(f"pe{i}", [128, 256], BF16)) for i in range(2)]
                pT = [EB(nc.sbuf_tensor(f"pT{i}", [128, 256], BF16)) for i in range(2)]
                rec = EB(nc.sbuf_tensor("rec", [128, 512], F32))
                S.dma("sp", mask[:], bandm, w=["mask"])
                S.dma("sp", kb_[:], kbias, w=["kb"])
                it = 0
                for hh in range(16):
                    kvh = hh // 4
                    if hh % 4 == 0:
                        S.dma("sp", kts[:], kTa[kvh * 128:(kvh + 1) * 128, :], w=["kts"])
                        for i, (k0, kst, m, q0, qst, n, off, bc, g) in enumerate(KS):
                            S.dma("sp", Vt[0:m, i, :], Va[k0:k0 + kst * (m - 1) + 1:kst, kvh * 128:(kvh + 1) * 128],
                                  w=["Vt"])
                    for g in range(3):
                        def q_consume(oc, tb, pst, pk, g=g):
                            sl = slice(tb * 512, (tb + 1) * 512)
                            S.op("act", lambda: nc.scalar.activation(qg[g][:, sl], pst[:], AF.Copy, scale=HD_SCALE),
                                 r=[pk], w=[("qg", g)])
                        linear_fm(S, nc, C, w_q, 1, g * D + hh * 128, C.h, "h", q_consume)
                    first = {0: True, 1: True}
                    for i, (k0, kst, m, q0, qst, n, off, bc, g) in enumerate(KS):
                        si = 4 + (it % 2)
                        pi_ = it % 2
                        it += 1
                        sps = C.ps[si]
                        S.group("pe", [lambda: nc.tensor.matmul(
                            sps[0:m, 0:n], kts[:, k0:k0 + kst * (m - 1) + 1:kst],
                            qg[g][:, q0:q0 + qst * (n - 1) + 1:qst], start=True, stop=True)],
                            r=["kts", ("qg", g)], w=[("ps", si)])
                        S.op("act", lambda: nc.scalar.activation(pe_[pi_][0:m, 0:n], sps[0:m, 0:n], AF.Exp,
                                                                 bias=kb_[0:m, bc:bc + 1], scale=1.0),
                             r=[("ps", si), "kb"], w=[("pe", pi_)])
                        S.op("dve", lambda: V.tensor_tensor(pT[pi_][0:m, 0:n], pe_[pi_][0:m, 0:n], mask[0:m, off:off + n], ALU.mult),
                             r=[("pe", pi_), "mask"], w=[("pT", pi_)])
                        segs = []
                        isplit = min(n, max(0, -(-(512 - q0) // qst)))
                        if isplit > 0:
                            segs.append((0, isplit, 0))
                        if isplit < n:
                            segs.append((isplit, n, 1))
                        fns = []
                        wk = []
                        for (i0, i1, bank) in segs:
                            c0 = q0 + qst * i0 - 512 * bank
                            cs = slice(c0, c0 + qst * (i1 - i0 - 1) + 1, qst)
                            st_ = first[bank]
                            first[bank] = False
                            fns.append(lambda cs=cs, bank=bank, i0=i0, i1=i1, st_=st_: nc.tensor.matmul(
                                C.ps[bank][:, cs], Vt[0:m, i, :], pT[pi_][0:m, i0:i1], start=st_, stop=False,
                                skip_group_check=True))
                            fns.append(lambda cs=cs, bank=bank, i0=i0, i1=i1, st_=st_: nc.tensor.matmul(
                                C.ps[2 + bank][:, cs], C.ones[0:m, :], pT[pi_][0:m, i0:i1], start=st_, stop=False,
                                skip_group_check=True))
                            wk += [("ps", bank), ("ps", 2 + bank)]
                        S.group("pe", fns, r=[("pT", pi_), "Vt", "ones"], w=wk)
                    for bank in range(2):
                        sl = slice(bank * 512, (bank + 1) * 512)
                        S.op("dve", lambda: V.reciprocal(rec[:], C.ps[2 + bank][:]), r=[("ps", 2 + bank)], w=["rec"])
                        S.op("dve", lambda: V.tensor_tensor(attn[:, hh, sl], C.ps[bank][:], rec[:], ALU.mult),
                             r=[("ps", bank), "rec"], w=["attn"])
                S.barrier()
            with ExitStack() as sc:
                EC = sc.enter_context
                C.stg = [EC(nc.sbuf_tensor(f"stgc{i}", [128, 2048], F32)) for i in range(3)]
                C.stg_i = 0
                C.wbuf = [EC(nc.sbuf_tensor(f"wbc{i}", [128, NCH, 128], BF16)) for i in range(4)]
                C.wb_i = 0
                C.lin_ps = [0, 1, 2, 3]

                def o_consume(oc, tb, pst, pk):
                    sl = slice(tb * 512, (tb + 1) * 512)
                    S.op("dve", lambda: V.tensor_tensor(C.x[:, oc, sl], C.x[:, oc, sl], pst[:], ALU.add),
                         r=[pk, "x"], w=["x"])
                linear_fm(S, nc, C, w_o, NCH, 0, attn, "attn", o_consume)
                S.barrier()
        with ExitStack() as sd:
            ED = sd.enter_context
            C.stg = [ED(nc.sbuf_tensor(f"stgd{i}", [128, 2048], F32)) for i in range(3)]
            C.stg_i = 0
            C.wbuf = [ED(nc.sbuf_tensor(f"wbd{i}", [128, NCH, 128], BF16)) for i in range(4)]
            C.wb_i = 0
            C.lin_ps = [0, 1, 2, 3]
            C.out_ps = [4, 5]
            wo = [ED(nc.sbuf_tensor(f"wo{i}", [128, 2048], BF16)) for i in range(4)]
            hid = [ED(nc.sbuf_tensor(f"hid{i}", [128, T], BF16)) for i in range(4)]
            sg = [ED(nc.sbuf_tensor(f"sg{i}", [128, 512], F32)) for i in range(2)]
            rmsnorm(S, nc, C, C.gn[:, 1, :])
            ffn(S, nc, C, w_in, w_out, wo, hid, sg)
            rmsnorm(S, nc, C, C.gn[:, 2, :], hkey="x", dst=C.x)
            for c in range(NCH):
                S.dma("sp", outT[c * 128:(c + 1) * 128, :], C.x[:, c, :], r=["x"], w=["out_d"])
            S.barrier()
        S.finish()
    return nc


HD_SCALE = 128 ** -0.5

_cache = {}


def _prog(name, fn):
    if name not in _cache:
        _cache[name] = fn()
    return _cache[name]


def _fm(v):
    return np.ascontiguousarray(np.asarray(v, np.float32).reshape(NCH, 128).T)


def layer0(inputs):
    x = np.asarray(inputs["x"], np.float32)
    n = 8
    gains = np.ascontiguousarray(np.stack([_fm(inputs["a_norm_mix"][0]), _fm(inputs["ffn_norm"][0]),
                                           _fm(inputs["kv_norm"]), _fm(inputs["s5_d"][0])], axis=1))
    ident = np.eye(128, dtype=np.float32)
    hp = np.arange(128) // 64
    g8 = (np.arange(128) // 16) % 2
    mask_sm = (hp[:, None] == g8[None, :]).astype(np.float32)
    base = {
        "lam_re": np.ascontiguousarray(inputs["s5_lam_re"][0]), "lam_im": np.ascontiguousarray(inputs["s5_lam_im"][0]),
        "log_dt": np.ascontiguousarray(inputs["s5_log_dt"][0].reshape(128, 1)),
        "b_re": np.ascontiguousarray(inputs["s5_b_re"][0]), "b_im": np.ascontiguousarray(inputs["s5_b_im"][0]),
        "c_re": np.ascontiguousarray(inputs["s5_c_re"][0].reshape(2048, 64)),
        "c_im": np.ascontiguousarray(inputs["s5_c_im"][0].reshape(2048, 64)),
        "gains": gains, "ident": ident, "mask_sm": mask_sm,
    }
    xTs = []
    for c in range(n):
        b, j = c // 4, c % 4
        xTs.append(np.ascontiguousarray(x[b, j * T:(j + 1) * T, :].T))
    xin = [np.zeros((128, 2, NPAIR), np.float32) for _ in range(n)]
    nc_s = _prog("l0s", lambda: build_l0(True))
    for r in range(3):
        maps = [dict(base, xT=xTs[c], xin=xin[c]) for c in range(n)]
        res = run_bass_kernel_spmd(nc_s, maps, core_ids=list(range(n)))
        for c in range(n):
            if c % 4 == r:
                xin[c + 1] = np.ascontiguousarray(res.results[c]["xend"])
    nc_f = _prog("l0f", lambda: build_l0(False))
    full = dict(base, w_glu=np.ascontiguousarray(inputs["s5_w_glu"][0]),
                w_in=np.ascontiguousarray(inputs["ffn_w_in"][0]), w_out=np.ascontiguousarray(inputs["ffn_w_out"][0]),
                w_kv=np.ascontiguousarray(inputs["w_kv"]))
    maps = [dict(full, xT=xTs[c], xin=xin[c]) for c in range(n)]
    res = run_bass_kernel_spmd(nc_f, maps, core_ids=list(range(n)))
    return res.results


def layer1(inputs, r0):
    n = 8
    gains = np.ascontiguousarray(np.stack([_fm(inputs["b_norm_mix"][0]), _fm(inputs["ffn_norm"][1]),
                                           _fm(inputs["final_norm"]), _fm(inputs["final_norm"])], axis=1))
    bdt = np.asarray(r0[0]["kT"]).dtype
    sidx = np.arange(128)[:, None]
    qidx = np.arange(256)[None, :]
    bandm = ((sidx <= qidx) & (qidx <= sidx + 128)).astype(np.float32).astype(bdt)
    base = {"gains": gains, "bandm": bandm,
            "w_q": np.ascontiguousarray(inputs["attn_w_q"][0]), "w_o": np.ascontiguousarray(inputs["attn_w_o"][0]),
            "w_in": np.ascontiguousarray(inputs["ffn_w_in"][1]), "w_out": np.ascontiguousarray(inputs["ffn_w_out"][1])}
    maps = []
    for c in range(n):
        b, j = c // 4, c % 4
        kparts, vparts = [], []
        for jj in (j - 2, j - 1, j):
            if jj < 0:
                kparts.append(np.zeros((512, T), bdt))
                vparts.append(np.zeros((512, T), bdt))
            else:
                kparts.append(np.asarray(r0[b * 4 + jj]["kT"]))
                vparts.append(np.asarray(r0[b * 4 + jj]["vT"]))
        kTa = np.ascontiguousarray(np.concatenate(kparts, axis=1))
        Va = np.ascontiguousarray(np.concatenate(vparts, axis=1).T)
        kbias = np.zeros((128, 4), np.float32)
        if j == 0:
            kbias[:, 1] = -30000.0
            kbias[:, 2] = -30000.0
        elif j == 1:
            kbias[:64, 2] = -30000.0
        maps.append(dict(base, xT=np.ascontiguousarray(r0[c]["x1T"]), kTa=kTa, Va=Va, kbias=kbias))
    nc1 = _prog("l1", build_l1)
    res = run_bass_kernel_spmd(nc1, maps, core_ids=list(range(n)))
    return res.results


def kernel(**inputs):
    inputs = {k: np.asarray(v) for k, v in inputs.items()}
    r0 = layer0(inputs)
    r1 = layer1(inputs, r0)
    out = np.empty((2, 4096, D), np.float32)
    for c in range(8):
        b, j = c // 4, c % 4
        out[b, j * T:(j + 1) * T, :] = np.asarray(r1[c]["outT"]).T
    return out
```

```python
import math
from contextlib import ExitStack
import numpy as np
import concourse.bass as bass
import concourse.mybir as mybir
from concourse.bass_utils import run_bass_kernel_spmd

F32 = mybir.dt.float32
BF16 = mybir.dt.bfloat16
AF = mybir.ActivationFunctionType
ALU = mybir.AluOpType

D = 2048
NCH = 16
T = 1024
NTB = 2
FF = 5632
NF = 44
EPS = 1e-6
TBS = 64
NPAIR = 64


class Sched:
    def __init__(self, nc, st):
        self.nc = nc
        self.eng = {"pe": nc.tensor, "act": nc.scalar, "dve": nc.vector, "pool": nc.gpsimd, "sp": nc.sync}
        self.sem = {}
        self.cnt = {}
        for e in ["pe", "act", "dve", "pool"]:
            self.sem["c_" + e] = st.enter_context(nc.semaphore("c_" + e))
            self.cnt["c_" + e] = 0
        for q in ["sp", "act", "pool"]:
            self.sem["d_" + q] = st.enter_context(nc.semaphore("d_" + q))
            self.cnt["d_" + q] = 0
        self.seen = {e: {} for e in self.eng}
        self.last_w = {}
        self.readers = {}
        self.stopped = False
        _CUR[0] = self

    def _wait(self, e, deps):
        need = {}
        for (c, v) in deps:
            if c == "c_" + e and e == "pe":
                continue
            if v > need.get(c, 0):
                need[c] = v
        for c, v in need.items():
            if self.seen[e].get(c, 0) >= v:
                continue
            self.eng[e].wait_ge(self.sem[c], v)
            self.seen[e][c] = v

    def _deps(self, r, w):
        deps = []
        for k in r:
            if k in self.last_w:
                deps.append(self.last_w[k])
        for k in w:
            if k in self.last_w:
                deps.append(self.last_w[k])
            deps.extend(self.readers.get(k, []))
        return deps

    def _record(self, tok, r, w):
        for k in w:
            self.last_w[k] = tok
            self.readers[k] = []
        for k in r:
            self.readers.setdefault(k, []).append(tok)

    def op(self, e, fn, r=(), w=()):
        if self.stopped:
            return
        self._wait(e, self._deps(r, w))
        ins = fn()
        c = "c_" + e
        self.cnt[c] += 1
        ins.then_inc(self.sem[c], 1)
        self._record((c, self.cnt[c]), r, w)

    def group(self, e, fns, r=(), w=()):
        if self.stopped:
            return
        self._wait(e, self._deps(r, w))
        ins = None
        for f in fns:
            ins = f()
        c = "c_" + e
        self.cnt[c] += 1
        ins.then_inc(self.sem[c], 1)
        self._record((c, self.cnt[c]), r, w)

    def dma(self, q, out, in_, r=(), w=(), **kw):
        if self.stopped:
            return
        self._wait(q, self._deps(r, w))
        ins = self.eng[q].dma_start(out=out, in_=in_, **kw)
        c = "d_" + q
        self.cnt[c] += 16
        ins.then_inc(self.sem[c], 16)
        self._record((c, self.cnt[c]), r, w)

    def barrier(self):
        if self.stopped:
            return
        for e in self.eng:
            deps = [(c, v) for c, v in self.cnt.items() if v > 0]
            self._wait(e, deps)
        self.last_w = {}
        self.readers = {}

    def finish(self):
        deps = [(c, v) for c, v in self.cnt.items() if v > 0]
        self._wait("sp", deps)


def _dram_in(nc, name, shape, dt=F32):
    return nc.dram_tensor(name, list(shape), dt, kind="ExternalInput").ap()


def _dram_out(nc, name, shape, dt=F32):
    return nc.dram_tensor(name, list(shape), dt, kind="ExternalOutput").ap()


class Ctx:
    pass


class _Stop(Exception):
    pass


import os
STOP = int(os.environ.get("L0_STOP", "0"))


_CUR = [None]


def ck(k):
    if STOP == k:
        _CUR[0].stopped = True


def rmsnorm(S, nc, C, gain, hkey="h", dst=None):
    for tb in range(NTB):
        sl = slice(tb * 512, (tb + 1) * 512)
        ss = C.ps[6 + tb]
        fns = []
        for c in range(NCH):
            sq = C.sq[c % 2]
            S.op("act", lambda c=c, sq=sq: nc.scalar.activation(sq[:], C.x[:, c, sl], AF.Square),
                 r=["x"], w=[("sq", c % 2)])
            S.op("pe", lambda c=c, sq=sq: nc.tensor.matmul(ss[:], C.ones[:], sq[:], start=(c == 0), stop=(c == NCH - 1)),
                 r=[("sq", c % 2), "ones"], w=[("ps", 6 + tb)])
        S.op("act", lambda: nc.scalar.activation(C.rstd[:], ss[:], AF.Sqrt, bias=C.epsc[:, 0:1], scale=1.0 / D),
             r=[("ps", 6 + tb)], w=["rstd"])
        S.op("dve", lambda: nc.vector.reciprocal(C.rstd[:], C.rstd[:]), r=["rstd"], w=["rstd"])
        for c in range(NCH):
            S.op("dve", lambda c=c: nc.vector.scalar_tensor_tensor(
                (C.h if dst is None else dst)[:, c, sl], C.x[:, c, sl], gain[:, c:c + 1], C.rstd[:], ALU.mult, ALU.mult),
                r=["x", "rstd", "gains"], w=[hkey])


def load_cast(S, nc, C, src_ap, dst, dkey, shape3=None):
    i = C.stg_i
    C.stg_i += 1
    s = i % len(C.stg)
    stg = C.stg[s]
    sv = stg[:] if shape3 is None else stg[:].rearrange("p (k n) -> p k n", k=shape3[0])
    S.dma("sp", sv, src_ap, w=[("stg", s)])
    dv = dst
    S.op("act", lambda: nc.scalar.copy(dv, sv), r=[("stg", s)], w=[dkey])


def linear_fm(S, nc, C, w_ap, n_out_chunks, col0, src, skey, consume):
    for oc in range(n_out_chunks):
        wb = C.wbuf[C.wb_i % len(C.wbuf)]
        wk = ("wb", C.wb_i % len(C.wbuf))
        C.wb_i += 1
        c0 = col0 + oc * 128
        load_cast(S, nc, C, w_ap[:, c0:c0 + 128].rearrange("(k p) n -> p k n", p=128),
                  wb[:], wk, shape3=(NCH, 128))
        for tb in range(NTB):
            pi = C.lin_ps[C.ps_i % len(C.lin_ps)]
            C.ps_i += 1
            pst = C.ps[pi]
            sl = slice(tb * 512, (tb + 1) * 512)
            S.group("pe", [lambda k=k: nc.tensor.matmul(pst[:], wb[:, k, :], src[:, k, sl],
                                                        start=(k == 0), stop=(k == NCH - 1))
                           for k in range(NCH)],
                    r=[wk, skey], w=[("ps", pi)])
            consume(oc, tb, pst, ("ps", pi))


def ffn(S, nc, C, w_in, w_out, wo, hid, sg):
    V = nc.vector
    GS = 2
    for g in range(NF // GS):
        for fi in range(GS):
            f = g * GS + fi
            hb = hid[f % 4]
            hk = ("hid", f % 4)
            wob = wo[f % 4]
            load_cast(S, nc, C, w_out[f * 128:(f + 1) * 128, :], wob[:], ("wo", f % 4))

            def ffn_consume(oc, tb, pst, pk, hb=hb, hk=hk):
                sl = slice(tb * 512, (tb + 1) * 512)
                if oc == 0:
                    S.op("act", lambda: nc.scalar.activation(sg[tb][:], pst[:], AF.Silu), r=[pk], w=[("sg", tb)])
                else:
                    S.op("dve", lambda: V.tensor_tensor(hb[:, sl], sg[tb][:], pst[:], ALU.mult),
                         r=[pk, ("sg", tb)], w=[hk])
            linear_fm(S, nc, C, w_in, 1, f * 128, C.h, "h",
                      lambda oc, tb, pst, pk: ffn_consume(0, tb, pst, pk))
            linear_fm(S, nc, C, w_in, 1, FF + f * 128, C.h, "h",
                      lambda oc, tb, pst, pk: ffn_consume(1, tb, pst, pk))
        for c in range(NCH):
            for tb in range(NTB):
                sl = slice(tb * 512, (tb + 1) * 512)
                pi = C.out_ps[C.ps_i % 2]
                C.ps_i += 1
                fns = []
                rk = []
                for fi in range(GS):
                    f = g * GS + fi
                    rk += [("wo", f % 4), ("hid", f % 4)]
                    fns.append(lambda f=f, fi=fi, pi=pi, c=c, sl=sl: nc.tensor.matmul(
                        C.ps[pi][:], wo[f % 4][:, c * 128:(c + 1) * 128], hid[f % 4][:, sl],
                        start=(fi == 0), stop=(fi == GS - 1)))
                S.group("pe", fns, r=rk, w=[("ps", pi)])
                S.op("dve", lambda pi=pi, c=c, sl=sl: V.tensor_tensor(C.x[:, c, sl], C.x[:, c, sl], C.ps[pi][:], ALU.add),
                     r=[("ps", pi), "x"], w=["x"])


def build_l0(state_only):
    nc = bass.Bass("TRN2", target_bir_lowering=False)
    xT = _dram_in(nc, "xT", [D, T])
    lam_re = _dram_in(nc, "lam_re", [128, 64])
    lam_im = _dram_in(nc, "lam_im", [128, 64])
    log_dt = _dram_in(nc, "log_dt", [128, 1])
    b_re = _dram_in(nc, "b_re", [128, 64, 16])
    b_im = _dram_in(nc, "b_im", [128, 64, 16])
    c_re = _dram_in(nc, "c_re", [2048, 64])
    c_im = _dram_in(nc, "c_im", [2048, 64])
    gains = _dram_in(nc, "gains", [128, 4, NCH])
    eprev = _dram_in(nc, "eprev", [128, 3, 2, NPAIR])
    ident_d = _dram_in(nc, "ident", [128, 128])
    mask_d = _dram_in(nc, "mask_sm", [128, 128])
    xend = _dram_out(nc, "xend", [128, 2, NPAIR])
    if not state_only:
        w_glu = _dram_in(nc, "w_glu", [D, 2 * D])
        w_in = _dram_in(nc, "w_in", [D, 2 * FF])
        w_out = _dram_in(nc, "w_out", [FF, D])
        w_kv = _dram_in(nc, "w_kv", [D, 1024])
        x1T = _dram_out(nc, "x1T", [D, T])
        kT = _dram_out(nc, "kT", [512, T], BF16)
        vT = _dram_out(nc, "vT", [512, T], BF16)

    with ExitStack() as st:
        E = st.enter_context
        S = Sched(nc, st)
        C = Ctx()
        C.x = E(nc.sbuf_tensor("x", [128, NCH, T], F32))
        C.h = E(nc.sbuf_tensor("h", [128, NCH, T], BF16))
        C.ones = E(nc.sbuf_tensor("ones", [128, 128], BF16))
        C.ident = E(nc.sbuf_tensor("identS", [128, 128], F32))
        C.mask = E(nc.sbuf_tensor("maskS", [128, 128], F32))
        C.gn = E(nc.sbuf_tensor("gn", [128, 4, NCH], F32))
        C.epsc = E(nc.sbuf_tensor("epsc", [128, 1], F32))
        C.rstd = E(nc.sbuf_tensor("rstd", [128, 512], F32))
        C.sq = [E(nc.sbuf_tensor(f"sq{i}", [128, 512], BF16)) for i in range(2)]
        C.ps = [E(nc.psum_tensor(f"ps{i}", [128, 512], F32)) for i in range(8)]
        C.Xc = E(nc.sbuf_tensor("Xc", [128, 2, NPAIR], F32))
        st01 = ExitStack()
        E01 = st01.enter_context
        C.A1 = E01(nc.sbuf_tensor("A1", [128, 2, NPAIR], F32))
        C.A2 = E01(nc.sbuf_tensor("A2", [128, 2, NPAIR], F32))
        C.lbu = E01(nc.sbuf_tensor("lbu", [128, NCH, 2, 128], BF16))
        C.cz = E01(nc.sbuf_tensor("cz", [128, NCH, 2, 128], BF16))
        C.dg = E01(nc.sbuf_tensor("dg", [128, NCH, 128], BF16))

        try:
            S.op("dve", lambda: nc.vector.memset(C.ones[:], 1.0), w=["ones"])
            S.op("dve", lambda: nc.vector.memset(C.epsc[:], EPS), w=["epsc"])
            S.dma("sp", C.ident[:], ident_d, w=["ident"])
            S.dma("sp", C.mask[:], mask_d, w=["mask"])
            S.dma("sp", C.gn[:], gains, w=["gains"])
            for c in range(NCH):
                S.dma("sp", C.x[:, c, :], xT[c * 128:(c + 1) * 128, :], w=["x"])

            ck(1)
            with ExitStack() as st0:
                E0 = st0.enter_context
                lr = E0(nc.sbuf_tensor("lr", [128, 64], F32))
                li = E0(nc.sbuf_tensor("li", [128, 64], F32))
                ldt = E0(nc.sbuf_tensor("ldt", [128, 1], F32))
                dt = E0(nc.sbuf_tensor("dt", [128, 1], F32))
                mag = E0(nc.sbuf_tensor("mag", [128, 64], F32))
                ang = E0(nc.sbuf_tensor("ang", [128, 64], F32))
                angc = E0(nc.sbuf_tensor("angc", [128, 64], F32))
                tm = E0(nc.sbuf_tensor("tm", [128, 64], F32))
                tm2 = E0(nc.sbuf_tensor("tm2", [128, 64], F32))
                nr = E0(nc.sbuf_tensor("nr", [128, 64], F32))
                rden = E0(nc.sbuf_tensor("rden", [128, 64], F32))
                T4 = E0(nc.sbuf_tensor("T4", [128, 4, 2, 64], F32))
                S4 = E0(nc.sbuf_tensor("S4", [128, 4, 128], F32))
                Bz = E0(nc.sbuf_tensor("Bz", [128, 2, 128, 16], F32))
                Bb = E0(nc.sbuf_tensor("Bb", [128, 2, 128, 16], F32))
                tb1 = E0(nc.sbuf_tensor("tb1", [128, 128, 16], F32))
                Cin = E0(nc.sbuf_tensor("Cin", [128, 2, NCH, 128], F32))
                dgf = E0(nc.sbuf_tensor("dgf", [128, 128], F32))

                S.dma("sp", lr[:], lam_re, w=["lr"])
                S.dma("sp", li[:], lam_im, w=["li"])
                S.dma("sp", ldt[:], log_dt, w=["ldt"])
                for ri, src in enumerate([b_re, b_im]):
                    for hh in range(2):
                        for gh in range(2):
                            S.dma("sp", Bz[hh * 64:(hh + 1) * 64, ri, gh * 64:(gh + 1) * 64, :],
                                  src[gh * 64:(gh + 1) * 64].rearrange("g p c -> p g c"), w=["Bz"])
                for ri, src in enumerate([c_re, c_im]):
                    for dup in range(2):
                        S.dma("sp", Cin[:, ri, :, dup * 64:(dup + 1) * 64],
                              src.rearrange("(k q) p -> q k p", q=128), w=["Cin"])

                ck(2)
                V = nc.vector
                S.op("act", lambda: nc.scalar.activation(dt[:], ldt[:], AF.Exp), r=["ldt"], w=["dt"])
                S.op("act", lambda: nc.scalar.activation(mag[:], lr[:], AF.Exp, scale=dt[:, 0:1]), r=["lr", "dt"], w=["mag"])
                S.op("dve", lambda: V.tensor_scalar(ang[:], li[:], dt[:, 0:1], None, ALU.mult), r=["li", "dt"], w=["ang"])
                S.op("dve", lambda: V.tensor_scalar(angc[:], ang[:], math.pi / 2, None, ALU.add), r=["ang"], w=["angc"])
                for a_, key in ((ang, "ang"), (angc, "angc")):
                    for _ in range(4):
                        S.op("dve", lambda a_=a_: V.tensor_scalar(tm[:], a_[:], math.pi, 2 * math.pi, ALU.is_gt, ALU.mult),
                             r=[key], w=["tm"])
                        S.op("dve", lambda a_=a_: V.tensor_tensor(a_[:], a_[:], tm[:], ALU.subtract), r=["tm", key], w=[key])
                S.op("act", lambda: nc.scalar.activation(ang[:], ang[:], AF.Sin), r=["ang"], w=["ang"])
                S.op("act", lambda: nc.scalar.activation(angc[:], angc[:], AF.Sin), r=["angc"], w=["angc"])
                for dup in range(2):
                    S.op("dve", lambda dup=dup: V.tensor_tensor(T4[:, 0, dup, :], mag[:], angc[:], ALU.mult),
                         r=["mag", "angc"], w=["T4"])
                    S.op("dve", lambda dup=dup: V.tensor_tensor(T4[:, 1, dup, :], mag[:], ang[:], ALU.mult),
                         r=["mag", "ang"], w=["T4"])
                a_n = T4[:, 0, 0, :]
                b_n = T4[:, 1, 0, :]
                S.op("dve", lambda: V.tensor_scalar(nr[:], a_n, -1.0, None, ALU.add), r=["T4"], w=["nr"])
                S.op("dve", lambda: V.tensor_tensor(tm[:], lr[:], lr[:], ALU.mult), r=["lr"], w=["tm"])
                S.op("dve", lambda: V.tensor_tensor(tm2[:], li[:], li[:], ALU.mult), r=["li"], w=["tm2"])
                S.op("dve", lambda: V.tensor_tensor(tm[:], tm[:], tm2[:], ALU.add), r=["tm", "tm2"], w=["tm"])
                S.op("dve", lambda: V.reciprocal(rden[:], tm[:]), r=["tm"], w=["rden"])
                S.op("dve", lambda: V.tensor_tensor(tm[:], nr[:], lr[:], ALU.mult), r=["nr", "lr"], w=["tm"])
                S.op("dve", lambda: V.tensor_tensor(tm2[:], b_n, li[:], ALU.mult), r=["T4", "li"], w=["tm2"])
                S.op("dve", lambda: V.tensor_tensor(tm[:], tm[:], tm2[:], ALU.add), r=["tm", "tm2"], w=["tm"])
                for dup in range(2):
                    S.op("dve", lambda dup=dup: V.tensor_tensor(T4[:, 2, dup, :], tm[:], rden[:], ALU.mult),
                         r=["tm", "rden"], w=["T4"])
                S.op("dve", lambda: V.tensor_tensor(tm[:], b_n, lr[:], ALU.mult), r=["T4", "lr"], w=["tm"])
                S.op("dve", lambda: V.tensor_tensor(tm2[:], nr[:], li[:], ALU.mult), r=["nr", "li"], w=["tm2"])
                S.op("dve", lambda: V.tensor_tensor(tm[:], tm[:], tm2[:], ALU.subtract), r=["tm", "tm2"], w=["tm"])
                for dup in range(2):
                    S.op("dve", lambda dup=dup: V.tensor_tensor(T4[:, 3, dup, :], tm[:], rden[:], ALU.mult),
                         r=["tm", "rden"], w=["T4"])
                ck(3)
                for q in range(4):
                    S.op("pe", lambda q=q: nc.tensor.transpose(C.ps[0][:, 0:128], T4[:, q, :, :].rearrange("g a p -> g (a p)"), C.ident[:]),
                         r=["T4", "ident"], w=[("ps", 0)])
                    S.op("dve", lambda q=q: V.tensor_copy(S4[:, q, :], C.ps[0][:, 0:128]), r=[("ps", 0)], w=["S4"])
                for hh in range(2):
                    ps_ = slice(hh * 64, (hh + 1) * 64)
                    for ri in range(2):
                        S.op("dve", lambda hh=hh, ps_=ps_, ri=ri: V.tensor_copy(C.A1[ps_, ri, :], S4[ps_, 0, hh::2]),
                             r=["S4"], w=["A1"])
                    S.op("dve", lambda hh=hh, ps_=ps_: V.tensor_scalar(C.A2[ps_, 0, :], S4[ps_, 1, hh::2], -1.0, None, ALU.mult),
                         r=["S4"], w=["A2"])
                    S.op("dve", lambda hh=hh, ps_=ps_: V.tensor_copy(C.A2[ps_, 1, :], S4[ps_, 1, hh::2]),
                         r=["S4"], w=["A2"])
                Ep = E0(nc.sbuf_tensor("Ep", [128, 3, 2, NPAIR], F32))
                mr = E0(nc.sbuf_tensor("mr", [128, NPAIR], F32))
                mi = E0(nc.sbuf_tensor("mi", [128, NPAIR], F32))
                q1 = E0(nc.sbuf_tensor("q1", [128, NPAIR], F32))
                q2 = E0(nc.sbuf_tensor("q2", [128, NPAIR], F32))
                q3 = E0(nc.sbuf_tensor("q3", [128, NPAIR], F32))
                S.dma("sp", Ep[:], eprev, w=["Ep"])
                S.op("dve", lambda: V.tensor_copy(mr[:], C.A1[:, 0, :]), r=["A1"], w=["mr"])
                S.op("dve", lambda: V.tensor_copy(mi[:], C.A2[:, 1, :]), r=["A2"], w=["mi"])
                for _ in range(10):
                    S.op("dve", lambda: V.tensor_tensor(q1[:], mr[:], mr[:], ALU.mult), r=["mr"], w=["q1"])
                    S.op("dve", lambda: V.tensor_tensor(q2[:], mi[:], mi[:], ALU.mult), r=["mi"], w=["q2"])
                    S.op("dve", lambda: V.tensor_tensor(q3[:], mr[:], mi[:], ALU.mult), r=["mr", "mi"], w=["q3"])
                    S.op("dve", lambda: V.tensor_tensor(mr[:], q1[:], q2[:], ALU.subtract), r=["q1", "q2"], w=["mr"])
                    S.op("dve", lambda: V.tensor_scalar(mi[:], q3[:], 2.0, None, ALU.mult), r=["q3"], w=["mi"])
                S.op("dve", lambda: V.tensor_copy(C.Xc[:], Ep[:, 2]), r=["Ep"], w=["Xc"])
                for k in (1, 0):
                    xr, xi = C.Xc[:, 0, :], C.Xc[:, 1, :]
                    S.op("dve", lambda: V.tensor_tensor(q1[:], mr[:], xr, ALU.mult), r=["mr", "Xc"], w=["q1"])
                    S.op("dve", lambda: V.tensor_tensor(q2[:], mi[:], xi, ALU.mult), r=["mi", "Xc"], w=["q2"])
                    S.op("dve", lambda: V.tensor_tensor(q1[:], q1[:], q2[:], ALU.subtract), r=["q1", "q2"], w=["q1"])
                    S.op("dve", lambda: V.tensor_tensor(q2[:], mr[:], xi, ALU.mult), r=["mr", "Xc"], w=["q2"])
                    S.op("dve", lambda: V.tensor_tensor(q3[:], mi[:], xr, ALU.mult), r=["mi", "Xc"], w=["q3"])
                    S.op("dve", lambda: V.tensor_tensor(q2[:], q2[:], q3[:], ALU.add), r=["q2", "q3"], w=["q2"])
                    S.op("dve", lambda k=k: V.tensor_tensor(C.Xc[:, 0, :], q1[:], Ep[:, k, 0, :], ALU.add), r=["q1", "Ep"], w=["Xc"])
                    S.op("dve", lambda k=k: V.tensor_tensor(C.Xc[:, 1, :], q2[:], Ep[:, k, 1, :], ALU.add), r=["q2", "Ep"], w=["Xc"])
                ck(4)
                fre = S4[:, 2, :].unsqueeze(2).broadcast_to([128, 128, 16])
                fim = S4[:, 3, :].unsqueeze(2).broadcast_to([128, 128, 16])
                S.op("dve", lambda: V.tensor_tensor(Bb[:, 0], Bz[:, 0], fre, ALU.mult), r=["Bz", "S4"], w=["Bb"])
                S.op("dve", lambda: V.tensor_tensor(tb1[:], Bz[:, 1], fim, ALU.mult), r=["Bz", "S4"], w=["tb1"])
                S.op("dve", lambda: V.tensor_tensor(Bb[:, 0], Bb[:, 0], tb1[:], ALU.subtract), r=["tb1", "Bb"], w=["Bb"])
                S.op("dve", lambda: V.tensor_tensor(Bb[:, 1], Bz[:, 1], fre, ALU.mult), r=["Bz", "S4"], w=["Bb"])
                S.op("dve", lambda: V.tensor_tensor(tb1[:], Bz[:, 0], fim, ALU.mult), r=["Bz", "S4"], w=["tb1"])
                S.op("dve", lambda: V.tensor_tensor(Bb[:, 1], Bb[:, 1], tb1[:], ALU.add), r=["tb1", "Bb"], w=["Bb"])
                mk = C.mask[:].rearrange("p (g c) -> p g c", c=16)
                for ri in range(2):
                    for ch in range(NCH):
                        S.op("dve", lambda ri=ri, ch=ch: V.tensor_tensor(
                            Bb[:, ri, ch * 8:(ch + 1) * 8, :], Bb[:, ri, ch * 8:(ch + 1) * 8, :], mk, ALU.mult),
                            r=["Bb", "mask"], w=["Bb"])
                ck(5)
                for ri in range(2):
                    for ch in range(NCH):
                        pi = (ri * NCH + ch) % 4
                        S.op("pe", lambda ri=ri, ch=ch, pi=pi: nc.tensor.transpose(
                            C.ps[pi][:, 0:128], Bb[:, ri, ch * 8:(ch + 1) * 8, :].rearrange("p g c -> p (g c)"), C.ident[:]),
                            r=["Bb", "ident"], w=[("ps", pi)])
                        S.op("act", lambda ri=ri, ch=ch, pi=pi: nc.scalar.copy(C.lbu[:, ch, ri, :], C.ps[pi][:, 0:128]),
                             r=[("ps", pi)], w=["lbu"])
                for ri in range(2):
                    for ch in range(NCH):
                        pi = (ri * NCH + ch) % 4
                        S.op("pe", lambda ri=ri, ch=ch, pi=pi: nc.tensor.transpose(
                            C.ps[pi][:, 0:128], Cin[:, ri, ch, :], C.ident[:]),
                            r=["Cin", "ident"], w=[("ps", pi)])
                        if ri == 0:
                            S.op("dve", lambda ri=ri, ch=ch, pi=pi: V.tensor_tensor(
                                C.cz[:, ch, ri, :], C.ps[pi][:, 0:128], C.mask[:], ALU.mult),
                                r=[("ps", pi), "mask"], w=["cz"])
                        else:
                            S.op("dve", lambda ri=ri, ch=ch, pi=pi: V.scalar_tensor_tensor(
                                C.cz[:, ch, ri, :], C.ps[pi][:, 0:128], -1.0, C.mask[:], ALU.mult, ALU.mult),
                                r=[("ps", pi), "mask"], w=["cz"])
                for ch in range(NCH):
                    S.op("dve", lambda ch=ch: V.tensor_scalar(C.dg[:, ch, :], C.ident[:], C.gn[:, 3, ch:ch + 1], None, ALU.mult),
                         r=["ident", "gains"], w=["dg"])
                S.barrier()

            ck(6)
            rmsnorm(S, nc, C, C.gn[:, 0, :])

            ck(7)
            with ExitStack() as st1:
                E1 = st1.enter_context
                Bf = E1(nc.sbuf_tensor("Bf", [128, 2, NPAIR, TBS], F32))
                t1 = E1(nc.sbuf_tensor("t1", [128, 2, NPAIR], F32))
                t2 = E1(nc.sbuf_tensor("t2", [128, 2, NPAIR], F32))
                Xb = E1(nc.sbuf_tensor("Xb", [128, 2, NPAIR, TBS], BF16))
                ysb = E1(nc.sbuf_tensor("ysb", [128, 512], F32))
                y2 = E1(nc.sbuf_tensor("y2", [128, 512], F32))
                y3 = E1(nc.sbuf_tensor("y3", [128, 512], F32))
                V = nc.vector
                nblk = int(os.environ.get("L0_NBLK", T // TBS))
                for blk in range(nblk):
                    tsl = slice(blk * TBS, (blk + 1) * TBS)
                    for ri in range(2):
                        for half in range(2):
                            fns = []
                            for cc in range(8):
                                ch = half * 8 + cc
                                for j in range(4):
                                    fns.append(lambda cc=cc, ch=ch, j=j, ri=ri: nc.tensor.matmul(
                                        C.ps[j][:, cc * TBS:(cc + 1) * TBS],
                                        C.lbu[32 * j:32 * j + 32, ch, ri, :], C.h[32 * j:32 * j + 32, ch, tsl],
                                        start=True, stop=True, tile_position=(32 * j, 0)))
                            S.group("pe", fns, r=["lbu", "h"], w=[("ps", 0), ("ps", 1), ("ps", 2), ("ps", 3)])
                            for j in range(4):
                                S.op("act", lambda j=j, ri=ri, half=half: nc.scalar.copy(
                                    Bf[:, ri, half * 32 + j:half * 32 + 32:4, :],
                                    C.ps[j][:].rearrange("p (a t) -> p a t", t=TBS)),
                                    r=[("ps", j)], w=["Bf"])
                    for t in range(TBS):
                        prev = C.Xc[:] if t == 0 else Bf[:, :, :, t - 1]
                        cur = Bf[:, :, :, t]
                        S.op("dve", lambda prev=prev: V.tensor_tensor(t1[:], C.A1[:], prev, ALU.mult),
                             r=["Bf", "Xc", "A1"], w=["t1"])
                        for ri in range(2):
                            pv = C.Xc[:, 1 - ri, :] if t == 0 else Bf[:, 1 - ri, :, t - 1]
                            S.op("dve", lambda pv=pv, ri=ri: V.tensor_tensor(t2[:, ri, :], C.A2[:, ri, :], pv, ALU.mult),
                                 r=["Bf", "Xc", "A2"], w=["t2"])
                        S.op("dve", lambda cur=cur: V.tensor_tensor(cur, cur, t1[:], ALU.add), r=["t1", "Bf"], w=["Bf"])
                        S.op("dve", lambda cur=cur: V.tensor_tensor(cur, cur, t2[:], ALU.add), r=["t2", "Bf"], w=["Bf"])
                    S.op("dve", lambda: V.tensor_copy(C.Xc[:], Bf[:, :, :, TBS - 1]), r=["Bf"], w=["Xc"])
                    if state_only:
                        continue
                    S.op("act", lambda: nc.scalar.copy(Xb[:], Bf[:]), r=["Bf"], w=["Xb"])
                    for half in range(2):
                        pi = 4 + half
                        fns = []
                        for cc in range(8):
                            ch = half * 8 + cc
                            osl = slice(cc * TBS, (cc + 1) * TBS)
                            fns.append(lambda ch=ch, osl=osl, pi=pi: nc.tensor.matmul(
                                C.ps[pi][:, osl], C.dg[:, ch, :], C.h[:, ch, tsl], start=(ch % 8 == 0), stop=False))
                            for j in range(4):
                                pair = ch * 4 + j
                                for ri in range(2):
                                    fns.append(lambda ch=ch, osl=osl, pi=pi, j=j, ri=ri, pair=pair: nc.tensor.matmul(
                                        C.ps[pi][32 * j:32 * j + 32, osl], C.cz[:, ch, ri, 32 * j:32 * j + 32],
                                        Xb[:, ri, pair, :], start=False, stop=(ri == 1 and j == 3),
                                        tile_position=(0, 32 * j), skip_group_check=True))
                        S.group("pe", fns, r=["dg", "h", "cz", "Xb"], w=[("ps", pi)])
                        yp = C.ps[pi]
                        S.op("act", lambda yp=yp: nc.scalar.copy(ysb[:], yp[:]), r=[("ps", pi)], w=["ysb"])
                        S.op("dve", lambda: V.tensor_tensor(y2[:], ysb[:], ysb[:], ALU.mult), r=["ysb"], w=["y2"])
                        S.op("dve", lambda: V.tensor_scalar(y2[:], y2[:], 0.044715, 1.0, ALU.mult, ALU.add), r=["y2"], w=["y2"])
                        S.op("dve", lambda: V.tensor_tensor(y2[:], y2[:], ysb[:], ALU.mult), r=["y2", "ysb"], w=["y2"])
                        S.op("act", lambda: nc.scalar.activation(y3[:], y2[:], AF.Sigmoid, scale=1.5957691216057308),
                             r=["y2"], w=["y3"])
                        S.op("dve", lambda half=half: V.tensor_tensor(
                            C.h[:, half * 8:(half + 1) * 8, tsl], ysb[:].rearrange("p (a t) -> p a t", t=TBS),
                            y3[:].rearrange("p (a t) -> p a t", t=TBS), ALU.mult),
                            r=["ysb", "y3", "h"], w=["h"])
                S.dma("sp", xend, C.Xc[:], r=["Xc"], w=["xend_d"])
                S.barrier()
            st01.close()

            if not state_only:
                with ExitStack() as st2:
                    E2 = st2.enter_context
                    C.stg = [E2(nc.sbuf_tensor(f"stg{i}", [128, 2048], F32)) for i in range(3)]
                    C.stg_i = 0
                    C.wbuf = [E2(nc.sbuf_tensor(f"wb{i}", [128, NCH, 128], BF16)) for i in range(4)]
                    C.wb_i = 0
                    C.ps_i = 0
                    C.lin_ps = [0, 1, 2, 3]
                    C.out_ps = [4, 5]
                    wo = [E2(nc.sbuf_tensor(f"wo{i}", [128, 2048], BF16)) for i in range(4)]
                    hid = [E2(nc.sbuf_tensor(f"hid{i}", [128, T], BF16)) for i in range(4)]
                    sg = [E2(nc.sbuf_tensor(f"sg{i}", [128, 512], F32)) for i in range(2)]
                    val = [E2(nc.sbuf_tensor(f"val{i}", [128, 512], F32)) for i in range(2)]
                    V = nc.vector
                    st_ = {}

                    def glu_consume(oc, tb, pst, pk):
                        sl = slice(tb * 512, (tb + 1) * 512)
                        if oc % 2 == 0:
                            i = tb
                            S.op("act", lambda: nc.scalar.copy(val[i][:], pst[:]), r=[pk], w=[("val", i)])
                        else:
                            i = tb
                            c = oc // 2
                            S.op("act", lambda: nc.scalar.activation(sg[i][:], pst[:], AF.Sigmoid), r=[pk], w=[("sg", i)])
                            S.op("dve", lambda: V.tensor_tensor(sg[i][:], sg[i][:], val[i][:], ALU.mult),
                                 r=[("sg", i), ("val", i)], w=[("sg", i)])
                            S.op("dve", lambda: V.tensor_tensor(C.x[:, c, sl], C.x[:, c, sl], sg[i][:], ALU.add),
                                 r=[("sg", i), "x"], w=["x"])

                    for c in range(NCH):
                        for which in range(2):
                            linear_fm(S, nc, C, w_glu, 1, which * D + c * 128, C.h, "h",
                                      lambda oc, tb, pst, pk, which=which, c=c: glu_consume(2 * c + which, tb, pst, pk))
                    rmsnorm(S, nc, C, C.gn[:, 1, :])
                    ffn(S, nc, C, w_in, w_out, wo, hid, sg)
                    for c in range(NCH):
                        S.dma("sp", x1T[c * 128:(c + 1) * 128, :], C.x[:, c, :], r=["x"], w=["x1_d"])
                    rmsnorm(S, nc, C, C.gn[:, 2, :])
                    kvb = [E2(nc.sbuf_tensor(f"kvb{i}", [128, 512], BF16)) for i in range(2)]

                    def kv_consume(oc, tb, pst, pk):
                        sl = slice(tb * 512, (tb + 1) * 512)
                        i = (oc * 2 + tb) % 2
                        S.op("act", lambda: nc.scalar.copy(kvb[i][:], pst[:]), r=[pk], w=[("kvb", i)])
                        dst = kT if oc < 4 else vT
                        o4 = oc % 4
                        S.dma("sp", dst[o4 * 128:(o4 + 1) * 128, sl], kvb[i][:], r=[("kvb", i)], w=["kv_d"])
                    linear_fm(S, nc, C, w_kv, 8, 0, C.h, "h", kv_consume)
                    S.barrier()
        except _Stop:
            pass
        S.finish()
    return nc


def _key_sets():
    ks = []
    for kb in range(-1, 8):
        k0 = 128 * kb
        qs, qe = max(0, k0), min(T, k0 + 256)
        ks.append((k0 + 2048, 1, 128, qs, 1, qe - qs, qs - k0, 1 if kb < 0 else 0, 0))
    for r in range(4):
        for kc in (-128, 0, 128):
            qs, qe = max(0, kc), min(256, kc + 256)
            ks.append((r + 4 * kc + 2048, 4, 128, r + 4 * qs, 4, qe - qs, qs - kc, 1 if kc < 0 else 0, 1))
    for r in range(16):
        ks.append((r, 16, 128, r, 16, 64, 128, 2, 2))
        ks.append((r + 2048, 16, 64, r, 16, 64, 0, 0, 2))
    return ks


def build_l1():
    nc = bass.Bass("TRN2", target_bir_lowering=False)
    xT = _dram_in(nc, "xT", [D, T])
    kTa = _dram_in(nc, "kTa", [512, 3072], BF16)
    Va = _dram_in(nc, "Va", [3072, 512], BF16)
    gains = _dram_in(nc, "gains", [128, 4, NCH])
    bandm = _dram_in(nc, "bandm", [128, 256], BF16)
    kbias = _dram_in(nc, "kbias", [128, 4])
    w_q = _dram_in(nc, "w_q", [D, 3 * D])
    w_o = _dram_in(nc, "w_o", [D, D])
    w_in = _dram_in(nc, "w_in", [D, 2 * FF])
    w_out = _dram_in(nc, "w_out", [FF, D])
    outT = _dram_out(nc, "outT", [D, T])
    KS = _key_sets()
    with ExitStack() as st:
        E = st.enter_context
        S = Sched(nc, st)
        C = Ctx()
        V = nc.vector
        C.x = E(nc.sbuf_tensor("x", [128, NCH, T], F32))
        C.h = E(nc.sbuf_tensor("h", [128, NCH, T], BF16))
        C.ones = E(nc.sbuf_tensor("ones", [128, 128], BF16))
        C.gn = E(nc.sbuf_tensor("gn", [128, 4, NCH], F32))
        C.epsc = E(nc.sbuf_tensor("epsc", [128, 1], F32))
        C.rstd = E(nc.sbuf_tensor("rstd", [128, 512], F32))
        C.sq = [E(nc.sbuf_tensor(f"sq{i}", [128, 512], BF16)) for i in range(2)]
        C.ps = [E(nc.psum_tensor(f"ps{i}", [128, 512], F32)) for i in range(8)]
        S.op("dve", lambda: V.memset(C.ones[:], 1.0), w=["ones"])
        S.op("dve", lambda: V.memset(C.epsc[:], EPS), w=["epsc"])
        S.dma("sp", C.gn[:], gains, w=["gains"])
        for c in range(NCH):
            S.dma("sp", C.x[:, c, :], xT[c * 128:(c + 1) * 128, :], w=["x"])
        rmsnorm(S, nc, C, C.gn[:, 0, :])
        with ExitStack() as sa:
            EA = sa.enter_context
            attn = EA(nc.sbuf_tensor("attn", [128, NCH, T], BF16))
            with ExitStack() as sb:
                EB = sb.enter_context
                C.stg = [EB(nc.sbuf_tensor(f"stg{i}", [128, 2048], F32)) for i in range(3)]
                C.stg_i = 0
                C.wbuf = [EB(nc.sbuf_tensor(f"wb{i}", [128, NCH, 128], BF16)) for i in range(3)]
                C.wb_i = 0
                C.ps_i = 0
                C.lin_ps = [6, 7]
                mask = EB(nc.sbuf_tensor("bandS", [128, 256], BF16))
                kb_ = EB(nc.sbuf_tensor("kbS", [128, 4], F32))
                kts = EB(nc.sbuf_tensor("kts", [128, 3072], BF16))
                Vt = EB(nc.sbuf_tensor("Vt", [128, len(KS), 128], BF16))
                qg = [EB(nc.sbuf_tensor(f"qg{i}", [128, T], BF16)) for i in range(3)]
                pe_ = [EB(nc.sbuf_tensor(f"pe{i}", [128, 256], BF16)) for i in range(2)]
                pT = [EB(nc.sbuf_tensor(f"pT{i}", [128, 256], BF16)) for i in range(2)]
                rec = EB(nc.sbuf_tensor("rec", [128, 512], F32))
                S.dma("sp", mask[:], bandm, w=["mask"])
                S.dma("sp", kb_[:], kbias, w=["kb"])
                it = 0
                for hh in range(16):
                    kvh = hh // 4
                    if hh % 4 == 0:
                        S.dma("sp", kts[:], kTa[kvh * 128:(kvh + 1) * 128, :], w=["kts"])
                        for i, (k0, kst, m, q0, qst, n, off, bc, g) in enumerate(KS):
                            S.dma("sp", Vt[0:m, i, :], Va[k0:k0 + kst * (m - 1) + 1:kst, kvh * 128:(kvh + 1) * 128],
                                  w=["Vt"])
                    for g in range(3):
                        def q_consume(oc, tb, pst, pk, g=g):
                            sl = slice(tb * 512, (tb + 1) * 512)
                            S.op("act", lambda: nc.scalar.activation(qg[g][:, sl], pst[:], AF.Copy, scale=HD_SCALE),
                                 r=[pk], w=[("qg", g)])
                        linear_fm(S, nc, C, w_q, 1, g * D + hh * 128, C.h, "h", q_consume)
                    first = {0: True, 1: True}
                    for i, (k0, kst, m, q0, qst, n, off, bc, g) in enumerate(KS):
                        si = 4 + (it % 2)
                        pi_ = it % 2
                        it += 1
                        sps = C.ps[si]
                        S.group("pe", [lambda: nc.tensor.matmul(
                            sps[0:m, 0:n], kts[:, k0:k0 + kst * (m - 1) + 1:kst],
                            qg[g][:, q0:q0 + qst * (n - 1) + 1:qst], start=True, stop=True)],
                            r=["kts", ("qg", g)], w=[("ps", si)])
                        S.op("act", lambda: nc.scalar.activation(pe_[pi_][0:m, 0:n], sps[0:m, 0:n], AF.Exp,
                                                                 bias=kb_[0:m, bc:bc + 1], scale=1.0),
                             r=[("ps", si), "kb"], w=[("pe", pi_)])
                        S.op("dve", lambda: V.tensor_tensor(pT[pi_][0:m, 0:n], pe_[pi_][0:m, 0:n], mask[0:m, off:off + n], ALU.mult),
                             r=[("pe", pi_), "mask"], w=[("pT", pi_)])
                        segs = []
                        isplit = min(n, max(0, -(-(512 - q0) // qst)))
                        if isplit > 0:
                            segs.append((0, isplit, 0))
                        if isplit < n:
                            segs.append((isplit, n, 1))
                        fns = []
                        wk = []
                        for (i0, i1, bank) in segs:
                            c0 = q0 + qst * i0 - 512 * bank
                            cs = slice(c0, c0 + qst * (i1 - i0 - 1) + 1, qst)
                            st_ = first[bank]
                            first[bank] = False
                            fns.append(lambda cs=cs, bank=bank, i0=i0, i1=i1, st_=st_: nc.tensor.matmul(
                                C.ps[bank][:, cs], Vt[0:m, i, :], pT[pi_][0:m, i0:i1], start=st_, stop=False,
                                skip_group_check=True))
                            fns.append(lambda cs=cs, bank=bank, i0=i0, i1=i1, st_=st_: nc.tensor.matmul(
                                C.ps[2 + bank][:, cs], C.ones[0:m, :], pT[pi_][0:m, i0:i1], start=st_, stop=False,
                                skip_group_check=True))
                            wk += [("ps", bank), ("ps", 2 + bank)]
                        S.group("pe", fns, r=[("pT", pi_), "Vt", "ones"], w=wk)
                    for bank in range(2):
                        sl = slice(bank * 512, (bank + 1) * 512)
                        S.op("dve", lambda: V.reciprocal(rec[:], C.ps[2 + bank][:]), r=[("ps", 2 + bank)], w=["rec"])
                        S.op("dve", lambda: V.tensor_tensor(attn[:, hh, sl], C.ps[bank][:], rec[:], ALU.mult),
                             r=[("ps", bank), "rec"], w=["attn"])
                S.barrier()
            with ExitStack() as sc:
                EC = sc.enter_context
                C.stg = [EC(nc.sbuf_tensor(f"stgc{i}", [128, 2048], F32)) for i in range(3)]
                C.stg_i = 0
                C.wbuf = [EC(nc.sbuf_tensor(f"wbc{i}", [128, NCH, 128], BF16)) for i in range(4)]
                C.wb_i = 0
                C.lin_ps = [0, 1, 2, 3]

                def o_consume(oc, tb, pst, pk):
                    sl = slice(tb * 512, (tb + 1) * 512)
                    S.op("dve", lambda: V.tensor_tensor(C.x[:, oc, sl], C.x[:, oc, sl], pst[:], ALU.add),
                         r=[pk, "x"], w=["x"])
                linear_fm(S, nc, C, w_o, NCH, 0, attn, "attn", o_consume)
                S.barrier()
        with ExitStack() as sd:
            ED = sd.enter_context
            C.stg = [ED(nc.sbuf_tensor(f"stgd{i}", [128, 2048], F32)) for i in range(3)]
            C.stg_i = 0
            C.wbuf = [ED(nc.sbuf_tensor(f"wbd{i}", [128, NCH, 128], BF16)) for i in range(4)]
            C.wb_i = 0
            C.lin_ps = [0, 1, 2, 3]
            C.out_ps = [4, 5]
            wo = [ED(nc.sbuf_tensor(f"wo{i}", [128, 2048], BF16)) for i in range(4)]
            hid = [ED(nc.sbuf_tensor(f"hid{i}", [128, T], BF16)) for i in range(4)]
            sg = [ED(nc.sbuf_tensor(f"sg{i}", [128, 512], F32)) for i in range(2)]
            rmsnorm(S, nc, C, C.gn[:, 1, :])
            ffn(S, nc, C, w_in, w_out, wo, hid, sg)
            rmsnorm(S, nc, C, C.gn[:, 2, :], hkey="x", dst=C.x)
            for c in range(NCH):
                S.dma("sp", outT[c * 128:(c + 1) * 128, :], C.x[:, c, :], r=["x"], w=["out_d"])
            S.barrier()
        S.finish()
    return nc


HD_SCALE = 128 ** -0.5

_cache = {}


def _prog(name, fn):
    if name not in _cache:
        _cache[name] = fn()
    return _cache[name]


def _fm(v):
    return np.ascontiguousarray(np.asarray(v, np.float32).reshape(NCH, 128).T)


def layer0(inputs):
    x = np.asarray(inputs["x"], np.float32)
    n = 8
    gains = np.ascontiguousarray(np.stack([_fm(inputs["a_norm_mix"][0]), _fm(inputs["ffn_norm"][0]),
                                           _fm(inputs["kv_norm"]), _fm(inputs["s5_d"][0])], axis=1))
    ident = np.eye(128, dtype=np.float32)
    hp = np.arange(128) // 64
    g8 = (np.arange(128) // 16) % 2
    mask_sm = (hp[:, None] == g8[None, :]).astype(np.float32)
    base = {
        "lam_re": np.ascontiguousarray(inputs["s5_lam_re"][0]), "lam_im": np.ascontiguousarray(inputs["s5_lam_im"][0]),
        "log_dt": np.ascontiguousarray(inputs["s5_log_dt"][0].reshape(128, 1)),
        "b_re": np.ascontiguousarray(inputs["s5_b_re"][0]), "b_im": np.ascontiguousarray(inputs["s5_b_im"][0]),
        "c_re": np.ascontiguousarray(inputs["s5_c_re"][0].reshape(2048, 64)),
        "c_im": np.ascontiguousarray(inputs["s5_c_im"][0].reshape(2048, 64)),
        "gains": gains, "ident": ident, "mask_sm": mask_sm,
    }
    xTs = []
    for c in range(n):
        b, j = c // 4, c % 4
        xTs.append(np.ascontiguousarray(x[b, j * T:(j + 1) * T, :].T))
    zero_e = np.zeros((128, 3, 2, NPAIR), np.float32)
    nc_s = _prog("l0s", lambda: build_l0(True))
    maps = [dict(base, xT=xTs[c], eprev=zero_e) for c in range(n)]
    res = run_bass_kernel_spmd(nc_s, maps, core_ids=list(range(n)))
    eprev = []
    for c in range(n):
        b, j = c // 4, c % 4
        e = np.zeros((128, 3, 2, NPAIR), np.float32)
        for k in range(3):
            if j - 1 - k >= 0:
                e[:, k] = np.asarray(res.results[b * 4 + j - 1 - k]["xend"])
        eprev.append(e)
    nc_f = _prog("l0f", lambda: build_l0(False))
    full = dict(base, w_glu=np.ascontiguousarray(inputs["s5_w_glu"][0]),
                w_in=np.ascontiguousarray(inputs["ffn_w_in"][0]), w_out=np.ascontiguousarray(inputs["ffn_w_out"][0]),
                w_kv=np.ascontiguousarray(inputs["w_kv"]))
    maps = [dict(full, xT=xTs[c], eprev=eprev[c]) for c in range(n)]
    res = run_bass_kernel_spmd(nc_f, maps, core_ids=list(range(n)))
    return res.results


def layer1(inputs, r0):
    n = 8
    gains = np.ascontiguousarray(np.stack([_fm(inputs["b_norm_mix"][0]), _fm(inputs["ffn_norm"][1]),
                                           _fm(inputs["final_norm"]), _fm(inputs["final_norm"])], axis=1))
    bdt = np.asarray(r0[0]["kT"]).dtype
    sidx = np.arange(128)[:, None]
    qidx = np.arange(256)[None, :]
    bandm = ((sidx <= qidx) & (qidx <= sidx + 128)).astype(np.float32).astype(bdt)
    base = {"gains": gains, "bandm": bandm,
            "w_q": np.ascontiguousarray(inputs["attn_w_q"][0]), "w_o": np.ascontiguousarray(inputs["attn_w_o"][0]),
            "w_in": np.ascontiguousarray(inputs["ffn_w_in"][1]), "w_out": np.ascontiguousarray(inputs["ffn_w_out"][1])}
    maps = []
    for c in range(n):
        b, j = c // 4, c % 4
        kparts, vparts = [], []
        for jj in (j - 2, j - 1, j):
            if jj < 0:
                kparts.append(np.zeros((512, T), bdt))
                vparts.append(np.zeros((512, T), bdt))
            else:
                kparts.append(np.asarray(r0[b * 4 + jj]["kT"]))
                vparts.append(np.asarray(r0[b * 4 + jj]["vT"]))
        kTa = np.ascontiguousarray(np.concatenate(kparts, axis=1))
        Va = np.ascontiguousarray(np.concatenate(vparts, axis=1).T)
        kbias = np.zeros((128, 4), np.float32)
        if j == 0:
            kbias[:, 1] = -30000.0
            kbias[:, 2] = -30000.0
        elif j == 1:
            kbias[:64, 2] = -30000.0
        maps.append(dict(base, xT=np.ascontiguousarray(r0[c]["x1T"]), kTa=kTa, Va=Va, kbias=kbias))
    nc1 = _prog("l1", build_l1)
    res = run_bass_kernel_spmd(nc1, maps, core_ids=list(range(n)))
    return res.results


def kernel(**inputs):
    inputs = {k: np.asarray(v) for k, v in inputs.items()}
    r0 = layer0(inputs)
    r1 = layer1(inputs, r0)
    out = np.empty((2, 4096, D), np.float32)
    for c in range(8):
        b, j = c // 4, c % 4
        out[b, j * T:(j + 1) * T, :] = np.asarray(r1[c]["outT"]).T
    return out
```

```python
import math
from contextlib import ExitStack
import numpy as np
import concourse.bass as bass
import concourse.mybir as mybir
from concourse.bass_utils import run_bass_kernel_spmd

F32 = mybir.dt.float32
BF16 = mybir.dt.bfloat16
AF = mybir.ActivationFunctionType
ALU = mybir.AluOpType

D = 2048
NCH = 16
T = 1024
NTB = 2
FF = 5632
NF = 44
EPS = 1e-6
TBS = 64
NPAIR = 64


class Sched:
    def __init__(self, nc, st):
        self.nc = nc
        self.eng = {"pe": nc.tensor, "act": nc.scalar, "dve": nc.vector, "pool": nc.gpsimd, "sp": nc.sync}
        self.sem = {}
        self.cnt = {}
        for e in ["pe", "act", "dve", "pool"]:
            self.sem["c_" + e] = st.enter_context(nc.semaphore("c_" + e))
            self.cnt["c_" + e] = 0
        for q in ["sp", "act", "pool"]:
            self.sem["d_" + q] = st.enter_context(nc.semaphore("d_" + q))
            self.cnt["d_" + q] = 0
        self.seen = {e: {} for e in self.eng}
        self.last_w = {}
        self.readers = {}
        self.stopped = False
        _CUR[0] = self

    def _wait(self, e, deps):
        need = {}
        for (c, v) in deps:
            if c == "c_" + e and e == "pe":
                continue
            if v > need.get(c, 0):
                need[c] = v
        for c, v in need.items():
            if self.seen[e].get(c, 0) >= v:
                continue
            self.eng[e].wait_ge(self.sem[c], v)
            self.seen[e][c] = v

    def _deps(self, r, w):
        deps = []
        for k in r:
            if k in self.last_w:
                deps.append(self.last_w[k])
        for k in w:
            if k in self.last_w:
                deps.append(self.last_w[k])
            deps.extend(self.readers.get(k, []))
        return deps

    def _record(self, tok, r, w):
        for k in w:
            self.last_w[k] = tok
            self.readers[k] = []
        for k in r:
            self.readers.setdefault(k, []).append(tok)

    def op(self, e, fn, r=(), w=()):
        if self.stopped:
            return
        self._wait(e, self._deps(r, w))
        ins = fn()
        c = "c_" + e
        self.cnt[c] += 1
        ins.then_inc(self.sem[c], 1)
        self._record((c, self.cnt[c]), r, w)

    def group(self, e, fns, r=(), w=()):
        if self.stopped:
            return
        self._wait(e, self._deps(r, w))
        ins = None
        for f in fns:
            ins = f()
        c = "c_" + e
        self.cnt[c] += 1
        ins.then_inc(self.sem[c], 1)
        self._record((c, self.cnt[c]), r, w)

    def dma(self, q, out, in_, r=(), w=(), **kw):
        if self.stopped:
            return
        self._wait(q, self._deps(r, w))
        ins = self.eng[q].dma_start(out=out, in_=in_, **kw)
        c = "d_" + q
        self.cnt[c] += 16
        ins.then_inc(self.sem[c], 16)
        self._record((c, self.cnt[c]), r, w)

    def barrier(self):
        if self.stopped:
            return
        for e in self.eng:
            deps = [(c, v) for c, v in self.cnt.items() if v > 0]
            self._wait(e, deps)
        self.last_w = {}
        self.readers = {}

    def finish(self):
        deps = [(c, v) for c, v in self.cnt.items() if v > 0]
        self._wait("sp", deps)


def _dram_in(nc, name, shape, dt=F32):
    return nc.dram_tensor(name, list(shape), dt, kind="ExternalInput").ap()


def _dram_out(nc, name, shape, dt=F32):
    return nc.dram_tensor(name, list(shape), dt, kind="ExternalOutput").ap()


class Ctx:
    pass


class _Stop(Exception):
    pass


import os
STOP = int(os.environ.get("L0_STOP", "0"))


_CUR = [None]


def ck(k):
    if STOP == k:
        _CUR[0].stopped = True


def rmsnorm(S, nc, C, gain, hkey="h", dst=None):
    for tb in range(NTB):
        sl = slice(tb * 512, (tb + 1) * 512)
        ss = C.ps[6 + tb]
        fns = []
        for c in range(NCH):
            sq = C.sq[c % 2]
            S.op("act", lambda c=c, sq=sq: nc.scalar.activation(sq[:], C.x[:, c, sl], AF.Square),
                 r=["x"], w=[("sq", c % 2)])
            S.op("pe", lambda c=c, sq=sq: nc.tensor.matmul(ss[:], C.ones[:], sq[:], start=(c == 0), stop=(c == NCH - 1)),
                 r=[("sq", c % 2), "ones"], w=[("ps", 6 + tb)])
        S.op("act", lambda: nc.scalar.activation(C.rstd[:], ss[:], AF.Sqrt, bias=C.epsc[:, 0:1], scale=1.0 / D),
             r=[("ps", 6 + tb)], w=["rstd"])
        S.op("dve", lambda: nc.vector.reciprocal(C.rstd[:], C.rstd[:]), r=["rstd"], w=["rstd"])
        for c in range(NCH):
            S.op("dve", lambda c=c: nc.vector.scalar_tensor_tensor(
                (C.h if dst is None else dst)[:, c, sl], C.x[:, c, sl], gain[:, c:c + 1], C.rstd[:], ALU.mult, ALU.mult),
                r=["x", "rstd", "gains"], w=[hkey])


def load_cast(S, nc, C, src_ap, dst, dkey, shape3=None):
    i = C.stg_i
    C.stg_i += 1
    s = i % len(C.stg)
    stg = C.stg[s]
    sv = stg[:] if shape3 is None else stg[:].rearrange("p (k n) -> p k n", k=shape3[0])
    S.dma("sp", sv, src_ap, w=[("stg", s)])
    dv = dst
    S.op("act", lambda: nc.scalar.copy(dv, sv), r=[("stg", s)], w=[dkey])


def linear_fm(S, nc, C, w_ap, n_out_chunks, col0, src, skey, consume):
    for oc in range(n_out_chunks):
        wb = C.wbuf[C.wb_i % len(C.wbuf)]
        wk = ("wb", C.wb_i % len(C.wbuf))
        C.wb_i += 1
        c0 = col0 + oc * 128
        load_cast(S, nc, C, w_ap[:, c0:c0 + 128].rearrange("(k p) n -> p k n", p=128),
                  wb[:], wk, shape3=(NCH, 128))
        for tb in range(NTB):
            pi = C.lin_ps[C.ps_i % len(C.lin_ps)]
            C.ps_i += 1
            pst = C.ps[pi]
            sl = slice(tb * 512, (tb + 1) * 512)
            S.group("pe", [lambda k=k: nc.tensor.matmul(pst[:], wb[:, k, :], src[:, k, sl],
                                                        start=(k == 0), stop=(k == NCH - 1))
                           for k in range(NCH)],
                    r=[wk, skey], w=[("ps", pi)])
            consume(oc, tb, pst, ("ps", pi))


def ffn(S, nc, C, w_in, w_out, wo, hid, sg):
    V = nc.vector
    GS = 2
    for g in range(NF // GS):
        for fi in range(GS):
            f = g * GS + fi
            hb = hid[f % 4]
            hk = ("hid", f % 4)
            wob = wo[f % 4]
            load_cast(S, nc, C, w_out[f * 128:(f + 1) * 128, :], wob[:], ("wo", f % 4))

            def ffn_consume(oc, tb, pst, pk, hb=hb, hk=hk):
                sl = slice(tb * 512, (tb + 1) * 512)
                if oc == 0:
                    S.op("act", lambda: nc.scalar.activation(sg[tb][:], pst[:], AF.Silu), r=[pk], w=[("sg", tb)])
                else:
                    S.op("dve", lambda: V.tensor_tensor(hb[:, sl], sg[tb][:], pst[:], ALU.mult),
                         r=[pk, ("sg", tb)], w=[hk])
            linear_fm(S, nc, C, w_in, 1, f * 128, C.h, "h",
                      lambda oc, tb, pst, pk: ffn_consume(0, tb, pst, pk))
            linear_fm(S, nc, C, w_in, 1, FF + f * 128, C.h, "h",
                      lambda oc, tb, pst, pk: ffn_consume(1, tb, pst, pk))
        for c in range(NCH):
            for tb in range(NTB):
                sl = slice(tb * 512, (tb + 1) * 512)
                pi = C.out_ps[C.ps_i % 2]
                C.ps_i += 1
                fns = []
                rk = []
                for fi in range(GS):
                    f = g * GS + fi
                    rk += [("wo", f % 4), ("hid", f % 4)]
                    fns.append(lambda f=f, fi=fi, pi=pi, c=c, sl=sl: nc.tensor.matmul(
                        C.ps[pi][:], wo[f % 4][:, c * 128:(c + 1) * 128], hid[f % 4][:, sl],
                        start=(fi == 0), stop=(fi == GS - 1)))
                S.group("pe", fns, r=rk, w=[("ps", pi)])
                S.op("dve", lambda pi=pi, c=c, sl=sl: V.tensor_tensor(C.x[:, c, sl], C.x[:, c, sl], C.ps[pi][:], ALU.add),
                     r=[("ps", pi), "x"], w=["x"])


def build_l0(state_only):
    nc = bass.Bass("TRN2", target_bir_lowering=False)
    xT = _dram_in(nc, "xT", [D, T])
    lam_re = _dram_in(nc, "lam_re", [128, 64])
    lam_im = _dram_in(nc, "lam_im", [128, 64])
    log_dt = _dram_in(nc, "log_dt", [128, 1])
    b_re = _dram_in(nc, "b_re", [128, 64, 16])
    b_im = _dram_in(nc, "b_im", [128, 64, 16])
    c_re = _dram_in(nc, "c_re", [2048, 64])
    c_im = _dram_in(nc, "c_im", [2048, 64])
    gains = _dram_in(nc, "gains", [128, 4, NCH])
    eprev = _dram_in(nc, "eprev", [128, 3, 2, NPAIR])
    ident_d = _dram_in(nc, "ident", [128, 128])
    mask_d = _dram_in(nc, "mask_sm", [128, 128])
    xend = _dram_out(nc, "xend", [128, 2, NPAIR])
    if not state_only:
        w_glu = _dram_in(nc, "w_glu", [D, 2 * D])
        w_in = _dram_in(nc, "w_in", [D, 2 * FF])
        w_out = _dram_in(nc, "w_out", [FF, D])
        w_kv = _dram_in(nc, "w_kv", [D, 1024])
        x1T = _dram_out(nc, "x1T", [D, T])
        kT = _dram_out(nc, "kT", [512, T], BF16)
        vT = _dram_out(nc, "vT", [512, T], BF16)

    with ExitStack() as st:
        E = st.enter_context
        S = Sched(nc, st)
        C = Ctx()
        C.x = E(nc.sbuf_tensor("x", [128, NCH, T], F32))
        C.h = E(nc.sbuf_tensor("h", [128, NCH, T], BF16))
        C.ones = E(nc.sbuf_tensor("ones", [128, 128], BF16))
        C.ident = E(nc.sbuf_tensor("identS", [128, 128], F32))
        C.mask = E(nc.sbuf_tensor("maskS", [128, 128], F32))
        C.gn = E(nc.sbuf_tensor("gn", [128, 4, NCH], F32))
        C.epsc = E(nc.sbuf_tensor("epsc", [128, 1], F32))
        C.rstd = E(nc.sbuf_tensor("rstd", [128, 512], F32))
        C.sq = [E(nc.sbuf_tensor(f"sq{i}", [128, 512], BF16)) for i in range(2)]
        C.ps = [E(nc.psum_tensor(f"ps{i}", [128, 512], F32)) for i in range(8)]
        C.Xc = E(nc.sbuf_tensor("Xc", [128, 2, NPAIR], F32))
        st01 = ExitStack()
        E01 = st01.enter_context
        C.A1 = E01(nc.sbuf_tensor("A1", [128, 2, NPAIR], F32))
        C.A2 = E01(nc.sbuf_tensor("A2", [128, 2, NPAIR], F32))
        C.lbu = E01(nc.sbuf_tensor("lbu", [128, NCH, 2, 128], BF16))
        C.cz = E01(nc.sbuf_tensor("cz", [128, NCH, 2, 128], BF16))
        C.dg = E01(nc.sbuf_tensor("dg", [128, NCH, 128], BF16))
        C.TA1 = E01(nc.sbuf_tensor("TA1", [128, 7, 2, NPAIR], F32))
        C.TA2 = E01(nc.sbuf_tensor("TA2", [128, 7, 2, NPAIR], F32))

        try:
            S.op("dve", lambda: nc.vector.memset(C.ones[:], 1.0), w=["ones"])
            S.op("dve", lambda: nc.vector.memset(C.epsc[:], EPS), w=["epsc"])
            S.dma("sp", C.ident[:], ident_d, w=["ident"])
            S.dma("sp", C.mask[:], mask_d, w=["mask"])
            S.dma("sp", C.gn[:], gains, w=["gains"])
            for c in range(NCH):
                S.dma("sp", C.x[:, c, :], xT[c * 128:(c + 1) * 128, :], w=["x"])

            ck(1)
            with ExitStack() as st0:
                E0 = st0.enter_context
                lr = E0(nc.sbuf_tensor("lr", [128, 64], F32))
                li = E0(nc.sbuf_tensor("li", [128, 64], F32))
                ldt = E0(nc.sbuf_tensor("ldt", [128, 1], F32))
                dt = E0(nc.sbuf_tensor("dt", [128, 1], F32))
                mag = E0(nc.sbuf_tensor("mag", [128, 64], F32))
                ang = E0(nc.sbuf_tensor("ang", [128, 64], F32))
                angc = E0(nc.sbuf_tensor("angc", [128, 64], F32))
                tm = E0(nc.sbuf_tensor("tm", [128, 64], F32))
                tm2 = E0(nc.sbuf_tensor("tm2", [128, 64], F32))
                nr = E0(nc.sbuf_tensor("nr", [128, 64], F32))
                rden = E0(nc.sbuf_tensor("rden", [128, 64], F32))
                T4 = E0(nc.sbuf_tensor("T4", [128, 4, 2, 64], F32))
                S4 = E0(nc.sbuf_tensor("S4", [128, 4, 128], F32))
                Bz = E0(nc.sbuf_tensor("Bz", [128, 2, 128, 16], F32))
                Bb = E0(nc.sbuf_tensor("Bb", [128, 2, 128, 16], F32))
                tb1 = E0(nc.sbuf_tensor("tb1", [128, 128, 16], F32))
                Cin = E0(nc.sbuf_tensor("Cin", [128, 2, NCH, 128], F32))
                dgf = E0(nc.sbuf_tensor("dgf", [128, 128], F32))

                S.dma("sp", lr[:], lam_re, w=["lr"])
                S.dma("sp", li[:], lam_im, w=["li"])
                S.dma("sp", ldt[:], log_dt, w=["ldt"])
                for ri, src in enumerate([b_re, b_im]):
                    for hh in range(2):
                        for gh in range(2):
                            S.dma("sp", Bz[hh * 64:(hh + 1) * 64, ri, gh * 64:(gh + 1) * 64, :],
                                  src[gh * 64:(gh + 1) * 64].rearrange("g p c -> p g c"), w=["Bz"])
                for ri, src in enumerate([c_re, c_im]):
                    for dup in range(2):
                        S.dma("sp", Cin[:, ri, :, dup * 64:(dup + 1) * 64],
                              src.rearrange("(k q) p -> q k p", q=128), w=["Cin"])

                ck(2)
                V = nc.vector
                S.op("act", lambda: nc.scalar.activation(dt[:], ldt[:], AF.Exp), r=["ldt"], w=["dt"])
                S.op("act", lambda: nc.scalar.activation(mag[:], lr[:], AF.Exp, scale=dt[:, 0:1]), r=["lr", "dt"], w=["mag"])
                S.op("dve", lambda: V.tensor_scalar(ang[:], li[:], dt[:, 0:1], None, ALU.mult), r=["li", "dt"], w=["ang"])
                S.op("dve", lambda: V.tensor_scalar(angc[:], ang[:], math.pi / 2, None, ALU.add), r=["ang"], w=["angc"])
                for a_, key in ((ang, "ang"), (angc, "angc")):
                    for _ in range(4):
                        S.op("dve", lambda a_=a_: V.tensor_scalar(tm[:], a_[:], math.pi, 2 * math.pi, ALU.is_gt, ALU.mult),
                             r=[key], w=["tm"])
                        S.op("dve", lambda a_=a_: V.tensor_tensor(a_[:], a_[:], tm[:], ALU.subtract), r=["tm", key], w=[key])
                S.op("act", lambda: nc.scalar.activation(ang[:], ang[:], AF.Sin), r=["ang"], w=["ang"])
                S.op("act", lambda: nc.scalar.activation(angc[:], angc[:], AF.Sin), r=["angc"], w=["angc"])
                for dup in range(2):
                    S.op("dve", lambda dup=dup: V.tensor_tensor(T4[:, 0, dup, :], mag[:], angc[:], ALU.mult),
                         r=["mag", "angc"], w=["T4"])
                    S.op("dve", lambda dup=dup: V.tensor_tensor(T4[:, 1, dup, :], mag[:], ang[:], ALU.mult),
                         r=["mag", "ang"], w=["T4"])
                a_n = T4[:, 0, 0, :]
                b_n = T4[:, 1, 0, :]
                S.op("dve", lambda: V.tensor_scalar(nr[:], a_n, -1.0, None, ALU.add), r=["T4"], w=["nr"])
                S.op("dve", lambda: V.tensor_tensor(tm[:], lr[:], lr[:], ALU.mult), r=["lr"], w=["tm"])
                S.op("dve", lambda: V.tensor_tensor(tm2[:], li[:], li[:], ALU.mult), r=["li"], w=["tm2"])
                S.op("dve", lambda: V.tensor_tensor(tm[:], tm[:], tm2[:], ALU.add), r=["tm", "tm2"], w=["tm"])
                S.op("dve", lambda: V.reciprocal(rden[:], tm[:]), r=["tm"], w=["rden"])
                S.op("dve", lambda: V.tensor_tensor(tm[:], nr[:], lr[:], ALU.mult), r=["nr", "lr"], w=["tm"])
                S.op("dve", lambda: V.tensor_tensor(tm2[:], b_n, li[:], ALU.mult), r=["T4", "li"], w=["tm2"])
                S.op("dve", lambda: V.tensor_tensor(tm[:], tm[:], tm2[:], ALU.add), r=["tm", "tm2"], w=["tm"])
                for dup in range(2):
                    S.op("dve", lambda dup=dup: V.tensor_tensor(T4[:, 2, dup, :], tm[:], rden[:], ALU.mult),
                         r=["tm", "rden"], w=["T4"])
                S.op("dve", lambda: V.tensor_tensor(tm[:], b_n, lr[:], ALU.mult), r=["T4", "lr"], w=["tm"])
                S.op("dve", lambda: V.tensor_tensor(tm2[:], nr[:], li[:], ALU.mult), r=["nr", "li"], w=["tm2"])
                S.op("dve", lambda: V.tensor_tensor(tm[:], tm[:], tm2[:], ALU.subtract), r=["tm", "tm2"], w=["tm"])
                for dup in range(2):
                    S.op("dve", lambda dup=dup: V.tensor_tensor(T4[:, 3, dup, :], tm[:], rden[:], ALU.mult),
                         r=["tm", "rden"], w=["T4"])
                ck(3)
                for q in range(4):
                    S.op("pe", lambda q=q: nc.tensor.transpose(C.ps[0][:, 0:128], T4[:, q, :, :].rearrange("g a p -> g (a p)"), C.ident[:]),
                         r=["T4", "ident"], w=[("ps", 0)])
                    S.op("dve", lambda q=q: V.tensor_copy(S4[:, q, :], C.ps[0][:, 0:128]), r=[("ps", 0)], w=["S4"])
                for hh in range(2):
                    ps_ = slice(hh * 64, (hh + 1) * 64)
                    for ri in range(2):
                        S.op("dve", lambda hh=hh, ps_=ps_, ri=ri: V.tensor_copy(C.A1[ps_, ri, :], S4[ps_, 0, hh::2]),
                             r=["S4"], w=["A1"])
                    S.op("dve", lambda hh=hh, ps_=ps_: V.tensor_scalar(C.A2[ps_, 0, :], S4[ps_, 1, hh::2], -1.0, None, ALU.mult),
                         r=["S4"], w=["A2"])
                    S.op("dve", lambda hh=hh, ps_=ps_: V.tensor_copy(C.A2[ps_, 1, :], S4[ps_, 1, hh::2]),
                         r=["S4"], w=["A2"])
                Ep = E0(nc.sbuf_tensor("Ep", [128, 3, 2, NPAIR], F32))
                mr = E0(nc.sbuf_tensor("mr", [128, NPAIR], F32))
                mi = E0(nc.sbuf_tensor("mi", [128, NPAIR], F32))
                q1 = E0(nc.sbuf_tensor("q1", [128, NPAIR], F32))
                q2 = E0(nc.sbuf_tensor("q2", [128, NPAIR], F32))
                q3 = E0(nc.sbuf_tensor("q3", [128, NPAIR], F32))
                S.dma("sp", Ep[:], eprev, w=["Ep"])
                S.op("dve", lambda: V.tensor_copy(mr[:], C.A1[:, 0, :]), r=["A1"], w=["mr"])
                S.op("dve", lambda: V.tensor_copy(mi[:], C.A2[:, 1, :]), r=["A2"], w=["mi"])
                for lv in range(10):
                    if lv < 7:
                        S.op("dve", lambda lv=lv: V.tensor_copy(C.TA1[:, lv, 0, :], mr[:]), r=["mr"], w=["TA"])
                        S.op("dve", lambda lv=lv: V.tensor_copy(C.TA1[:, lv, 1, :], mr[:]), r=["mr"], w=["TA"])
                        S.op("dve", lambda lv=lv: V.tensor_scalar(C.TA2[:, lv, 0, :], mi[:], -1.0, None, ALU.mult), r=["mi"], w=["TA"])
                        S.op("dve", lambda lv=lv: V.tensor_copy(C.TA2[:, lv, 1, :], mi[:]), r=["mi"], w=["TA"])
                    S.op("dve", lambda: V.tensor_tensor(q1[:], mr[:], mr[:], ALU.mult), r=["mr"], w=["q1"])
                    S.op("dve", lambda: V.tensor_tensor(q2[:], mi[:], mi[:], ALU.mult), r=["mi"], w=["q2"])
                    S.op("dve", lambda: V.tensor_tensor(q3[:], mr[:], mi[:], ALU.mult), r=["mr", "mi"], w=["q3"])
                    S.op("dve", lambda: V.tensor_tensor(mr[:], q1[:], q2[:], ALU.subtract), r=["q1", "q2"], w=["mr"])
                    S.op("dve", lambda: V.tensor_scalar(mi[:], q3[:], 2.0, None, ALU.mult), r=["q3"], w=["mi"])
                S.op("dve", lambda: V.tensor_copy(C.Xc[:], Ep[:, 2]), r=["Ep"], w=["Xc"])
                for k in (1, 0):
                    xr, xi = C.Xc[:, 0, :], C.Xc[:, 1, :]
                    S.op("dve", lambda: V.tensor_tensor(q1[:], mr[:], xr, ALU.mult), r=["mr", "Xc"], w=["q1"])
                    S.op("dve", lambda: V.tensor_tensor(q2[:], mi[:], xi, ALU.mult), r=["mi", "Xc"], w=["q2"])
                    S.op("dve", lambda: V.tensor_tensor(q1[:], q1[:], q2[:], ALU.subtract), r=["q1", "q2"], w=["q1"])
                    S.op("dve", lambda: V.tensor_tensor(q2[:], mr[:], xi, ALU.mult), r=["mr", "Xc"], w=["q2"])
                    S.op("dve", lambda: V.tensor_tensor(q3[:], mi[:], xr, ALU.mult), r=["mi", "Xc"], w=["q3"])
                    S.op("dve", lambda: V.tensor_tensor(q2[:], q2[:], q3[:], ALU.add), r=["q2", "q3"], w=["q2"])
                    S.op("dve", lambda k=k: V.tensor_tensor(C.Xc[:, 0, :], q1[:], Ep[:, k, 0, :], ALU.add), r=["q1", "Ep"], w=["Xc"])
                    S.op("dve", lambda k=k: V.tensor_tensor(C.Xc[:, 1, :], q2[:], Ep[:, k, 1, :], ALU.add), r=["q2", "Ep"], w=["Xc"])
                ck(4)
                fre = S4[:, 2, :].unsqueeze(2).broadcast_to([128, 128, 16])
                fim = S4[:, 3, :].unsqueeze(2).broadcast_to([128, 128, 16])
                S.op("dve", lambda: V.tensor_tensor(Bb[:, 0], Bz[:, 0], fre, ALU.mult), r=["Bz", "S4"], w=["Bb"])
                S.op("dve", lambda: V.tensor_tensor(tb1[:], Bz[:, 1], fim, ALU.mult), r=["Bz", "S4"], w=["tb1"])
                S.op("dve", lambda: V.tensor_tensor(Bb[:, 0], Bb[:, 0], tb1[:], ALU.subtract), r=["tb1", "Bb"], w=["Bb"])
                S.op("dve", lambda: V.tensor_tensor(Bb[:, 1], Bz[:, 1], fre, ALU.mult), r=["Bz", "S4"], w=["Bb"])
                S.op("dve", lambda: V.tensor_tensor(tb1[:], Bz[:, 0], fim, ALU.mult), r=["Bz", "S4"], w=["tb1"])
                S.op("dve", lambda: V.tensor_tensor(Bb[:, 1], Bb[:, 1], tb1[:], ALU.add), r=["tb1", "Bb"], w=["Bb"])
                mk = C.mask[:].rearrange("p (g c) -> p g c", c=16)
                for ri in range(2):
                    for ch in range(NCH):
                        S.op("dve", lambda ri=ri, ch=ch: V.tensor_tensor(
                            Bb[:, ri, ch * 8:(ch + 1) * 8, :], Bb[:, ri, ch * 8:(ch + 1) * 8, :], mk, ALU.mult),
                            r=["Bb", "mask"], w=["Bb"])
                ck(5)
                for ri in range(2):
                    for ch in range(NCH):
                        pi = (ri * NCH + ch) % 4
                        S.op("pe", lambda ri=ri, ch=ch, pi=pi: nc.tensor.transpose(
                            C.ps[pi][:, 0:128], Bb[:, ri, ch * 8:(ch + 1) * 8, :].rearrange("p g c -> p (g c)"), C.ident[:]),
                            r=["Bb", "ident"], w=[("ps", pi)])
                        S.op("act", lambda ri=ri, ch=ch, pi=pi: nc.scalar.copy(C.lbu[:, ch, ri, :], C.ps[pi][:, 0:128]),
                             r=[("ps", pi)], w=["lbu"])
                for ri in range(2):
                    for ch in range(NCH):
                        pi = (ri * NCH + ch) % 4
                        S.op("pe", lambda ri=ri, ch=ch, pi=pi: nc.tensor.transpose(
                            C.ps[pi][:, 0:128], Cin[:, ri, ch, :], C.ident[:]),
                            r=["Cin", "ident"], w=[("ps", pi)])
                        if ri == 0:
                            S.op("dve", lambda ri=ri, ch=ch, pi=pi: V.tensor_tensor(
                                C.cz[:, ch, ri, :], C.ps[pi][:, 0:128], C.mask[:], ALU.mult),
                                r=[("ps", pi), "mask"], w=["cz"])
                        else:
                            S.op("dve", lambda ri=ri, ch=ch, pi=pi: V.scalar_tensor_tensor(
                                C.cz[:, ch, ri, :], C.ps[pi][:, 0:128], -1.0, C.mask[:], ALU.mult, ALU.mult),
                                r=[("ps", pi), "mask"], w=["cz"])
                for ch in range(NCH):
                    S.op("dve", lambda ch=ch: V.tensor_scalar(C.dg[:, ch, :], C.ident[:], C.gn[:, 3, ch:ch + 1], None, ALU.mult),
                         r=["ident", "gains"], w=["dg"])
                S.barrier()

            ck(6)
            rmsnorm(S, nc, C, C.gn[:, 0, :])

            ck(7)
            with ExitStack() as st1:
                E1 = st1.enter_context
                Bf = E1(nc.sbuf_tensor("Bf", [128, 2, NPAIR, TBS], F32))
                t1 = E1(nc.sbuf_tensor("t1", [128, 2, NPAIR], F32))
                t2 = E1(nc.sbuf_tensor("t2", [128, 2, NPAIR], F32))
                if state_only:
                    tr1 = E1(nc.sbuf_tensor("tr1", [128, 2, NPAIR, TBS // 2], F32))
                    tr2 = E1(nc.sbuf_tensor("tr2", [128, 2, NPAIR, TBS // 2], F32))
                if not state_only:
                    Xb = E1(nc.sbuf_tensor("Xb", [128, 2, NPAIR, TBS], BF16))
                    ysb = E1(nc.sbuf_tensor("ysb", [128, 512], F32))
                    y2 = E1(nc.sbuf_tensor("y2", [128, 512], F32))
                    y3 = E1(nc.sbuf_tensor("y3", [128, 512], F32))
                V = nc.vector
                nblk = int(os.environ.get("L0_NBLK", T // TBS))
                for blk in range(nblk):
                    tsl = slice(blk * TBS, (blk + 1) * TBS)
                    for ri in range(2):
                        for half in range(2):
                            fns = []
                            for cc in range(8):
                                ch = half * 8 + cc
                                for j in range(4):
                                    fns.append(lambda cc=cc, ch=ch, j=j, ri=ri: nc.tensor.matmul(
                                        C.ps[j][:, cc * TBS:(cc + 1) * TBS],
                                        C.lbu[32 * j:32 * j + 32, ch, ri, :], C.h[32 * j:32 * j + 32, ch, tsl],
                                        start=True, stop=True, tile_position=(32 * j, 0)))
                            S.group("pe", fns, r=["lbu", "h"], w=[("ps", 0), ("ps", 1), ("ps", 2), ("ps", 3)])
                            for j in range(4):
                                S.op("act", lambda j=j, ri=ri, half=half: nc.scalar.copy(
                                    Bf[:, ri, half * 32 + j:half * 32 + 32:4, :],
                                    C.ps[j][:].rearrange("p (a t) -> p a t", t=TBS)),
                                    r=[("ps", j)], w=["Bf"])
                    if state_only:
                        for lv in range(6):
                            sgl = 1 << lv
                            m = TBS >> (lv + 1)
                            Lv = Bf[:, :, :, sgl - 1::2 * sgl]
                            Rv = Bf[:, :, :, 2 * sgl - 1::2 * sgl]
                            a1 = C.TA1[:, lv].unsqueeze(3).broadcast_to([128, 2, NPAIR, m])
                            S.op("dve", lambda: V.tensor_tensor(tr1[:, :, :, 0:m], Lv, a1, ALU.mult), r=["Bf", "TA"], w=["tr1"])
                            for ri in range(2):
                                a2 = C.TA2[:, lv, ri, :].unsqueeze(2).broadcast_to([128, NPAIR, m])
                                S.op("dve", lambda ri=ri, a2=a2: V.tensor_tensor(
                                    tr2[:, ri, :, 0:m], Bf[:, 1 - ri, :, sgl - 1::2 * sgl], a2, ALU.mult),
                                    r=["Bf", "TA"], w=["tr2"])
                            S.op("dve", lambda: V.tensor_tensor(Rv, Rv, tr1[:, :, :, 0:m], ALU.add), r=["tr1", "Bf"], w=["Bf"])
                            S.op("dve", lambda: V.tensor_tensor(Rv, Rv, tr2[:, :, :, 0:m], ALU.add), r=["tr2", "Bf"], w=["Bf"])
                        S.op("dve", lambda: V.tensor_tensor(t1[:], C.TA1[:, 6], C.Xc[:], ALU.mult), r=["Xc", "TA"], w=["t1"])
                        for ri in range(2):
                            S.op("dve", lambda ri=ri: V.tensor_tensor(t2[:, ri, :], C.TA2[:, 6, ri, :], C.Xc[:, 1 - ri, :], ALU.mult),
                                 r=["Xc", "TA"], w=["t2"])
                        S.op("dve", lambda: V.tensor_tensor(C.Xc[:], t1[:], t2[:], ALU.add), r=["t1", "t2"], w=["Xc"])
                        S.op("dve", lambda: V.tensor_tensor(C.Xc[:], C.Xc[:], Bf[:, :, :, TBS - 1], ALU.add), r=["Bf", "Xc"], w=["Xc"])
                        continue
                    for t in range(TBS):
                        prev = C.Xc[:] if t == 0 else Bf[:, :, :, t - 1]
                        cur = Bf[:, :, :, t]
                        S.op("dve", lambda prev=prev: V.tensor_tensor(t1[:], C.A1[:], prev, ALU.mult),
                             r=["Bf", "Xc", "A1"], w=["t1"])
                        for ri in range(2):
                            pv = C.Xc[:, 1 - ri, :] if t == 0 else Bf[:, 1 - ri, :, t - 1]
                            S.op("dve", lambda pv=pv, ri=ri: V.tensor_tensor(t2[:, ri, :], C.A2[:, ri, :], pv, ALU.mult),
                                 r=["Bf", "Xc", "A2"], w=["t2"])
                        S.op("dve", lambda cur=cur: V.tensor_tensor(cur, cur, t1[:], ALU.add), r=["t1", "Bf"], w=["Bf"])
                        S.op("dve", lambda cur=cur: V.tensor_tensor(cur, cur, t2[:], ALU.add), r=["t2", "Bf"], w=["Bf"])
                    S.op("dve", lambda: V.tensor_copy(C.Xc[:], Bf[:, :, :, TBS - 1]), r=["Bf"], w=["Xc"])
                    if state_only:
                        continue
                    S.op("act", lambda: nc.scalar.copy(Xb[:], Bf[:]), r=["Bf"], w=["Xb"])
                    for half in range(2):
                        pi = 4 + half
                        fns = []
                        for cc in range(8):
                            ch = half * 8 + cc
                            osl = slice(cc * TBS, (cc + 1) * TBS)
                            fns.append(lambda ch=ch, osl=osl, pi=pi: nc.tensor.matmul(
                                C.ps[pi][:, osl], C.dg[:, ch, :], C.h[:, ch, tsl], start=(ch % 8 == 0), stop=False))
                            for j in range(4):
                                pair = ch * 4 + j
                                for ri in range(2):
                                    fns.append(lambda ch=ch, osl=osl, pi=pi, j=j, ri=ri, pair=pair: nc.tensor.matmul(
                                        C.ps[pi][32 * j:32 * j + 32, osl], C.cz[:, ch, ri, 32 * j:32 * j + 32],
                                        Xb[:, ri, pair, :], start=False, stop=(ri == 1 and j == 3),
                                        tile_position=(0, 32 * j), skip_group_check=True))
                        S.group("pe", fns, r=["dg", "h", "cz", "Xb"], w=[("ps", pi)])
                        yp = C.ps[pi]
                        S.op("act", lambda yp=yp: nc.scalar.copy(ysb[:], yp[:]), r=[("ps", pi)], w=["ysb"])
                        S.op("dve", lambda: V.tensor_tensor(y2[:], ysb[:], ysb[:], ALU.mult), r=["ysb"], w=["y2"])
                        S.op("dve", lambda: V.tensor_scalar(y2[:], y2[:], 0.044715, 1.0, ALU.mult, ALU.add), r=["y2"], w=["y2"])
                        S.op("dve", lambda: V.tensor_tensor(y2[:], y2[:], ysb[:], ALU.mult), r=["y2", "ysb"], w=["y2"])
                        S.op("act", lambda: nc.scalar.activation(y3[:], y2[:], AF.Sigmoid, scale=1.5957691216057308),
                             r=["y2"], w=["y3"])
                        S.op("dve", lambda half=half: V.tensor_tensor(
                            C.h[:, half * 8:(half + 1) * 8, tsl], ysb[:].rearrange("p (a t) -> p a t", t=TBS),
                            y3[:].rearrange("p (a t) -> p a t", t=TBS), ALU.mult),
                            r=["ysb", "y3", "h"], w=["h"])
                S.dma("sp", xend, C.Xc[:], r=["Xc"], w=["xend_d"])
                S.barrier()
            st01.close()

            if not state_only:
                with ExitStack() as st2:
                    E2 = st2.enter_context
                    C.stg = [E2(nc.sbuf_tensor(f"stg{i}", [128, 2048], F32)) for i in range(3)]
                    C.stg_i = 0
                    C.wbuf = [E2(nc.sbuf_tensor(f"wb{i}", [128, NCH, 128], BF16)) for i in range(4)]
                    C.wb_i = 0
                    C.ps_i = 0
                    C.lin_ps = [0, 1, 2, 3]
                    C.out_ps = [4, 5]
                    wo = [E2(nc.sbuf_tensor(f"wo{i}", [128, 2048], BF16)) for i in range(4)]
                    hid = [E2(nc.sbuf_tensor(f"hid{i}", [128, T], BF16)) for i in range(4)]
                    sg = [E2(nc.sbuf_tensor(f"sg{i}", [128, 512], F32)) for i in range(2)]
                    val = [E2(nc.sbuf_tensor(f"val{i}", [128, 512], F32)) for i in range(2)]
                    V = nc.vector
                    st_ = {}

                    def glu_consume(oc, tb, pst, pk):
                        sl = slice(tb * 512, (tb + 1) * 512)
                        if oc % 2 == 0:
                            i = tb
                            S.op("act", lambda: nc.scalar.copy(val[i][:], pst[:]), r=[pk], w=[("val", i)])
                        else:
                            i = tb
                            c = oc // 2
                            S.op("act", lambda: nc.scalar.activation(sg[i][:], pst[:], AF.Sigmoid), r=[pk], w=[("sg", i)])
                            S.op("dve", lambda: V.tensor_tensor(sg[i][:], sg[i][:], val[i][:], ALU.mult),
                                 r=[("sg", i), ("val", i)], w=[("sg", i)])
                            S.op("dve", lambda: V.tensor_tensor(C.x[:, c, sl], C.x[:, c, sl], sg[i][:], ALU.add),
                                 r=[("sg", i), "x"], w=["x"])

                    for c in range(NCH):
                        for which in range(2):
                            linear_fm(S, nc, C, w_glu, 1, which * D + c * 128, C.h, "h",
                                      lambda oc, tb, pst, pk, which=which, c=c: glu_consume(2 * c + which, tb, pst, pk))
                    rmsnorm(S, nc, C, C.gn[:, 1, :])
                    ffn(S, nc, C, w_in, w_out, wo, hid, sg)
                    for c in range(NCH):
                        S.dma("sp", x1T[c * 128:(c + 1) * 128, :], C.x[:, c, :], r=["x"], w=["x1_d"])
                    rmsnorm(S, nc, C, C.gn[:, 2, :])
                    kvb = [E2(nc.sbuf_tensor(f"kvb{i}", [128, 512], BF16)) for i in range(2)]

                    def kv_consume(oc, tb, pst, pk):
                        sl = slice(tb * 512, (tb + 1) * 512)
                        i = (oc * 2 + tb) % 2
                        S.op("act", lambda: nc.scalar.copy(kvb[i][:], pst[:]), r=[pk], w=[("kvb", i)])
                        dst = kT if oc < 4 else vT
                        o4 = oc % 4
                        S.dma("sp", dst[o4 * 128:(o4 + 1) * 128, sl], kvb[i][:], r=[("kvb", i)], w=["kv_d"])
                    linear_fm(S, nc, C, w_kv, 8, 0, C.h, "h", kv_consume)
                    S.barrier()
        except _Stop:
            pass
        S.finish()
    return nc


def _key_sets():
    ks = []
    for kb in range(-1, 8):
        k0 = 128 * kb
        qs, qe = max(0, k0), min(T, k0 + 256)
        ks.append((k0 + 2048, 1, 128, qs, 1, qe - qs, qs - k0, 1 if kb < 0 else 0, 0))
    for r in range(4):
        for kc in (-128, 0, 128):
            qs, qe = max(0, kc), min(256, kc + 256)
            ks.append((r + 4 * kc + 2048, 4, 128, r + 4 * qs, 4, qe - qs, qs - kc, 1 if kc < 0 else 0, 1))
    for r in range(16):
        ks.append((r, 16, 128, r, 16, 64, 128, 2, 2))
        ks.append((r + 2048, 16, 64, r, 16, 64, 0, 0, 2))
    return ks


def build_l1():
    nc = bass.Bass("TRN2", target_bir_lowering=False)
    xT = _dram_in(nc, "xT", [D, T])
    kTa = _dram_in(nc, "kTa", [512, 3072], BF16)
    Va = _dram_in(nc, "Va", [3072, 512], BF16)
    gains = _dram_in(nc, "gains", [128, 4, NCH])
    bandm = _dram_in(nc, "bandm", [128, 256], BF16)
    kbias = _dram_in(nc, "kbias", [128, 4])
    w_q = _dram_in(nc, "w_q", [D, 3 * D])
    w_o = _dram_in(nc, "w_o", [D, D])
    w_in = _dram_in(nc, "w_in", [D, 2 * FF])
    w_out = _dram_in(nc, "w_out", [FF, D])
    outT = _dram_out(nc, "outT", [D, T])
    KS = _key_sets()
    with ExitStack() as st:
        E = st.enter_context
        S = Sched(nc, st)
        C = Ctx()
        V = nc.vector
        C.x = E(nc.sbuf_tensor("x", [128, NCH, T], F32))
        C.h = E(nc.sbuf_tensor("h", [128, NCH, T], BF16))
        C.ones = E(nc.sbuf_tensor("ones", [128, 128], BF16))
        C.gn = E(nc.sbuf_tensor("gn", [128, 4, NCH], F32))
        C.epsc = E(nc.sbuf_tensor("epsc", [128, 1], F32))
        C.rstd = E(nc.sbuf_tensor("rstd", [128, 512], F32))
        C.sq = [E(nc.sbuf_tensor(f"sq{i}", [128, 512], BF16)) for i in range(2)]
        C.ps = [E(nc.psum_tensor(f"ps{i}", [128, 512], F32)) for i in range(8)]
        S.op("dve", lambda: V.memset(C.ones[:], 1.0), w=["ones"])
        S.op("dve", lambda: V.memset(C.epsc[:], EPS), w=["epsc"])
        S.dma("sp", C.gn[:], gains, w=["gains"])
        for c in range(NCH):
            S.dma("sp", C.x[:, c, :], xT[c * 128:(c + 1) * 128, :], w=["x"])
        rmsnorm(S, nc, C, C.gn[:, 0, :])
        with ExitStack() as sa:
            EA = sa.enter_context
            attn = EA(nc.sbuf_tensor("attn", [128, NCH, T], BF16))
            with ExitStack() as sb:
                EB = sb.enter_context
                C.stg = [EB(nc.sbuf_tensor(f"stg{i}", [128, 2048], F32)) for i in range(3)]
                C.stg_i = 0
                C.wbuf = [EB(nc.sbuf_tensor(f"wb{i}", [128, NCH, 128], BF16)) for i in range(3)]
                C.wb_i = 0
                C.ps_i = 0
                C.lin_ps = [6, 7]
                mask = EB(nc.sbuf_tensor("bandS", [128, 256], BF16))
                kb_ = EB(nc.sbuf_tensor("kbS", [128, 4], F32))
                kts = EB(nc.sbuf_tensor("kts", [128, 3072], BF16))
                Vt = EB(nc.sbuf_tensor("Vt", [128, len(KS), 128], BF16))
                qg = [EB(nc.sbuf_tensor(f"qg{i}", [128, T], BF16)) for i in range(3)]
                pe_ = [EB(nc.sbuf_tensor(f"pe{i}", [128, 256], BF16)) for i in range(2)]
                pT = [EB(nc.sbuf_tensor(f"pT{i}", [128, 256], BF16)) for i in range(2)]
                rec = EB(nc.sbuf_tensor("rec", [128, 512], F32))
                S.dma("sp", mask[:], bandm, w=["mask"])
                S.dma("sp", kb_[:], kbias, w=["kb"])
                it = 0
                for hh in range(16):
                    kvh = hh // 4
                    if hh % 4 == 0:
                        S.dma("sp", kts[:], kTa[kvh * 128:(kvh + 1) * 128, :], w=["kts"])
                        for i, (k0, kst, m, q0, qst, n, off, bc, g) in enumerate(KS):
                            S.dma("sp", Vt[0:m, i, :], Va[k0:k0 + kst * (m - 1) + 1:kst, kvh * 128:(kvh + 1) * 128],
                                  w=["Vt"])
                    for g in range(3):
                        def q_consume(oc, tb, pst, pk, g=g):
                            sl = slice(tb * 512, (tb + 1) * 512)
                            S.op("act", lambda: nc.scalar.activation(qg[g][:, sl], pst[:], AF.Copy, scale=HD_SCALE),
                                 r=[pk], w=[("qg", g)])
                        linear_fm(S, nc, C, w_q, 1, g * D + hh * 128, C.h, "h", q_consume)
                    first = {0: True, 1: True}
                    for i, (k0, kst, m, q0, qst, n, off, bc, g) in enumerate(KS):
                        si = 4 + (it % 2)
                        pi_ = it % 2
                        it += 1
                        sps = C.ps[si]
                        S.group("pe", [lambda: nc.tensor.matmul(
                            sps[0:m, 0:n], kts[:, k0:k0 + kst * (m - 1) + 1:kst],
                            qg[g][:, q0:q0 + qst * (n - 1) + 1:qst], start=True, stop=True)],
                            r=["kts", ("qg", g)], w=[("ps", si)])
                        S.op("act", lambda: nc.scalar.activation(pe_[pi_][0:m, 0:n], sps[0:m, 0:n], AF.Exp,
                                                                 bias=kb_[0:m, bc:bc + 1], scale=1.0),
                             r=[("ps", si), "kb"], w=[("pe", pi_)])
                        S.op("dve", lambda: V.tensor_tensor(pT[pi_][0:m, 0:n], pe_[pi_][0:m, 0:n], mask[0:m, off:off + n], ALU.mult),
                             r=[("pe", pi_), "mask"], w=[("pT", pi_)])
                        segs = []
                        isplit = min(n, max(0, -(-(512 - q0) // qst)))
                        if isplit > 0:
                            segs.append((0, isplit, 0))
                        if isplit < n:
                            segs.append((isplit, n, 1))
                        fns = []
                        wk = []
                        for (i0, i1, bank) in segs:
                            c0 = q0 + qst * i0 - 512 * bank
                            cs = slice(c0, c0 + qst * (i1 - i0 - 1) + 1, qst)
                            st_ = first[bank]
                            first[bank] = False
                            fns.append(lambda cs=cs, bank=bank, i0=i0, i1=i1, st_=st_: nc.tensor.matmul(
                                C.ps[bank][:, cs], Vt[0:m, i, :], pT[pi_][0:m, i0:i1], start=st_, stop=False,
                                skip_group_check=True))
                            fns.append(lambda cs=cs, bank=bank, i0=i0, i1=i1, st_=st_: nc.tensor.matmul(
                                C.ps[2 + bank][:, cs], C.ones[0:m, :], pT[pi_][0:m, i0:i1], start=st_, stop=False,
                                skip_group_check=True))
                            wk += [("ps", bank), ("ps", 2 + bank)]
                        S.group("pe", fns, r=[("pT", pi_), "Vt", "ones"], w=wk)
                    for bank in range(2):
                        sl = slice(bank * 512, (bank + 1) * 512)
                        S.op("dve", lambda: V.reciprocal(rec[:], C.ps[2 + bank][:]), r=[("ps", 2 + bank)], w=["rec"])
                        S.op("dve", lambda: V.tensor_tensor(attn[:, hh, sl], C.ps[bank][:], rec[:], ALU.mult),
                             r=[("ps", bank), "rec"], w=["attn"])
                S.barrier()
            with ExitStack() as sc:
                EC = sc.enter_context
                C.stg = [EC(nc.sbuf_tensor(f"stgc{i}", [128, 2048], F32)) for i in range(3)]
                C.stg_i = 0
                C.wbuf = [EC(nc.sbuf_tensor(f"wbc{i}", [128, NCH, 128], BF16)) for i in range(4)]
                C.wb_i = 0
                C.lin_ps = [0, 1, 2, 3]

                def o_consume(oc, tb, pst, pk):
                    sl = slice(tb * 512, (tb + 1) * 512)
                    S.op("dve", lambda: V.tensor_tensor(C.x[:, oc, sl], C.x[:, oc, sl], pst[:], ALU.add),
                         r=[pk, "x"], w=["x"])
                linear_fm(S, nc, C, w_o, NCH, 0, attn, "attn", o_consume)
                S.barrier()
        with ExitStack() as sd:
            ED = sd.enter_context
            C.stg = [ED(nc.sbuf_tensor(f"stgd{i}", [128, 2048], F32)) for i in range(3)]
            C.stg_i = 0
            C.wbuf = [ED(nc.sbuf_tensor(f"wbd{i}", [128, NCH, 128], BF16)) for i in range(4)]
            C.wb_i = 0
            C.lin_ps = [0, 1, 2, 3]
            C.out_ps = [4, 5]
            wo = [ED(nc.sbuf_tensor(f"wo{i}", [128, 2048], BF16)) for i in range(4)]
            hid = [ED(nc.sbuf_tensor(f"hid{i}", [128, T], BF16)) for i in range(4)]
            sg = [ED(nc.sbuf_tensor(f"sg{i}", [128, 512], F32)) for i in range(2)]
            rmsnorm(S, nc, C, C.gn[:, 1, :])
            ffn(S, nc, C, w_in, w_out, wo, hid, sg)
            rmsnorm(S, nc, C, C.gn[:, 2, :], hkey="x", dst=C.x)
            for c in range(NCH):
                S.dma("sp", outT[c * 128:(c + 1) * 128, :], C.x[:, c, :], r=["x"], w=["out_d"])
            S.barrier()
        S.finish()
    return nc


HD_SCALE = 128 ** -0.5

_cache = {}


def _prog(name, fn):
    if name not in _cache:
        _cache[name] = fn()
    return _cache[name]


def _fm(v):
    return np.ascontiguousarray(np.asarray(v, np.float32).reshape(NCH, 128).T)


def layer0(inputs):
    x = np.asarray(inputs["x"], np.float32)
    n = 8
    gains = np.ascontiguousarray(np.stack([_fm(inputs["a_norm_mix"][0]), _fm(inputs["ffn_norm"][0]),
                                           _fm(inputs["kv_norm"]), _fm(inputs["s5_d"][0])], axis=1))
    ident = np.eye(128, dtype=np.float32)
    hp = np.arange(128) // 64
    g8 = (np.arange(128) // 16) % 2
    mask_sm = (hp[:, None] == g8[None, :]).astype(np.float32)
    base = {
        "lam_re": np.ascontiguousarray(inputs["s5_lam_re"][0]), "lam_im": np.ascontiguousarray(inputs["s5_lam_im"][0]),
        "log_dt": np.ascontiguousarray(inputs["s5_log_dt"][0].reshape(128, 1)),
        "b_re": np.ascontiguousarray(inputs["s5_b_re"][0]), "b_im": np.ascontiguousarray(inputs["s5_b_im"][0]),
        "c_re": np.ascontiguousarray(inputs["s5_c_re"][0].reshape(2048, 64)),
        "c_im": np.ascontiguousarray(inputs["s5_c_im"][0].reshape(2048, 64)),
        "gains": gains, "ident": ident, "mask_sm": mask_sm,
    }
    xTs = []
    for c in range(n):
        b, j = c // 4, c % 4
        xTs.append(np.ascontiguousarray(x[b, j * T:(j + 1) * T, :].T))
    zero_e = np.zeros((128, 3, 2, NPAIR), np.float32)
    nc_s = _prog("l0s", lambda: build_l0(True))
    maps = [dict(base, xT=xTs[c], eprev=zero_e) for c in range(n)]
    res = run_bass_kernel_spmd(nc_s, maps, core_ids=list(range(n)))
    eprev = []
    for c in range(n):
        b, j = c // 4, c % 4
        e = np.zeros((128, 3, 2, NPAIR), np.float32)
        for k in range(3):
            if j - 1 - k >= 0:
                e[:, k] = np.asarray(res.results[b * 4 + j - 1 - k]["xend"])
        eprev.append(e)
    nc_f = _prog("l0f", lambda: build_l0(False))
    full = dict(base, w_glu=np.ascontiguousarray(inputs["s5_w_glu"][0]),
                w_in=np.ascontiguousarray(inputs["ffn_w_in"][0]), w_out=np.ascontiguousarray(inputs["ffn_w_out"][0]),
                w_kv=np.ascontiguousarray(inputs["w_kv"]))
    maps = [dict(full, xT=xTs[c], eprev=eprev[c]) for c in range(n)]
    res = run_bass_kernel_spmd(nc_f, maps, core_ids=list(range(n)))
    return res.results


def layer1(inputs, r0):
    n = 8
    gains = np.ascontiguousarray(np.stack([_fm(inputs["b_norm_mix"][0]), _fm(inputs["ffn_norm"][1]),
                                           _fm(inputs["final_norm"]), _fm(inputs["final_norm"])], axis=1))
    bdt = np.asarray(r0[0]["kT"]).dtype
    sidx = np.arange(128)[:, None]
    qidx = np.arange(256)[None, :]
    bandm = ((sidx <= qidx) & (qidx <= sidx + 128)).astype(np.float32).astype(bdt)
    base = {"gains": gains, "bandm": bandm,
            "w_q": np.ascontiguousarray(inputs["attn_w_q"][0]), "w_o": np.ascontiguousarray(inputs["attn_w_o"][0]),
            "w_in": np.ascontiguousarray(inputs["ffn_w_in"][1]), "w_out": np.ascontiguousarray(inputs["ffn_w_out"][1])}
    maps = []
    for c in range(n):
        b, j = c // 4, c % 4
        kparts, vparts = [], []
        for jj in (j - 2, j - 1, j):
            if jj < 0:
                kparts.append(np.zeros((512, T), bdt))
                vparts.append(np.zeros((512, T), bdt))
            else:
                kparts.append(np.asarray(r0[b * 4 + jj]["kT"]))
                vparts.append(np.asarray(r0[b * 4 + jj]["vT"]))
        kTa = np.ascontiguousarray(np.concatenate(kparts, axis=1))
        Va = np.ascontiguousarray(np.concatenate(vparts, axis=1).T)
        kbias = np.zeros((128, 4), np.float32)
        if j == 0:
            kbias[:, 1] = -30000.0
            kbias[:, 2] = -30000.0
        elif j == 1:
            kbias[:64, 2] = -30000.0
        maps.append(dict(base, xT=np.ascontiguousarray(r0[c]["x1T"]), kTa=kTa, Va=Va, kbias=kbias))
    nc1 = _prog("l1", build_l1)
    res = run_bass_kernel_spmd(nc1, maps, core_ids=list(range(n)))
    return res.results


def kernel(**inputs):
    inputs = {k: np.asarray(v) for k, v in inputs.items()}
    r0 = layer0(inputs)
    r1 = layer1(inputs, r0)
    out = np.empty((2, 4096, D), np.float32)
    for c in range(8):
        b, j = c // 4, c % 4
        out[b, j * T:(j + 1) * T, :] = np.asarray(r1[c]["outT"]).T
    return out
```

```python
import math
from contextlib import ExitStack
import numpy as np
import concourse.bass as bass
import concourse.mybir as mybir
from concourse.bass_utils import run_bass_kernel_spmd

F32 = mybir.dt.float32
BF16 = mybir.dt.bfloat16
AF = mybir.ActivationFunctionType
ALU = mybir.AluOpType

D = 2048
NCH = 16
T = 1024
NTB = 2
FF = 5632
NF = 44
EPS = 1e-6
TBS = 64
NPAIR = 64


class Sched:
    def __init__(self, nc, st):
        self.nc = nc
        self.eng = {"pe": nc.tensor, "act": nc.scalar, "dve": nc.vector, "pool": nc.gpsimd, "sp": nc.sync}
        self.sem = {}
        self.cnt = {}
        for e in ["pe", "act", "dve", "pool"]:
            self.sem["c_" + e] = st.enter_context(nc.semaphore("c_" + e))
            self.cnt["c_" + e] = 0
        for q in ["sp", "act", "pool"]:
            self.sem["d_" + q] = st.enter_context(nc.semaphore("d_" + q))
            self.cnt["d_" + q] = 0
        self.seen = {e: {} for e in self.eng}
        self.last_w = {}
        self.readers = {}
        self.stopped = False
        _CUR[0] = self

    def _wait(self, e, deps):
        need = {}
        for (c, v) in deps:
            if c == "c_" + e and e == "pe":
                continue
            if v > need.get(c, 0):
                need[c] = v
        for c, v in need.items():
            if self.seen[e].get(c, 0) >= v:
                continue
            self.eng[e].wait_ge(self.sem[c], v)
            self.seen[e][c] = v

    def _deps(self, r, w):
        deps = []
        for k in r:
            if k in self.last_w:
                deps.append(self.last_w[k])
        for k in w:
            if k in self.last_w:
                deps.append(self.last_w[k])
            deps.extend(self.readers.get(k, []))
        return deps

    def _record(self, tok, r, w):
        for k in w:
            self.last_w[k] = tok
            self.readers[k] = []
        for k in r:
            self.readers.setdefault(k, []).append(tok)

    def op(self, e, fn, r=(), w=()):
        if self.stopped:
            return
        self._wait(e, self._deps(r, w))
        ins = fn()
        c = "c_" + e
        self.cnt[c] += 1
        ins.then_inc(self.sem[c], 1)
        self._record((c, self.cnt[c]), r, w)

    def group(self, e, fns, r=(), w=()):
        if self.stopped:
            return
        self._wait(e, self._deps(r, w))
        ins = None
        for f in fns:
            ins = f()
        c = "c_" + e
        self.cnt[c] += 1
        ins.then_inc(self.sem[c], 1)
        self._record((c, self.cnt[c]), r, w)

    def dma(self, q, out, in_, r=(), w=(), **kw):
        if self.stopped:
            return
        self._wait(q, self._deps(r, w))
        ins = self.eng[q].dma_start(out=out, in_=in_, **kw)
        c = "d_" + q
        self.cnt[c] += 16
        ins.then_inc(self.sem[c], 16)
        self._record((c, self.cnt[c]), r, w)

    def barrier(self):
        if self.stopped:
            return
        for e in self.eng:
            deps = [(c, v) for c, v in self.cnt.items() if v > 0]
            self._wait(e, deps)
        self.last_w = {}
        self.readers = {}

    def finish(self):
        deps = [(c, v) for c, v in self.cnt.items() if v > 0]
        self._wait("sp", deps)


def _dram_in(nc, name, shape, dt=F32):
    return nc.dram_tensor(name, list(shape), dt, kind="ExternalInput").ap()


def _dram_out(nc, name, shape, dt=F32):
    return nc.dram_tensor(name, list(shape), dt, kind="ExternalOutput").ap()


class Ctx:
    pass


class _Stop(Exception):
    pass


import os
STOP = int(os.environ.get("L0_STOP", "0"))


_CUR = [None]


def ck(k):
    if STOP == k:
        _CUR[0].stopped = True


def rmsnorm(S, nc, C, gain, hkey="h", dst=None):
    for tb in range(NTB):
        sl = slice(tb * 512, (tb + 1) * 512)
        ss = C.ps[6 + tb]
        fns = []
        for c in range(NCH):
            sq = C.sq[c % 2]
            S.op("act", lambda c=c, sq=sq: nc.scalar.activation(sq[:], C.x[:, c, sl], AF.Square),
                 r=["x"], w=[("sq", c % 2)])
            S.op("pe", lambda c=c, sq=sq: nc.tensor.matmul(ss[:], C.ones[:], sq[:], start=(c == 0), stop=(c == NCH - 1)),
                 r=[("sq", c % 2), "ones"], w=[("ps", 6 + tb)])
        S.op("act", lambda: nc.scalar.activation(C.rstd[:], ss[:], AF.Sqrt, bias=C.epsc[:, 0:1], scale=1.0 / D),
             r=[("ps", 6 + tb)], w=["rstd"])
        S.op("dve", lambda: nc.vector.reciprocal(C.rstd[:], C.rstd[:]), r=["rstd"], w=["rstd"])
        for c in range(NCH):
            S.op("dve", lambda c=c: nc.vector.scalar_tensor_tensor(
                (C.h if dst is None else dst)[:, c, sl], C.x[:, c, sl], gain[:, c:c + 1], C.rstd[:], ALU.mult, ALU.mult),
                r=["x", "rstd", "gains"], w=[hkey])


def load_cast(S, nc, C, src_ap, dst, dkey, shape3=None):
    i = C.stg_i
    C.stg_i += 1
    s = i % len(C.stg)
    stg = C.stg[s]
    sv = stg[:] if shape3 is None else stg[:].rearrange("p (k n) -> p k n", k=shape3[0])
    S.dma("sp", sv, src_ap, w=[("stg", s)])
    dv = dst
    S.op("act", lambda: nc.scalar.copy(dv, sv), r=[("stg", s)], w=[dkey])


def linear_fm(S, nc, C, w_ap, n_out_chunks, col0, src, skey, consume):
    for oc in range(n_out_chunks):
        wb = C.wbuf[C.wb_i % len(C.wbuf)]
        wk = ("wb", C.wb_i % len(C.wbuf))
        C.wb_i += 1
        c0 = col0 + oc * 128
        load_cast(S, nc, C, w_ap[:, c0:c0 + 128].rearrange("(k p) n -> p k n", p=128),
                  wb[:], wk, shape3=(NCH, 128))
        for tb in range(NTB):
            pi = C.lin_ps[C.ps_i % len(C.lin_ps)]
            C.ps_i += 1
            pst = C.ps[pi]
            sl = slice(tb * 512, (tb + 1) * 512)
            S.group("pe", [lambda k=k: nc.tensor.matmul(pst[:], wb[:, k, :], src[:, k, sl],
                                                        start=(k == 0), stop=(k == NCH - 1))
                           for k in range(NCH)],
                    r=[wk, skey], w=[("ps", pi)])
            consume(oc, tb, pst, ("ps", pi))


def ffn(S, nc, C, w_in, w_out, wo, hid, sg):
    V = nc.vector
    GS = 2
    for g in range(NF // GS):
        for fi in range(GS):
            f = g * GS + fi
            hb = hid[f % 4]
            hk = ("hid", f % 4)
            wob = wo[f % 4]
            load_cast(S, nc, C, w_out[f * 128:(f + 1) * 128, :], wob[:], ("wo", f % 4))

            def ffn_consume(oc, tb, pst, pk, hb=hb, hk=hk):
                sl = slice(tb * 512, (tb + 1) * 512)
                if oc == 0:
                    S.op("act", lambda: nc.scalar.activation(sg[tb][:], pst[:], AF.Silu), r=[pk], w=[("sg", tb)])
                else:
                    S.op("dve", lambda: V.tensor_tensor(hb[:, sl], sg[tb][:], pst[:], ALU.mult),
                         r=[pk, ("sg", tb)], w=[hk])
            linear_fm(S, nc, C, w_in, 1, f * 128, C.h, "h",
                      lambda oc, tb, pst, pk: ffn_consume(0, tb, pst, pk))
            linear_fm(S, nc, C, w_in, 1, FF + f * 128, C.h, "h",
                      lambda oc, tb, pst, pk: ffn_consume(1, tb, pst, pk))
        for c in range(NCH):
            for tb in range(NTB):
                sl = slice(tb * 512, (tb + 1) * 512)
                pi = C.out_ps[C.ps_i % 2]
                C.ps_i += 1
                fns = []
                rk = []
                for fi in range(GS):
                    f = g * GS + fi
                    rk += [("wo", f % 4), ("hid", f % 4)]
                    fns.append(lambda f=f, fi=fi, pi=pi, c=c, sl=sl: nc.tensor.matmul(
                        C.ps[pi][:], wo[f % 4][:, c * 128:(c + 1) * 128], hid[f % 4][:, sl],
                        start=(fi == 0), stop=(fi == GS - 1)))
                S.group("pe", fns, r=rk, w=[("ps", pi)])
                S.op("dve", lambda pi=pi, c=c, sl=sl: V.tensor_tensor(C.x[:, c, sl], C.x[:, c, sl], C.ps[pi][:], ALU.add),
                     r=[("ps", pi), "x"], w=["x"])


def build_l0(state_only):
    nc = bass.Bass("TRN2", target_bir_lowering=False)
    xT = _dram_in(nc, "xT", [D, T])
    lam_re = _dram_in(nc, "lam_re", [128, 64])
    lam_im = _dram_in(nc, "lam_im", [128, 64])
    log_dt = _dram_in(nc, "log_dt", [128, 1])
    b_re = _dram_in(nc, "b_re", [128, 64, 16])
    b_im = _dram_in(nc, "b_im", [128, 64, 16])
    c_re = _dram_in(nc, "c_re", [2048, 64])
    c_im = _dram_in(nc, "c_im", [2048, 64])
    gains = _dram_in(nc, "gains", [128, 4, NCH])
    eprev = _dram_in(nc, "eprev", [128, 3, 2, NPAIR])
    ident_d = _dram_in(nc, "ident", [128, 128])
    mask_d = _dram_in(nc, "mask_sm", [128, 128])
    xend = _dram_out(nc, "xend", [128, 2, NPAIR])
    if not state_only:
        w_glu = _dram_in(nc, "w_glu", [D, 2 * D])
        w_in = _dram_in(nc, "w_in", [D, 2 * FF])
        w_out = _dram_in(nc, "w_out", [FF, D])
        w_kv = _dram_in(nc, "w_kv", [D, 1024])
        x1T = _dram_out(nc, "x1T", [D, T])
        kT = _dram_out(nc, "kT", [512, T], BF16)
        vT = _dram_out(nc, "vT", [512, T], BF16)

    with ExitStack() as st:
        E = st.enter_context
        S = Sched(nc, st)
        C = Ctx()
        C.x = E(nc.sbuf_tensor("x", [128, NCH, T], F32))
        C.h = E(nc.sbuf_tensor("h", [128, NCH, T], BF16))
        C.ones = E(nc.sbuf_tensor("ones", [128, 128], BF16))
        C.ident = E(nc.sbuf_tensor("identS", [128, 128], F32))
        C.mask = E(nc.sbuf_tensor("maskS", [128, 128], F32))
        C.gn = E(nc.sbuf_tensor("gn", [128, 4, NCH], F32))
        C.epsc = E(nc.sbuf_tensor("epsc", [128, 1], F32))
        C.rstd = E(nc.sbuf_tensor("rstd", [128, 512], F32))
        C.sq = [E(nc.sbuf_tensor(f"sq{i}", [128, 512], BF16)) for i in range(2)]
        C.ps = [E(nc.psum_tensor(f"ps{i}", [128, 512], F32)) for i in range(8)]
        C.Xc = E(nc.sbuf_tensor("Xc", [128, 2, NPAIR], F32))
        st01 = ExitStack()
        E01 = st01.enter_context
        C.A1 = E01(nc.sbuf_tensor("A1", [128, 2, NPAIR], F32))
        C.A2 = E01(nc.sbuf_tensor("A2", [128, 2, NPAIR], F32))
        C.lbu = E01(nc.sbuf_tensor("lbu", [128, NCH, 2, 128], BF16))
        C.cz = E01(nc.sbuf_tensor("cz", [128, NCH, 2, 128], BF16))
        C.dg = E01(nc.sbuf_tensor("dg", [128, NCH, 128], BF16))
        C.TA1 = E01(nc.sbuf_tensor("TA1", [128, 7, 2, NPAIR], F32))
        C.TA2 = E01(nc.sbuf_tensor("TA2", [128, 7, 2, NPAIR], F32))

        try:
            S.op("dve", lambda: nc.vector.memset(C.ones[:], 1.0), w=["ones"])
            S.op("dve", lambda: nc.vector.memset(C.epsc[:], EPS), w=["epsc"])
            S.dma("sp", C.ident[:], ident_d, w=["ident"])
            S.dma("sp", C.mask[:], mask_d, w=["mask"])
            S.dma("sp", C.gn[:], gains, w=["gains"])
            for c in range(NCH):
                S.dma("sp", C.x[:, c, :], xT[c * 128:(c + 1) * 128, :], w=["x"])

            ck(1)
            with ExitStack() as st0:
                E0 = st0.enter_context
                lr = E0(nc.sbuf_tensor("lr", [128, 64], F32))
                li = E0(nc.sbuf_tensor("li", [128, 64], F32))
                ldt = E0(nc.sbuf_tensor("ldt", [128, 1], F32))
                dt = E0(nc.sbuf_tensor("dt", [128, 1], F32))
                mag = E0(nc.sbuf_tensor("mag", [128, 64], F32))
                ang = E0(nc.sbuf_tensor("ang", [128, 64], F32))
                angc = E0(nc.sbuf_tensor("angc", [128, 64], F32))
                tm = E0(nc.sbuf_tensor("tm", [128, 64], F32))
                tm2 = E0(nc.sbuf_tensor("tm2", [128, 64], F32))
                nr = E0(nc.sbuf_tensor("nr", [128, 64], F32))
                rden = E0(nc.sbuf_tensor("rden", [128, 64], F32))
                T4 = E0(nc.sbuf_tensor("T4", [128, 4, 2, 64], F32))
                S4 = E0(nc.sbuf_tensor("S4", [128, 4, 128], F32))
                Bz = E0(nc.sbuf_tensor("Bz", [128, 2, 128, 16], F32))
                Bb = E0(nc.sbuf_tensor("Bb", [128, 2, 128, 16], F32))
                tb1 = E0(nc.sbuf_tensor("tb1", [128, 128, 16], F32))
                Cin = E0(nc.sbuf_tensor("Cin", [128, 2, NCH, 128], F32))
                dgf = E0(nc.sbuf_tensor("dgf", [128, 128], F32))

                S.dma("sp", lr[:], lam_re, w=["lr"])
                S.dma("sp", li[:], lam_im, w=["li"])
                S.dma("sp", ldt[:], log_dt, w=["ldt"])
                for ri, src in enumerate([b_re, b_im]):
                    for hh in range(2):
                        for gh in range(2):
                            S.dma("sp", Bz[hh * 64:(hh + 1) * 64, ri, gh * 64:(gh + 1) * 64, :],
                                  src[gh * 64:(gh + 1) * 64].rearrange("g p c -> p g c"), w=["Bz"])
                for ri, src in enumerate([c_re, c_im]):
                    for dup in range(2):
                        S.dma("sp", Cin[:, ri, :, dup * 64:(dup + 1) * 64],
                              src.rearrange("(k q) p -> q k p", q=128), w=["Cin"])

                ck(2)
                V = nc.vector
                S.op("act", lambda: nc.scalar.activation(dt[:], ldt[:], AF.Exp), r=["ldt"], w=["dt"])
                S.op("act", lambda: nc.scalar.activation(mag[:], lr[:], AF.Exp, scale=dt[:, 0:1]), r=["lr", "dt"], w=["mag"])
                S.op("dve", lambda: V.tensor_scalar(ang[:], li[:], dt[:, 0:1], None, ALU.mult), r=["li", "dt"], w=["ang"])
                S.op("dve", lambda: V.tensor_scalar(angc[:], ang[:], math.pi / 2, None, ALU.add), r=["ang"], w=["angc"])
                for a_, key in ((ang, "ang"), (angc, "angc")):
                    for _ in range(4):
                        S.op("dve", lambda a_=a_: V.tensor_scalar(tm[:], a_[:], math.pi, 2 * math.pi, ALU.is_gt, ALU.mult),
                             r=[key], w=["tm"])
                        S.op("dve", lambda a_=a_: V.tensor_tensor(a_[:], a_[:], tm[:], ALU.subtract), r=["tm", key], w=[key])
                S.op("act", lambda: nc.scalar.activation(ang[:], ang[:], AF.Sin), r=["ang"], w=["ang"])
                S.op("act", lambda: nc.scalar.activation(angc[:], angc[:], AF.Sin), r=["angc"], w=["angc"])
                for dup in range(2):
                    S.op("dve", lambda dup=dup: V.tensor_tensor(T4[:, 0, dup, :], mag[:], angc[:], ALU.mult),
                         r=["mag", "angc"], w=["T4"])
                    S.op("dve", lambda dup=dup: V.tensor_tensor(T4[:, 1, dup, :], mag[:], ang[:], ALU.mult),
                         r=["mag", "ang"], w=["T4"])
                a_n = T4[:, 0, 0, :]
                b_n = T4[:, 1, 0, :]
                S.op("dve", lambda: V.tensor_scalar(nr[:], a_n, -1.0, None, ALU.add), r=["T4"], w=["nr"])
                S.op("dve", lambda: V.tensor_tensor(tm[:], lr[:], lr[:], ALU.mult), r=["lr"], w=["tm"])
                S.op("dve", lambda: V.tensor_tensor(tm2[:], li[:], li[:], ALU.mult), r=["li"], w=["tm2"])
                S.op("dve", lambda: V.tensor_tensor(tm[:], tm[:], tm2[:], ALU.add), r=["tm", "tm2"], w=["tm"])
                S.op("dve", lambda: V.reciprocal(rden[:], tm[:]), r=["tm"], w=["rden"])
                S.op("dve", lambda: V.tensor_tensor(tm[:], nr[:], lr[:], ALU.mult), r=["nr", "lr"], w=["tm"])
                S.op("dve", lambda: V.tensor_tensor(tm2[:], b_n, li[:], ALU.mult), r=["T4", "li"], w=["tm2"])
                S.op("dve", lambda: V.tensor_tensor(tm[:], tm[:], tm2[:], ALU.add), r=["tm", "tm2"], w=["tm"])
                for dup in range(2):
                    S.op("dve", lambda dup=dup: V.tensor_tensor(T4[:, 2, dup, :], tm[:], rden[:], ALU.mult),
                         r=["tm", "rden"], w=["T4"])
                S.op("dve", lambda: V.tensor_tensor(tm[:], b_n, lr[:], ALU.mult), r=["T4", "lr"], w=["tm"])
                S.op("dve", lambda: V.tensor_tensor(tm2[:], nr[:], li[:], ALU.mult), r=["nr", "li"], w=["tm2"])
                S.op("dve", lambda: V.tensor_tensor(tm[:], tm[:], tm2[:], ALU.subtract), r=["tm", "tm2"], w=["tm"])
                for dup in range(2):
                    S.op("dve", lambda dup=dup: V.tensor_tensor(T4[:, 3, dup, :], tm[:], rden[:], ALU.mult),
                         r=["tm", "rden"], w=["T4"])
                ck(3)
                for q in range(4):
                    S.op("pe", lambda q=q: nc.tensor.transpose(C.ps[0][:, 0:128], T4[:, q, :, :].rearrange("g a p -> g (a p)"), C.ident[:]),
                         r=["T4", "ident"], w=[("ps", 0)])
                    S.op("dve", lambda q=q: V.tensor_copy(S4[:, q, :], C.ps[0][:, 0:128]), r=[("ps", 0)], w=["S4"])
                for hh in range(2):
                    ps_ = slice(hh * 64, (hh + 1) * 64)
                    for ri in range(2):
                        S.op("dve", lambda hh=hh, ps_=ps_, ri=ri: V.tensor_copy(C.A1[ps_, ri, :], S4[ps_, 0, hh::2]),
                             r=["S4"], w=["A1"])
                    S.op("dve", lambda hh=hh, ps_=ps_: V.tensor_scalar(C.A2[ps_, 0, :], S4[ps_, 1, hh::2], -1.0, None, ALU.mult),
                         r=["S4"], w=["A2"])
                    S.op("dve", lambda hh=hh, ps_=ps_: V.tensor_copy(C.A2[ps_, 1, :], S4[ps_, 1, hh::2]),
                         r=["S4"], w=["A2"])
                Ep = E0(nc.sbuf_tensor("Ep", [128, 3, 2, NPAIR], F32))
                mr = E0(nc.sbuf_tensor("mr", [128, NPAIR], F32))
                mi = E0(nc.sbuf_tensor("mi", [128, NPAIR], F32))
                q1 = E0(nc.sbuf_tensor("q1", [128, NPAIR], F32))
                q2 = E0(nc.sbuf_tensor("q2", [128, NPAIR], F32))
                q3 = E0(nc.sbuf_tensor("q3", [128, NPAIR], F32))
                S.dma("sp", Ep[:], eprev, w=["Ep"])
                S.op("dve", lambda: V.tensor_copy(mr[:], C.A1[:, 0, :]), r=["A1"], w=["mr"])
                S.op("dve", lambda: V.tensor_copy(mi[:], C.A2[:, 1, :]), r=["A2"], w=["mi"])
                for lv in range(10):
                    if lv < 7:
                        S.op("dve", lambda lv=lv: V.tensor_copy(C.TA1[:, lv, 0, :], mr[:]), r=["mr"], w=["TA"])
                        S.op("dve", lambda lv=lv: V.tensor_copy(C.TA1[:, lv, 1, :], mr[:]), r=["mr"], w=["TA"])
                        S.op("dve", lambda lv=lv: V.tensor_scalar(C.TA2[:, lv, 0, :], mi[:], -1.0, None, ALU.mult), r=["mi"], w=["TA"])
                        S.op("dve", lambda lv=lv: V.tensor_copy(C.TA2[:, lv, 1, :], mi[:]), r=["mi"], w=["TA"])
                    S.op("dve", lambda: V.tensor_tensor(q1[:], mr[:], mr[:], ALU.mult), r=["mr"], w=["q1"])
                    S.op("dve", lambda: V.tensor_tensor(q2[:], mi[:], mi[:], ALU.mult), r=["mi"], w=["q2"])
                    S.op("dve", lambda: V.tensor_tensor(q3[:], mr[:], mi[:], ALU.mult), r=["mr", "mi"], w=["q3"])
                    S.op("dve", lambda: V.tensor_tensor(mr[:], q1[:], q2[:], ALU.subtract), r=["q1", "q2"], w=["mr"])
                    S.op("dve", lambda: V.tensor_scalar(mi[:], q3[:], 2.0, None, ALU.mult), r=["q3"], w=["mi"])
                S.op("dve", lambda: V.tensor_copy(C.Xc[:], Ep[:, 2]), r=["Ep"], w=["Xc"])
                for k in (1, 0):
                    xr, xi = C.Xc[:, 0, :], C.Xc[:, 1, :]
                    S.op("dve", lambda: V.tensor_tensor(q1[:], mr[:], xr, ALU.mult), r=["mr", "Xc"], w=["q1"])
                    S.op("dve", lambda: V.tensor_tensor(q2[:], mi[:], xi, ALU.mult), r=["mi", "Xc"], w=["q2"])
                    S.op("dve", lambda: V.tensor_tensor(q1[:], q1[:], q2[:], ALU.subtract), r=["q1", "q2"], w=["q1"])
                    S.op("dve", lambda: V.tensor_tensor(q2[:], mr[:], xi, ALU.mult), r=["mr", "Xc"], w=["q2"])
                    S.op("dve", lambda: V.tensor_tensor(q3[:], mi[:], xr, ALU.mult), r=["mi", "Xc"], w=["q3"])
                    S.op("dve", lambda: V.tensor_tensor(q2[:], q2[:], q3[:], ALU.add), r=["q2", "q3"], w=["q2"])
                    S.op("dve", lambda k=k: V.tensor_tensor(C.Xc[:, 0, :], q1[:], Ep[:, k, 0, :], ALU.add), r=["q1", "Ep"], w=["Xc"])
                    S.op("dve", lambda k=k: V.tensor_tensor(C.Xc[:, 1, :], q2[:], Ep[:, k, 1, :], ALU.add), r=["q2", "Ep"], w=["Xc"])
                ck(4)
                fre = S4[:, 2, :].unsqueeze(2).broadcast_to([128, 128, 16])
                fim = S4[:, 3, :].unsqueeze(2).broadcast_to([128, 128, 16])
                S.op("dve", lambda: V.tensor_tensor(Bb[:, 0], Bz[:, 0], fre, ALU.mult), r=["Bz", "S4"], w=["Bb"])
                S.op("dve", lambda: V.tensor_tensor(tb1[:], Bz[:, 1], fim, ALU.mult), r=["Bz", "S4"], w=["tb1"])
                S.op("dve", lambda: V.tensor_tensor(Bb[:, 0], Bb[:, 0], tb1[:], ALU.subtract), r=["tb1", "Bb"], w=["Bb"])
                S.op("dve", lambda: V.tensor_tensor(Bb[:, 1], Bz[:, 1], fre, ALU.mult), r=["Bz", "S4"], w=["Bb"])
                S.op("dve", lambda: V.tensor_tensor(tb1[:], Bz[:, 0], fim, ALU.mult), r=["Bz", "S4"], w=["tb1"])
                S.op("dve", lambda: V.tensor_tensor(Bb[:, 1], Bb[:, 1], tb1[:], ALU.add), r=["tb1", "Bb"], w=["Bb"])
                mk = C.mask[:].rearrange("p (g c) -> p g c", c=16)
                for ri in range(2):
                    for ch in range(NCH):
                        S.op("dve", lambda ri=ri, ch=ch: V.tensor_tensor(
                            Bb[:, ri, ch * 8:(ch + 1) * 8, :], Bb[:, ri, ch * 8:(ch + 1) * 8, :], mk, ALU.mult),
                            r=["Bb", "mask"], w=["Bb"])
                ck(5)
                for ri in range(2):
                    for ch in range(NCH):
                        pi = (ri * NCH + ch) % 4
                        S.op("pe", lambda ri=ri, ch=ch, pi=pi: nc.tensor.transpose(
                            C.ps[pi][:, 0:128], Bb[:, ri, ch * 8:(ch + 1) * 8, :].rearrange("p g c -> p (g c)"), C.ident[:]),
                            r=["Bb", "ident"], w=[("ps", pi)])
                        S.op("act", lambda ri=ri, ch=ch, pi=pi: nc.scalar.copy(C.lbu[:, ch, ri, :], C.ps[pi][:, 0:128]),
                             r=[("ps", pi)], w=["lbu"])
                for ri in range(2):
                    for ch in range(NCH):
                        pi = (ri * NCH + ch) % 4
                        S.op("pe", lambda ri=ri, ch=ch, pi=pi: nc.tensor.transpose(
                            C.ps[pi][:, 0:128], Cin[:, ri, ch, :], C.ident[:]),
                            r=["Cin", "ident"], w=[("ps", pi)])
                        if ri == 0:
                            S.op("dve", lambda ri=ri, ch=ch, pi=pi: V.tensor_tensor(
                                C.cz[:, ch, ri, :], C.ps[pi][:, 0:128], C.mask[:], ALU.mult),
                                r=[("ps", pi), "mask"], w=["cz"])
                        else:
                            S.op("dve", lambda ri=ri, ch=ch, pi=pi: V.scalar_tensor_tensor(
                                C.cz[:, ch, ri, :], C.ps[pi][:, 0:128], -1.0, C.mask[:], ALU.mult, ALU.mult),
                                r=[("ps", pi), "mask"], w=["cz"])
                for ch in range(NCH):
                    S.op("dve", lambda ch=ch: V.tensor_scalar(C.dg[:, ch, :], C.ident[:], C.gn[:, 3, ch:ch + 1], None, ALU.mult),
                         r=["ident", "gains"], w=["dg"])
                S.barrier()

            ck(6)
            rmsnorm(S, nc, C, C.gn[:, 0, :])

            ck(7)
            with ExitStack() as st1:
                E1 = st1.enter_context
                Bf = E1(nc.sbuf_tensor("Bf", [128, 2, NPAIR, TBS], F32))
                t1 = E1(nc.sbuf_tensor("t1", [128, 2, NPAIR], F32))
                t2 = E1(nc.sbuf_tensor("t2", [128, 2, NPAIR], F32))
                if state_only:
                    tr1 = E1(nc.sbuf_tensor("tr1", [128, 2, NPAIR, TBS // 2], F32))
                    tr2 = E1(nc.sbuf_tensor("tr2", [128, 2, NPAIR, TBS // 2], F32))
                if not state_only:
                    Xb = E1(nc.sbuf_tensor("Xb", [128, 2, NPAIR, TBS], BF16))
                    ysb = E1(nc.sbuf_tensor("ysb", [128, 512], F32))
                    y2 = E1(nc.sbuf_tensor("y2", [128, 512], F32))
                    y3 = E1(nc.sbuf_tensor("y3", [128, 512], F32))
                V = nc.vector
                nblk = int(os.environ.get("L0_NBLK", T // TBS))
                for blk in range(nblk):
                    tsl = slice(blk * TBS, (blk + 1) * TBS)
                    for ri in range(2):
                        for half in range(2):
                            fns = []
                            for cc in range(8):
                                ch = half * 8 + cc
                                for j in range(4):
                                    fns.append(lambda cc=cc, ch=ch, j=j, ri=ri: nc.tensor.matmul(
                                        C.ps[j][:, cc * TBS:(cc + 1) * TBS],
                                        C.lbu[32 * j:32 * j + 32, ch, ri, :], C.h[32 * j:32 * j + 32, ch, tsl],
                                        start=True, stop=True, tile_position=(32 * j, 0)))
                            S.group("pe", fns, r=["lbu", "h"], w=[("ps", 0), ("ps", 1), ("ps", 2), ("ps", 3)])
                            for j in range(4):
                                S.op("act", lambda j=j, ri=ri, half=half: nc.scalar.copy(
                                    Bf[:, ri, half * 32 + j:half * 32 + 32:4, :],
                                    C.ps[j][:].rearrange("p (a t) -> p a t", t=TBS)),
                                    r=[("ps", j)], w=["Bf"])
                    if state_only:
                        for lv in range(6):
                            sgl = 1 << lv
                            m = TBS >> (lv + 1)
                            Lv = Bf[:, :, :, sgl - 1::2 * sgl]
                            Rv = Bf[:, :, :, 2 * sgl - 1::2 * sgl]
                            a1 = C.TA1[:, lv].unsqueeze(3).broadcast_to([128, 2, NPAIR, m])
                            S.op("dve", lambda: V.tensor_tensor(tr1[:, :, :, 0:m], Lv, a1, ALU.mult), r=["Bf", "TA"], w=["tr1"])
                            for ri in range(2):
                                a2 = C.TA2[:, lv, ri, :].unsqueeze(2).broadcast_to([128, NPAIR, m])
                                S.op("dve", lambda ri=ri, a2=a2: V.tensor_tensor(
                                    tr2[:, ri, :, 0:m], Bf[:, 1 - ri, :, sgl - 1::2 * sgl], a2, ALU.mult),
                                    r=["Bf", "TA"], w=["tr2"])
                            S.op("dve", lambda: V.tensor_tensor(Rv, Rv, tr1[:, :, :, 0:m], ALU.add), r=["tr1", "Bf"], w=["Bf"])
                            S.op("dve", lambda: V.tensor_tensor(Rv, Rv, tr2[:, :, :, 0:m], ALU.add), r=["tr2", "Bf"], w=["Bf"])
                        S.op("dve", lambda: V.tensor_tensor(t1[:], C.TA1[:, 6], C.Xc[:], ALU.mult), r=["Xc", "TA"], w=["t1"])
                        for ri in range(2):
                            S.op("dve", lambda ri=ri: V.tensor_tensor(t2[:, ri, :], C.TA2[:, 6, ri, :], C.Xc[:, 1 - ri, :], ALU.mult),
                                 r=["Xc", "TA"], w=["t2"])
                        S.op("dve", lambda: V.tensor_tensor(C.Xc[:], t1[:], t2[:], ALU.add), r=["t1", "t2"], w=["Xc"])
                        S.op("dve", lambda: V.tensor_tensor(C.Xc[:], C.Xc[:], Bf[:, :, :, TBS - 1], ALU.add), r=["Bf", "Xc"], w=["Xc"])
                        continue
                    for t in range(TBS):
                        prev = C.Xc[:] if t == 0 else Bf[:, :, :, t - 1]
                        cur = Bf[:, :, :, t]
                        S.op("dve", lambda prev=prev: V.tensor_tensor(t1[:], C.A1[:], prev, ALU.mult),
                             r=["Bf", "Xc", "A1"], w=["t1"])
                        for ri in range(2):
                            pv = C.Xc[:, 1 - ri, :] if t == 0 else Bf[:, 1 - ri, :, t - 1]
                            S.op("dve", lambda pv=pv, ri=ri: V.tensor_tensor(t2[:, ri, :], C.A2[:, ri, :], pv, ALU.mult),
                                 r=["Bf", "Xc", "A2"], w=["t2"])
                        S.op("dve", lambda cur=cur: V.tensor_tensor(cur, cur, t1[:], ALU.add), r=["t1", "Bf"], w=["Bf"])
                        S.op("dve", lambda cur=cur: V.tensor_tensor(cur, cur, t2[:], ALU.add), r=["t2", "Bf"], w=["Bf"])
                    S.op("dve", lambda: V.tensor_copy(C.Xc[:], Bf[:, :, :, TBS - 1]), r=["Bf"], w=["Xc"])
                    if state_only:
                        continue
                    S.op("act", lambda: nc.scalar.copy(Xb[:], Bf[:]), r=["Bf"], w=["Xb"])
                    for half in range(2):
                        pi = 4 + half
                        fns = []
                        for cc in range(8):
                            ch = half * 8 + cc
                            osl = slice(cc * TBS, (cc + 1) * TBS)
                            fns.append(lambda ch=ch, osl=osl, pi=pi: nc.tensor.matmul(
                                C.ps[pi][:, osl], C.dg[:, ch, :], C.h[:, ch, tsl], start=(ch % 8 == 0), stop=False))
                            for j in range(4):
                                pair = ch * 4 + j
                                for ri in range(2):
                                    fns.append(lambda ch=ch, osl=osl, pi=pi, j=j, ri=ri, pair=pair: nc.tensor.matmul(
                                        C.ps[pi][32 * j:32 * j + 32, osl], C.cz[:, ch, ri, 32 * j:32 * j + 32],
                                        Xb[:, ri, pair, :], start=False, stop=(ri == 1 and j == 3),
                                        tile_position=(0, 32 * j), skip_group_check=True))
                        S.group("pe", fns, r=["dg", "h", "cz", "Xb"], w=[("ps", pi)])
                        yp = C.ps[pi]
                        S.op("act", lambda yp=yp: nc.scalar.copy(ysb[:], yp[:]), r=[("ps", pi)], w=["ysb"])
                        S.op("dve", lambda: V.tensor_tensor(y2[:], ysb[:], ysb[:], ALU.mult), r=["ysb"], w=["y2"])
                        S.op("dve", lambda: V.tensor_scalar(y2[:], y2[:], 0.044715, 1.0, ALU.mult, ALU.add), r=["y2"], w=["y2"])
                        S.op("dve", lambda: V.tensor_tensor(y2[:], y2[:], ysb[:], ALU.mult), r=["y2", "ysb"], w=["y2"])
                        S.op("act", lambda: nc.scalar.activation(y3[:], y2[:], AF.Sigmoid, scale=1.5957691216057308),
                             r=["y2"], w=["y3"])
                        S.op("dve", lambda half=half: V.tensor_tensor(
                            C.h[:, half * 8:(half + 1) * 8, tsl], ysb[:].rearrange("p (a t) -> p a t", t=TBS),
                            y3[:].rearrange("p (a t) -> p a t", t=TBS), ALU.mult),
                            r=["ysb", "y3", "h"], w=["h"])
                S.dma("sp", xend, C.Xc[:], r=["Xc"], w=["xend_d"])
                S.barrier()
            st01.close()

            if not state_only:
                with ExitStack() as st2:
                    E2 = st2.enter_context
                    C.stg = [E2(nc.sbuf_tensor(f"stg{i}", [128, 2048], F32)) for i in range(3)]
                    C.stg_i = 0
                    C.wbuf = [E2(nc.sbuf_tensor(f"wb{i}", [128, NCH, 128], BF16)) for i in range(4)]
                    C.wb_i = 0
                    C.ps_i = 0
                    C.lin_ps = [0, 1, 2, 3]
                    C.out_ps = [4, 5]
                    wo = [E2(nc.sbuf_tensor(f"wo{i}", [128, 2048], BF16)) for i in range(4)]
                    hid = [E2(nc.sbuf_tensor(f"hid{i}", [128, T], BF16)) for i in range(4)]
                    sg = [E2(nc.sbuf_tensor(f"sg{i}", [128, 512], F32)) for i in range(2)]
                    val = [E2(nc.sbuf_tensor(f"val{i}", [128, 512], F32)) for i in range(2)]
                    V = nc.vector
                    st_ = {}

                    def glu_consume(oc, tb, pst, pk):
                        sl = slice(tb * 512, (tb + 1) * 512)
                        if oc % 2 == 0:
                            i = tb
                            S.op("act", lambda: nc.scalar.copy(val[i][:], pst[:]), r=[pk], w=[("val", i)])
                        else:
                            i = tb
                            c = oc // 2
                            S.op("act", lambda: nc.scalar.activation(sg[i][:], pst[:], AF.Sigmoid), r=[pk], w=[("sg", i)])
                            S.op("dve", lambda: V.tensor_tensor(sg[i][:], sg[i][:], val[i][:], ALU.mult),
                                 r=[("sg", i), ("val", i)], w=[("sg", i)])
                            S.op("dve", lambda: V.tensor_tensor(C.x[:, c, sl], C.x[:, c, sl], sg[i][:], ALU.add),
                                 r=[("sg", i), "x"], w=["x"])

                    for c in range(NCH):
                        for which in range(2):
                            linear_fm(S, nc, C, w_glu, 1, which * D + c * 128, C.h, "h",
                                      lambda oc, tb, pst, pk, which=which, c=c: glu_consume(2 * c + which, tb, pst, pk))
                    rmsnorm(S, nc, C, C.gn[:, 1, :])
                    ffn(S, nc, C, w_in, w_out, wo, hid, sg)
                    for c in range(NCH):
                        S.dma("sp", x1T[c * 128:(c + 1) * 128, :], C.x[:, c, :], r=["x"], w=["x1_d"])
                    rmsnorm(S, nc, C, C.gn[:, 2, :])
                    kvb = [E2(nc.sbuf_tensor(f"kvb{i}", [128, 512], BF16)) for i in range(2)]

                    def kv_consume(oc, tb, pst, pk):
                        sl = slice(tb * 512, (tb + 1) * 512)
                        i = (oc * 2 + tb) % 2
                        S.op("act", lambda: nc.scalar.copy(kvb[i][:], pst[:]), r=[pk], w=[("kvb", i)])
                        dst = kT if oc < 4 else vT
                        o4 = oc % 4
                        S.dma("sp", dst[o4 * 128:(o4 + 1) * 128, sl], kvb[i][:], r=[("kvb", i)], w=["kv_d"])
                    linear_fm(S, nc, C, w_kv, 8, 0, C.h, "h", kv_consume)
                    S.barrier()
        except _Stop:
            pass
        S.finish()
    return nc


def _key_sets():
    ks = []
    for kb in range(-1, 8):
        k0 = 128 * kb
        qs, qe = max(0, k0), min(T, k0 + 256)
        ks.append((k0 + 2048, 1, 128, qs, 1, qe - qs, qs - k0, 1 if kb < 0 else 0, 0))
    for r in range(4):
        for kc in (-128, 0, 128):
            qs, qe = max(0, kc), min(256, kc + 256)
            ks.append((r + 4 * kc + 2048, 4, 128, r + 4 * qs, 4, qe - qs, qs - kc, 1 if kc < 0 else 0, 1))
    for r in range(16):
        ks.append((r, 16, 128, r, 16, 64, 128, 2, 2))
        ks.append((r + 2048, 16, 64, r, 16, 64, 0, 0, 2))
    return ks


def build_l1():
    nc = bass.Bass("TRN2", target_bir_lowering=False)
    xT = _dram_in(nc, "xT", [D, T])
    kTa = _dram_in(nc, "kTa", [512, 3072], BF16)
    Va = _dram_in(nc, "Va", [3072, 512], BF16)
    gains = _dram_in(nc, "gains", [128, 4, NCH])
    bandm = _dram_in(nc, "bandm", [128, 256], BF16)
    kbias = _dram_in(nc, "kbias", [128, 4])
    w_q = _dram_in(nc, "w_q", [D, 3 * D])
    w_o = _dram_in(nc, "w_o", [D, D])
    w_in = _dram_in(nc, "w_in", [D, 2 * FF])
    w_out = _dram_in(nc, "w_out", [FF, D])
    outT = _dram_out(nc, "outT", [D, T])
    KS = _key_sets()
    with ExitStack() as st:
        E = st.enter_context
        S = Sched(nc, st)
        C = Ctx()
        V = nc.vector
        C.x = E(nc.sbuf_tensor("x", [128, NCH, T], F32))
        C.h = E(nc.sbuf_tensor("h", [128, NCH, T], BF16))
        C.ones = E(nc.sbuf_tensor("ones", [128, 128], BF16))
        C.gn = E(nc.sbuf_tensor("gn", [128, 4, NCH], F32))
        C.epsc = E(nc.sbuf_tensor("epsc", [128, 1], F32))
        C.rstd = E(nc.sbuf_tensor("rstd", [128, 512], F32))
        C.sq = [E(nc.sbuf_tensor(f"sq{i}", [128, 512], BF16)) for i in range(2)]
        C.ps = [E(nc.psum_tensor(f"ps{i}", [128, 512], F32)) for i in range(8)]
        S.op("dve", lambda: V.memset(C.ones[:], 1.0), w=["ones"])
        S.op("dve", lambda: V.memset(C.epsc[:], EPS), w=["epsc"])
        S.dma("sp", C.gn[:], gains, w=["gains"])
        for c in range(NCH):
            S.dma("sp", C.x[:, c, :], xT[c * 128:(c + 1) * 128, :], w=["x"])
        rmsnorm(S, nc, C, C.gn[:, 0, :])
        with ExitStack() as sa:
            EA = sa.enter_context
            attn = EA(nc.sbuf_tensor("attn", [128, NCH, T], BF16))
            with ExitStack() as sb:
                EB = sb.enter_context
                C.stg = [EB(nc.sbuf_tensor(f"stg{i}", [128, 2048], F32)) for i in range(3)]
                C.stg_i = 0
                C.wbuf = [EB(nc.sbuf_tensor(f"wb{i}", [128, NCH, 128], BF16)) for i in range(3)]
                C.wb_i = 0
                C.ps_i = 0
                C.lin_ps = [6, 7]
                mask = EB(nc.sbuf_tensor("bandS", [128, 256], BF16))
                kb_ = EB(nc.sbuf_tensor("kbS", [128, 4], F32))
                kts = EB(nc.sbuf_tensor("kts", [128, 3072], BF16))
                Vt = EB(nc.sbuf_tensor("Vt", [128, len(KS), 128], BF16))
                qg = [EB(nc.sbuf_tensor(f"qg{i}", [128, T], BF16)) for i in range(3)]
                pe_ = [EB(nc.sbuf_tensor(f"pe{i}", [128, 256], BF16)) for i in range(4)]
                pT = [EB(nc.sbuf_tensor(f"pT{i}", [128, 256], BF16)) for i in range(4)]
                rec = EB(nc.sbuf_tensor("rec", [128, 512], F32))
                S.dma("sp", mask[:], bandm, w=["mask"])
                S.dma("sp", kb_[:], kbias, w=["kb"])
                it = 0
                for hh in range(16):
                    kvh = hh // 4
                    if hh % 4 == 0:
                        S.dma("sp", kts[:], kTa[kvh * 128:(kvh + 1) * 128, :], w=["kts"])
                        for i, (k0, kst, m, q0, qst, n, off, bc, g) in enumerate(KS):
                            S.dma("sp", Vt[0:m, i, :], Va[k0:k0 + kst * (m - 1) + 1:kst, kvh * 128:(kvh + 1) * 128],
                                  w=["Vt"])
                    for g in range(3):
                        def q_consume(oc, tb, pst, pk, g=g):
                            sl = slice(tb * 512, (tb + 1) * 512)
                            S.op("act", lambda: nc.scalar.activation(qg[g][:, sl], pst[:], AF.Copy, scale=HD_SCALE),
                                 r=[pk], w=[("qg", g)])
                        linear_fm(S, nc, C, w_q, 1, g * D + hh * 128, C.h, "h", q_consume)
                    first = {0: True, 1: True}
                    LOOK = 1
                    nks = len(KS)

                    def stage_a(i):
                        (k0, kst, m, q0, qst, n, off, bc, g) = KS[i]
                        slot = (hh * nks + i) % 2
                        sps = C.ps[4 + slot]
                        S.group("pe", [lambda: nc.tensor.matmul(
                            sps[0:m, 0:n], kts[:, k0:k0 + kst * (m - 1) + 1:kst],
                            qg[g][:, q0:q0 + qst * (n - 1) + 1:qst], start=True, stop=True)],
                            r=["kts", ("qg", g)], w=[("s", slot)])
                        S.op("act", lambda: nc.scalar.activation(pe_[slot][0:m, 0:n], sps[0:m, 0:n], AF.Exp,
                                                                 bias=kb_[0:m, bc:bc + 1], scale=1.0),
                             r=[("s", slot), "kb"], w=[("pe", slot)])
                        S.op("dve", lambda: V.tensor_tensor(pT[slot][0:m, 0:n], pe_[slot][0:m, 0:n], mask[0:m, off:off + n], ALU.mult),
                             r=[("pe", slot), "mask"], w=[("pT", slot)])

                    def stage_b(i):
                        (k0, kst, m, q0, qst, n, off, bc, g) = KS[i]
                        slot = (hh * nks + i) % 2
                        segs = []
                        isplit = min(n, max(0, -(-(512 - q0) // qst)))
                        if isplit > 0:
                            segs.append((0, isplit, 0))
                        if isplit < n:
                            segs.append((isplit, n, 1))
                        fns = []
                        wk = []
                        for (i0, i1, bank) in segs:
                            c0 = q0 + qst * i0 - 512 * bank
                            cs = slice(c0, c0 + qst * (i1 - i0 - 1) + 1, qst)
                            st_ = first[bank]
                            first[bank] = False
                            fns.append(lambda cs=cs, bank=bank, i0=i0, i1=i1, st_=st_: nc.tensor.matmul(
                                C.ps[bank][:, cs], Vt[0:m, i, :], pT[slot][0:m, i0:i1], start=st_, stop=False,
                                skip_group_check=True))
                            fns.append(lambda cs=cs, bank=bank, i0=i0, i1=i1, st_=st_: nc.tensor.matmul(
                                C.ps[2 + bank][:, cs], C.ones[0:m, :], pT[slot][0:m, i0:i1], start=st_, stop=False,
                                skip_group_check=True))
                            wk += [("ps", bank), ("ps", 2 + bank)]
                        S.group("pe", fns, r=[("pT", slot), "Vt", "ones"], w=wk)

                    for step in range(nks + LOOK):
                        if step < nks:
                            stage_a(step)
                        if step - LOOK >= 0:
                            stage_b(step - LOOK)
                    for bank in range(2):
                        sl = slice(bank * 512, (bank + 1) * 512)
                        S.op("dve", lambda: V.reciprocal(rec[:], C.ps[2 + bank][:]), r=[("ps", 2 + bank)], w=["rec"])
                        S.op("dve", lambda: V.tensor_tensor(attn[:, hh, sl], C.ps[bank][:], rec[:], ALU.mult),
                             r=[("ps", bank), "rec"], w=["attn"])
                S.barrier()
            with ExitStack() as sc:
                EC = sc.enter_context
                C.stg = [EC(nc.sbuf_tensor(f"stgc{i}", [128, 2048], F32)) for i in range(3)]
                C.stg_i = 0
                C.wbuf = [EC(nc.sbuf_tensor(f"wbc{i}", [128, NCH, 128], BF16)) for i in range(4)]
                C.wb_i = 0
                C.lin_ps = [0, 1, 2, 3]

                def o_consume(oc, tb, pst, pk):
                    sl = slice(tb * 512, (tb + 1) * 512)
                    S.op("dve", lambda: V.tensor_tensor(C.x[:, oc, sl], C.x[:, oc, sl], pst[:], ALU.add),
                         r=[pk, "x"], w=["x"])
                linear_fm(S, nc, C, w_o, NCH, 0, attn, "attn", o_consume)
                S.barrier()
        with ExitStack() as sd:
            ED = sd.enter_context
            C.stg = [ED(nc.sbuf_tensor(f"stgd{i}", [128, 2048], F32)) for i in range(3)]
            C.stg_i = 0
            C.wbuf = [ED(nc.sbuf_tensor(f"wbd{i}", [128, NCH, 128], BF16)) for i in range(4)]
            C.wb_i = 0
            C.lin_ps = [0, 1, 2, 3]
            C.out_ps = [4, 5]
            wo = [ED(nc.sbuf_tensor(f"wo{i}", [128, 2048], BF16)) for i in range(4)]
            hid = [ED(nc.sbuf_tensor(f"hid{i}", [128, T], BF16)) for i in range(4)]
            sg = [ED(nc.sbuf_tensor(f"sg{i}", [128, 512], F32)) for i in range(2)]
            rmsnorm(S, nc, C, C.gn[:, 1, :])
            ffn(S, nc, C, w_in, w_out, wo, hid, sg)
            rmsnorm(S, nc, C, C.gn[:, 2, :], hkey="x", dst=C.x)
            for c in range(NCH):
                S.dma("sp", outT[c * 128:(c + 1) * 128, :], C.x[:, c, :], r=["x"], w=["out_d"])
            S.barrier()
        S.finish()
    return nc


HD_SCALE = 128 ** -0.5

_cache = {}


def _prog(name, fn):
    if name not in _cache:
        _cache[name] = fn()
    return _cache[name]


def _fm(v):
    return np.ascontiguousarray(np.asarray(v, np.float32).reshape(NCH, 128).T)


def layer0(inputs):
    x = np.asarray(inputs["x"], np.float32)
    n = 8
    gains = np.ascontiguousarray(np.stack([_fm(inputs["a_norm_mix"][0]), _fm(inputs["ffn_norm"][0]),
                                           _fm(inputs["kv_norm"]), _fm(inputs["s5_d"][0])], axis=1))
    ident = np.eye(128, dtype=np.float32)
    hp = np.arange(128) // 64
    g8 = (np.arange(128) // 16) % 2
    mask_sm = (hp[:, None] == g8[None, :]).astype(np.float32)
    base = {
        "lam_re": np.ascontiguousarray(inputs["s5_lam_re"][0]), "lam_im": np.ascontiguousarray(inputs["s5_lam_im"][0]),
        "log_dt": np.ascontiguousarray(inputs["s5_log_dt"][0].reshape(128, 1)),
        "b_re": np.ascontiguousarray(inputs["s5_b_re"][0]), "b_im": np.ascontiguousarray(inputs["s5_b_im"][0]),
        "c_re": np.ascontiguousarray(inputs["s5_c_re"][0].reshape(2048, 64)),
        "c_im": np.ascontiguousarray(inputs["s5_c_im"][0].reshape(2048, 64)),
        "gains": gains, "ident": ident, "mask_sm": mask_sm,
    }
    xTs = []
    for c in range(n):
        b, j = c // 4, c % 4
        xTs.append(np.ascontiguousarray(x[b, j * T:(j + 1) * T, :].T))
    zero_e = np.zeros((128, 3, 2, NPAIR), np.float32)
    nc_s = _prog("l0s", lambda: build_l0(True))
    maps = [dict(base, xT=xTs[c], eprev=zero_e) for c in range(n)]
    res = run_bass_kernel_spmd(nc_s, maps, core_ids=list(range(n)))
    eprev = []
    for c in range(n):
        b, j = c // 4, c % 4
        e = np.zeros((128, 3, 2, NPAIR), np.float32)
        for k in range(3):
            if j - 1 - k >= 0:
                e[:, k] = np.asarray(res.results[b * 4 + j - 1 - k]["xend"])
        eprev.append(e)
    nc_f = _prog("l0f", lambda: build_l0(False))
    full = dict(base, w_glu=np.ascontiguousarray(inputs["s5_w_glu"][0]),
                w_in=np.ascontiguousarray(inputs["ffn_w_in"][0]), w_out=np.ascontiguousarray(inputs["ffn_w_out"][0]),
                w_kv=np.ascontiguousarray(inputs["w_kv"]))
    maps = [dict(full, xT=xTs[c], eprev=eprev[c]) for c in range(n)]
    res = run_bass_kernel_spmd(nc_f, maps, core_ids=list(range(n)))
    return res.results


def layer1(inputs, r0):
    n = 8
    gains = np.ascontiguousarray(np.stack([_fm(inputs["b_norm_mix"][0]), _fm(inputs["ffn_norm"][1]),
                                           _fm(inputs["final_norm"]), _fm(inputs["final_norm"])], axis=1))
    bdt = np.asarray(r0[0]["kT"]).dtype
    sidx = np.arange(128)[:, None]
    qidx = np.arange(256)[None, :]
    bandm = ((sidx <= qidx) & (qidx <= sidx + 128)).astype(np.float32).astype(bdt)
    base = {"gains": gains, "bandm": bandm,
            "w_q": np.ascontiguousarray(inputs["attn_w_q"][0]), "w_o": np.ascontiguousarray(inputs["attn_w_o"][0]),
            "w_in": np.ascontiguousarray(inputs["ffn_w_in"][1]), "w_out": np.ascontiguousarray(inputs["ffn_w_out"][1])}
    maps = []
    for c in range(n):
        b, j = c // 4, c % 4
        kparts, vparts = [], []
        for jj in (j - 2, j - 1, j):
            if jj < 0:
                kparts.append(np.zeros((512, T), bdt))
                vparts.append(np.zeros((512, T), bdt))
            else:
                kparts.append(np.asarray(r0[b * 4 + jj]["kT"]))
                vparts.append(np.asarray(r0[b * 4 + jj]["vT"]))
        kTa = np.ascontiguousarray(np.concatenate(kparts, axis=1))
        Va = np.ascontiguousarray(np.concatenate(vparts, axis=1).T)
        kbias = np.zeros((128, 4), np.float32)
        if j == 0:
            kbias[:, 1] = -30000.0
            kbias[:, 2] = -30000.0
        elif j == 1:
            kbias[:64, 2] = -30000.0
        maps.append(dict(base, xT=np.ascontiguousarray(r0[c]["x1T"]), kTa=kTa, Va=Va, kbias=kbias))
    nc1 = _prog("l1", build_l1)
    res = run_bass_kernel_spmd(nc1, maps, core_ids=list(range(n)))
    return res.results


def kernel(**inputs):
    inputs = {k: np.asarray(v) for k, v in inputs.items()}
    r0 = layer0(inputs)
    r1 = layer1(inputs, r0)
    out = np.empty((2, 4096, D), np.float32)
    for c in range(8):
        b, j = c // 4, c % 4
        out[b, j * T:(j + 1) * T, :] = np.asarray(r1[c]["outT"]).T
    return out
```

```python
import math
from contextlib import ExitStack
import numpy as np
import concourse.bass as bass
import concourse.mybir as mybir
from concourse.bass_utils import run_bass_kernel_spmd

F32 = mybir.dt.float32
BF16 = mybir.dt.bfloat16
AF = mybir.ActivationFunctionType
ALU = mybir.AluOpType

D = 2048
NCH = 16
T = 1024
NTB = 2
FF = 5632
NF = 44
EPS = 1e-6
TBS = 64
NPAIR = 64


class Sched:
    def __init__(self, nc, st):
        self.nc = nc
        self.eng = {"pe": nc.tensor, "act": nc.scalar, "dve": nc.vector, "pool": nc.gpsimd, "sp": nc.sync}
        self.sem = {}
        self.cnt = {}
        for e in ["pe", "act", "dve", "pool"]:
            self.sem["c_" + e] = st.enter_context(nc.semaphore("c_" + e))
            self.cnt["c_" + e] = 0
        for q in ["sp", "act", "pool"]:
            self.sem["d_" + q] = st.enter_context(nc.semaphore("d_" + q))
            self.cnt["d_" + q] = 0
        self.seen = {e: {} for e in self.eng}
        self.last_w = {}
        self.readers = {}
        self.stopped = False
        _CUR[0] = self

    def _wait(self, e, deps):
        need = {}
        for (c, v) in deps:
            if c == "c_" + e and e == "pe":
                continue
            if v > need.get(c, 0):
                need[c] = v
        for c, v in need.items():
            if self.seen[e].get(c, 0) >= v:
                continue
            self.eng[e].wait_ge(self.sem[c], v)
            self.seen[e][c] = v

    def _deps(self, r, w):
        deps = []
        for k in r:
            if k in self.last_w:
                deps.append(self.last_w[k])
        for k in w:
            if k in self.last_w:
                deps.append(self.last_w[k])
            deps.extend(self.readers.get(k, []))
        return deps

    def _record(self, tok, r, w):
        for k in w:
            self.last_w[k] = tok
            self.readers[k] = []
        for k in r:
            self.readers.setdefault(k, []).append(tok)

    def op(self, e, fn, r=(), w=()):
        if self.stopped:
            return
        self._wait(e, self._deps(r, w))
        ins = fn()
        c = "c_" + e
        self.cnt[c] += 1
        ins.then_inc(self.sem[c], 1)
        self._record((c, self.cnt[c]), r, w)

    def group(self, e, fns, r=(), w=()):
        if self.stopped:
            return
        self._wait(e, self._deps(r, w))
        ins = None
        for f in fns:
            ins = f()
        c = "c_" + e
        self.cnt[c] += 1
        ins.then_inc(self.sem[c], 1)
        self._record((c, self.cnt[c]), r, w)

    def dma(self, q, out, in_, r=(), w=(), **kw):
        if self.stopped:
            return
        self._wait(q, self._deps(r, w))
        ins = self.eng[q].dma_start(out=out, in_=in_, **kw)
        c = "d_" + q
        self.cnt[c] += 16
        ins.then_inc(self.sem[c], 16)
        self._record((c, self.cnt[c]), r, w)

    def barrier(self):
        if self.stopped:
            return
        for e in self.eng:
            deps = [(c, v) for c, v in self.cnt.items() if v > 0]
            self._wait(e, deps)
        self.last_w = {}
        self.readers = {}

    def finish(self):
        deps = [(c, v) for c, v in self.cnt.items() if v > 0]
        self._wait("sp", deps)


def _dram_in(nc, name, shape, dt=F32):
    return nc.dram_tensor(name, list(shape), dt, kind="ExternalInput").ap()


def _dram_out(nc, name, shape, dt=F32):
    return nc.dram_tensor(name, list(shape), dt, kind="ExternalOutput").ap()


class Ctx:
    pass


class _Stop(Exception):
    pass


import os
STOP = int(os.environ.get("L0_STOP", "0"))


_CUR = [None]


def ck(k):
    if STOP == k:
        _CUR[0].stopped = True


def rmsnorm(S, nc, C, gain, hkey="h", dst=None):
    for tb in range(NTB):
        sl = slice(tb * 512, (tb + 1) * 512)
        ss = C.ps[6 + tb]
        fns = []
        for c in range(NCH):
            sq = C.sq[c % 2]
            S.op("act", lambda c=c, sq=sq: nc.scalar.activation(sq[:], C.x[:, c, sl], AF.Square),
                 r=["x"], w=[("sq", c % 2)])
            S.op("pe", lambda c=c, sq=sq: nc.tensor.matmul(ss[:], C.ones[:], sq[:], start=(c == 0), stop=(c == NCH - 1)),
                 r=[("sq", c % 2), "ones"], w=[("ps", 6 + tb)])
        S.op("act", lambda: nc.scalar.activation(C.rstd[:], ss[:], AF.Sqrt, bias=C.epsc[:, 0:1], scale=1.0 / D),
             r=[("ps", 6 + tb)], w=["rstd"])
        S.op("dve", lambda: nc.vector.reciprocal(C.rstd[:], C.rstd[:]), r=["rstd"], w=["rstd"])
        for c in range(NCH):
            S.op("dve", lambda c=c: nc.vector.scalar_tensor_tensor(
                (C.h if dst is None else dst)[:, c, sl], C.x[:, c, sl], gain[:, c:c + 1], C.rstd[:], ALU.mult, ALU.mult),
                r=["x", "rstd", "gains"], w=[hkey])


def load_cast(S, nc, C, src_ap, dst, dkey, shape3=None):
    i = C.stg_i
    C.stg_i += 1
    s = i % len(C.stg)
    stg = C.stg[s]
    sv = stg[:] if shape3 is None else stg[:].rearrange("p (k n) -> p k n", k=shape3[0])
    S.dma("sp", sv, src_ap, w=[("stg", s)])
    dv = dst
    S.op("act", lambda: nc.scalar.copy(dv, sv), r=[("stg", s)], w=[dkey])


def linear_fm(S, nc, C, w_ap, n_out_chunks, col0, src, skey, consume):
    for oc in range(n_out_chunks):
        wb = C.wbuf[C.wb_i % len(C.wbuf)]
        wk = ("wb", C.wb_i % len(C.wbuf))
        C.wb_i += 1
        c0 = col0 + oc * 128
        load_cast(S, nc, C, w_ap[:, c0:c0 + 128].rearrange("(k p) n -> p k n", p=128),
                  wb[:], wk, shape3=(NCH, 128))
        for tb in range(NTB):
            pi = C.lin_ps[C.ps_i % len(C.lin_ps)]
            C.ps_i += 1
            pst = C.ps[pi]
            sl = slice(tb * 512, (tb + 1) * 512)
            S.group("pe", [lambda k=k: nc.tensor.matmul(pst[:], wb[:, k, :], src[:, k, sl],
                                                        start=(k == 0), stop=(k == NCH - 1))
                           for k in range(NCH)],
                    r=[wk, skey], w=[("ps", pi)])
            consume(oc, tb, pst, ("ps", pi))


def ffn(S, nc, C, w_in, w_out, wo, hid, sg):
    V = nc.vector
    GS = 2
    for g in range(NF // GS):
        for fi in range(GS):
            f = g * GS + fi
            hb = hid[f % 4]
            hk = ("hid", f % 4)
            wob = wo[f % 4]
            load_cast(S, nc, C, w_out[f * 128:(f + 1) * 128, :], wob[:], ("wo", f % 4))

            def ffn_consume(oc, tb, pst, pk, hb=hb, hk=hk):
                sl = slice(tb * 512, (tb + 1) * 512)
                if oc == 0:
                    S.op("act", lambda: nc.scalar.activation(sg[tb][:], pst[:], AF.Silu), r=[pk], w=[("sg", tb)])
                else:
                    S.op("dve", lambda: V.tensor_tensor(hb[:, sl], sg[tb][:], pst[:], ALU.mult),
                         r=[pk, ("sg", tb)], w=[hk])
            linear_fm(S, nc, C, w_in, 1, f * 128, C.h, "h",
                      lambda oc, tb, pst, pk: ffn_consume(0, tb, pst, pk))
            linear_fm(S, nc, C, w_in, 1, FF + f * 128, C.h, "h",
                      lambda oc, tb, pst, pk: ffn_consume(1, tb, pst, pk))
        for c in range(NCH):
            for tb in range(NTB):
                sl = slice(tb * 512, (tb + 1) * 512)
                pi = C.out_ps[C.ps_i % 2]
                C.ps_i += 1
                fns = []
                rk = []
                for fi in range(GS):
                    f = g * GS + fi
                    rk += [("wo", f % 4), ("hid", f % 4)]
                    fns.append(lambda f=f, fi=fi, pi=pi, c=c, sl=sl: nc.tensor.matmul(
                        C.ps[pi][:], wo[f % 4][:, c * 128:(c + 1) * 128], hid[f % 4][:, sl],
                        start=(fi == 0), stop=(fi == GS - 1)))
                S.group("pe", fns, r=rk, w=[("ps", pi)])
                S.op("dve", lambda pi=pi, c=c, sl=sl: V.tensor_tensor(C.x[:, c, sl], C.x[:, c, sl], C.ps[pi][:], ALU.add),
                     r=[("ps", pi), "x"], w=["x"])


def build_l0(state_only):
    nc = bass.Bass("TRN2", target_bir_lowering=False)
    xT = _dram_in(nc, "xT", [D, T])
    lam_re = _dram_in(nc, "lam_re", [128, 64])
    lam_im = _dram_in(nc, "lam_im", [128, 64])
    log_dt = _dram_in(nc, "log_dt", [128, 1])
    b_re = _dram_in(nc, "b_re", [128, 64, 16])
    b_im = _dram_in(nc, "b_im", [128, 64, 16])
    c_re = _dram_in(nc, "c_re", [2048, 64])
    c_im = _dram_in(nc, "c_im", [2048, 64])
    gains = _dram_in(nc, "gains", [128, 4, NCH])
    eprev = _dram_in(nc, "eprev", [128, 3, 2, NPAIR])
    ident_d = _dram_in(nc, "ident", [128, 128])
    mask_d = _dram_in(nc, "mask_sm", [128, 128])
    xend = _dram_out(nc, "xend", [128, 2, NPAIR])
    if not state_only:
        w_glu = _dram_in(nc, "w_glu", [D, 2 * D])
        w_in = _dram_in(nc, "w_in", [D, 2 * FF])
        w_out = _dram_in(nc, "w_out", [FF, D])
        w_kv = _dram_in(nc, "w_kv", [D, 1024])
        x1T = _dram_out(nc, "x1T", [D, T])
        kT = _dram_out(nc, "kT", [512, T], BF16)
        vT = _dram_out(nc, "vT", [512, T], BF16)

    with ExitStack() as st:
        E = st.enter_context
        S = Sched(nc, st)
        C = Ctx()
        C.x = E(nc.sbuf_tensor("x", [128, NCH, T], F32))
        C.h = E(nc.sbuf_tensor("h", [128, NCH, T], BF16))
        C.ones = E(nc.sbuf_tensor("ones", [128, 128], BF16))
        C.ident = E(nc.sbuf_tensor("identS", [128, 128], F32))
        C.mask = E(nc.sbuf_tensor("maskS", [128, 128], F32))
        C.gn = E(nc.sbuf_tensor("gn", [128, 4, NCH], F32))
        C.epsc = E(nc.sbuf_tensor("epsc", [128, 1], F32))
        C.rstd = E(nc.sbuf_tensor("rstd", [128, 512], F32))
        C.sq = [E(nc.sbuf_tensor(f"sq{i}", [128, 512], BF16)) for i in range(2)]
        C.ps = [E(nc.psum_tensor(f"ps{i}", [128, 512], F32)) for i in range(8)]
        C.Xc = E(nc.sbuf_tensor("Xc", [128, 2, NPAIR], F32))
        st01 = ExitStack()
        E01 = st01.enter_context
        C.A1 = E01(nc.sbuf_tensor("A1", [128, 2, NPAIR], F32))
        C.A2 = E01(nc.sbuf_tensor("A2", [128, 2, NPAIR], F32))
        C.lbu = E01(nc.sbuf_tensor("lbu", [128, NCH, 2, 128], BF16))
        C.cz = E01(nc.sbuf_tensor("cz", [128, NCH, 2, 128], BF16))
        C.dg = E01(nc.sbuf_tensor("dg", [128, NCH, 128], BF16))
        C.TA1 = E01(nc.sbuf_tensor("TA1", [128, 7, 2, NPAIR], F32))
        C.TA2 = E01(nc.sbuf_tensor("TA2", [128, 7, 2, NPAIR], F32))

        try:
            S.op("dve", lambda: nc.vector.memset(C.ones[:], 1.0), w=["ones"])
            S.op("dve", lambda: nc.vector.memset(C.epsc[:], EPS), w=["epsc"])
            S.dma("sp", C.ident[:], ident_d, w=["ident"])
            S.dma("sp", C.mask[:], mask_d, w=["mask"])
            S.dma("sp", C.gn[:], gains, w=["gains"])
            for c in range(NCH):
                S.dma("sp", C.x[:, c, :], xT[c * 128:(c + 1) * 128, :], w=["x"])

            ck(1)
            with ExitStack() as st0:
                E0 = st0.enter_context
                lr = E0(nc.sbuf_tensor("lr", [128, 64], F32))
                li = E0(nc.sbuf_tensor("li", [128, 64], F32))
                ldt = E0(nc.sbuf_tensor("ldt", [128, 1], F32))
                dt = E0(nc.sbuf_tensor("dt", [128, 1], F32))
                mag = E0(nc.sbuf_tensor("mag", [128, 64], F32))
                ang = E0(nc.sbuf_tensor("ang", [128, 64], F32))
                angc = E0(nc.sbuf_tensor("angc", [128, 64], F32))
                tm = E0(nc.sbuf_tensor("tm", [128, 64], F32))
                tm2 = E0(nc.sbuf_tensor("tm2", [128, 64], F32))
                nr = E0(nc.sbuf_tensor("nr", [128, 64], F32))
                rden = E0(nc.sbuf_tensor("rden", [128, 64], F32))
                T4 = E0(nc.sbuf_tensor("T4", [128, 4, 2, 64], F32))
                S4 = E0(nc.sbuf_tensor("S4", [128, 4, 128], F32))
                Bz = E0(nc.sbuf_tensor("Bz", [128, 2, 128, 16], F32))
                Bb = E0(nc.sbuf_tensor("Bb", [128, 2, 128, 16], F32))
                tb1 = E0(nc.sbuf_tensor("tb1", [128, 128, 16], F32))
                Cin = E0(nc.sbuf_tensor("Cin", [128, 2, NCH, 128], F32))
                dgf = E0(nc.sbuf_tensor("dgf", [128, 128], F32))

                S.dma("sp", lr[:], lam_re, w=["lr"])
                S.dma("sp", li[:], lam_im, w=["li"])
                S.dma("sp", ldt[:], log_dt, w=["ldt"])
                for ri, src in enumerate([b_re, b_im]):
                    for hh in range(2):
                        for gh in range(2):
                            S.dma("sp", Bz[hh * 64:(hh + 1) * 64, ri, gh * 64:(gh + 1) * 64, :],
                                  src[gh * 64:(gh + 1) * 64].rearrange("g p c -> p g c"), w=["Bz"])
                for ri, src in enumerate([c_re, c_im]):
                    for dup in range(2):
                        S.dma("sp", Cin[:, ri, :, dup * 64:(dup + 1) * 64],
                              src.rearrange("(k q) p -> q k p", q=128), w=["Cin"])

                ck(2)
                V = nc.vector
                S.op("act", lambda: nc.scalar.activation(dt[:], ldt[:], AF.Exp), r=["ldt"], w=["dt"])
                S.op("act", lambda: nc.scalar.activation(mag[:], lr[:], AF.Exp, scale=dt[:, 0:1]), r=["lr", "dt"], w=["mag"])
                S.op("dve", lambda: V.tensor_scalar(ang[:], li[:], dt[:, 0:1], None, ALU.mult), r=["li", "dt"], w=["ang"])
                S.op("dve", lambda: V.tensor_scalar(angc[:], ang[:], math.pi / 2, None, ALU.add), r=["ang"], w=["angc"])
                for a_, key in ((ang, "ang"), (angc, "angc")):
                    for _ in range(4):
                        S.op("dve", lambda a_=a_: V.tensor_scalar(tm[:], a_[:], math.pi, 2 * math.pi, ALU.is_gt, ALU.mult),
                             r=[key], w=["tm"])
                        S.op("dve", lambda a_=a_: V.tensor_tensor(a_[:], a_[:], tm[:], ALU.subtract), r=["tm", key], w=[key])
                S.op("act", lambda: nc.scalar.activation(ang[:], ang[:], AF.Sin), r=["ang"], w=["ang"])
                S.op("act", lambda: nc.scalar.activation(angc[:], angc[:], AF.Sin), r=["angc"], w=["angc"])
                for dup in range(2):
                    S.op("dve", lambda dup=dup: V.tensor_tensor(T4[:, 0, dup, :], mag[:], angc[:], ALU.mult),
                         r=["mag", "angc"], w=["T4"])
                    S.op("dve", lambda dup=dup: V.tensor_tensor(T4[:, 1, dup, :], mag[:], ang[:], ALU.mult),
                         r=["mag", "ang"], w=["T4"])
                a_n = T4[:, 0, 0, :]
                b_n = T4[:, 1, 0, :]
                S.op("dve", lambda: V.tensor_scalar(nr[:], a_n, -1.0, None, ALU.add), r=["T4"], w=["nr"])
                S.op("dve", lambda: V.tensor_tensor(tm[:], lr[:], lr[:], ALU.mult), r=["lr"], w=["tm"])
                S.op("dve", lambda: V.tensor_tensor(tm2[:], li[:], li[:], ALU.mult), r=["li"], w=["tm2"])
                S.op("dve", lambda: V.tensor_tensor(tm[:], tm[:], tm2[:], ALU.add), r=["tm", "tm2"], w=["tm"])
                S.op("dve", lambda: V.reciprocal(rden[:], tm[:]), r=["tm"], w=["rden"])
                S.op("dve", lambda: V.tensor_tensor(tm[:], nr[:], lr[:], ALU.mult), r=["nr", "lr"], w=["tm"])
                S.op("dve", lambda: V.tensor_tensor(tm2[:], b_n, li[:], ALU.mult), r=["T4", "li"], w=["tm2"])
                S.op("dve", lambda: V.tensor_tensor(tm[:], tm[:], tm2[:], ALU.add), r=["tm", "tm2"], w=["tm"])
                for dup in range(2):
                    S.op("dve", lambda dup=dup: V.tensor_tensor(T4[:, 2, dup, :], tm[:], rden[:], ALU.mult),
                         r=["tm", "rden"], w=["T4"])
                S.op("dve", lambda: V.tensor_tensor(tm[:], b_n, lr[:], ALU.mult), r=["T4", "lr"], w=["tm"])
                S.op("dve", lambda: V.tensor_tensor(tm2[:], nr[:], li[:], ALU.mult), r=["nr", "li"], w=["tm2"])
                S.op("dve", lambda: V.tensor_tensor(tm[:], tm[:], tm2[:], ALU.subtract), r=["tm", "tm2"], w=["tm"])
                for dup in range(2):
                    S.op("dve", lambda dup=dup: V.tensor_tensor(T4[:, 3, dup, :], tm[:], rden[:], ALU.mult),
                         r=["tm", "rden"], w=["T4"])
                ck(3)
                for q in range(4):
                    S.op("pe", lambda q=q: nc.tensor.transpose(C.ps[0][:, 0:128], T4[:, q, :, :].rearrange("g a p -> g (a p)"), C.ident[:]),
                         r=["T4", "ident"], w=[("ps", 0)])
                    S.op("dve", lambda q=q: V.tensor_copy(S4[:, q, :], C.ps[0][:, 0:128]), r=[("ps", 0)], w=["S4"])
                for hh in range(2):
                    ps_ = slice(hh * 64, (hh + 1) * 64)
                    for ri in range(2):
                        S.op("dve", lambda hh=hh, ps_=ps_, ri=ri: V.tensor_copy(C.A1[ps_, ri, :], S4[ps_, 0, hh::2]),
                             r=["S4"], w=["A1"])
                    S.op("dve", lambda hh=hh, ps_=ps_: V.tensor_scalar(C.A2[ps_, 0, :], S4[ps_, 1, hh::2], -1.0, None, ALU.mult),
                         r=["S4"], w=["A2"])
                    S.op("dve", lambda hh=hh, ps_=ps_: V.tensor_copy(C.A2[ps_, 1, :], S4[ps_, 1, hh::2]),
                         r=["S4"], w=["A2"])
                Ep = E0(nc.sbuf_tensor("Ep", [128, 3, 2, NPAIR], F32))
                mr = E0(nc.sbuf_tensor("mr", [128, NPAIR], F32))
                mi = E0(nc.sbuf_tensor("mi", [128, NPAIR], F32))
                q1 = E0(nc.sbuf_tensor("q1", [128, NPAIR], F32))
                q2 = E0(nc.sbuf_tensor("q2", [128, NPAIR], F32))
                q3 = E0(nc.sbuf_tensor("q3", [128, NPAIR], F32))
                S.dma("sp", Ep[:], eprev, w=["Ep"])
                S.op("dve", lambda: V.tensor_copy(mr[:], C.A1[:, 0, :]), r=["A1"], w=["mr"])
                S.op("dve", lambda: V.tensor_copy(mi[:], C.A2[:, 1, :]), r=["A2"], w=["mi"])
                for lv in range(10):
                    if lv < 7:
                        S.op("dve", lambda lv=lv: V.tensor_copy(C.TA1[:, lv, 0, :], mr[:]), r=["mr"], w=["TA"])
                        S.op("dve", lambda lv=lv: V.tensor_copy(C.TA1[:, lv, 1, :], mr[:]), r=["mr"], w=["TA"])
                        S.op("dve", lambda lv=lv: V.tensor_scalar(C.TA2[:, lv, 0, :], mi[:], -1.0, None, ALU.mult), r=["mi"], w=["TA"])
                        S.op("dve", lambda lv=lv: V.tensor_copy(C.TA2[:, lv, 1, :], mi[:]), r=["mi"], w=["TA"])
                    S.op("dve", lambda: V.tensor_tensor(q1[:], mr[:], mr[:], ALU.mult), r=["mr"], w=["q1"])
                    S.op("dve", lambda: V.tensor_tensor(q2[:], mi[:], mi[:], ALU.mult), r=["mi"], w=["q2"])
                    S.op("dve", lambda: V.tensor_tensor(q3[:], mr[:], mi[:], ALU.mult), r=["mr", "mi"], w=["q3"])
                    S.op("dve", lambda: V.tensor_tensor(mr[:], q1[:], q2[:], ALU.subtract), r=["q1", "q2"], w=["mr"])
                    S.op("dve", lambda: V.tensor_scalar(mi[:], q3[:], 2.0, None, ALU.mult), r=["q3"], w=["mi"])
                S.op("dve", lambda: V.tensor_copy(C.Xc[:], Ep[:, 2]), r=["Ep"], w=["Xc"])
                for k in (1, 0):
                    xr, xi = C.Xc[:, 0, :], C.Xc[:, 1, :]
                    S.op("dve", lambda: V.tensor_tensor(q1[:], mr[:], xr, ALU.mult), r=["mr", "Xc"], w=["q1"])
                    S.op("dve", lambda: V.tensor_tensor(q2[:], mi[:], xi, ALU.mult), r=["mi", "Xc"], w=["q2"])
                    S.op("dve", lambda: V.tensor_tensor(q1[:], q1[:], q2[:], ALU.subtract), r=["q1", "q2"], w=["q1"])
                    S.op("dve", lambda: V.tensor_tensor(q2[:], mr[:], xi, ALU.mult), r=["mr", "Xc"], w=["q2"])
                    S.op("dve", lambda: V.tensor_tensor(q3[:], mi[:], xr, ALU.mult), r=["mi", "Xc"], w=["q3"])
                    S.op("dve", lambda: V.tensor_tensor(q2[:], q2[:], q3[:], ALU.add), r=["q2", "q3"], w=["q2"])
                    S.op("dve", lambda k=k: V.tensor_tensor(C.Xc[:, 0, :], q1[:], Ep[:, k, 0, :], ALU.add), r=["q1", "Ep"], w=["Xc"])
                    S.op("dve", lambda k=k: V.tensor_tensor(C.Xc[:, 1, :], q2[:], Ep[:, k, 1, :], ALU.add), r=["q2", "Ep"], w=["Xc"])
                ck(4)
                fre = S4[:, 2, :].unsqueeze(2).broadcast_to([128, 128, 16])
                fim = S4[:, 3, :].unsqueeze(2).broadcast_to([128, 128, 16])
                S.op("dve", lambda: V.tensor_tensor(Bb[:, 0], Bz[:, 0], fre, ALU.mult), r=["Bz", "S4"], w=["Bb"])
                S.op("dve", lambda: V.tensor_tensor(tb1[:], Bz[:, 1], fim, ALU.mult), r=["Bz", "S4"], w=["tb1"])
                S.op("dve", lambda: V.tensor_tensor(Bb[:, 0], Bb[:, 0], tb1[:], ALU.subtract), r=["tb1", "Bb"], w=["Bb"])
                S.op("dve", lambda: V.tensor_tensor(Bb[:, 1], Bz[:, 1], fre, ALU.mult), r=["Bz", "S4"], w=["Bb"])
                S.op("dve", lambda: V.tensor_tensor(tb1[:], Bz[:, 0], fim, ALU.mult), r=["Bz", "S4"], w=["tb1"])
                S.op("dve", lambda: V.tensor_tensor(Bb[:, 1], Bb[:, 1], tb1[:], ALU.add), r=["tb1", "Bb"], w=["Bb"])
                mk = C.mask[:].rearrange("p (g c) -> p g c", c=16)
                for ri in range(2):
                    for ch in range(NCH):
                        S.op("dve", lambda ri=ri, ch=ch: V.tensor_tensor(
                            Bb[:, ri, ch * 8:(ch + 1) * 8, :], Bb[:, ri, ch * 8:(ch + 1) * 8, :], mk, ALU.mult),
                            r=["Bb", "mask"], w=["Bb"])
                ck(5)
                for ri in range(2):
                    for ch in range(NCH):
                        pi = (ri * NCH + ch) % 4
                        S.op("pe", lambda ri=ri, ch=ch, pi=pi: nc.tensor.transpose(
                            C.ps[pi][:, 0:128], Bb[:, ri, ch * 8:(ch + 1) * 8, :].rearrange("p g c -> p (g c)"), C.ident[:]),
                            r=["Bb", "ident"], w=[("ps", pi)])
                        S.op("act", lambda ri=ri, ch=ch, pi=pi: nc.scalar.copy(C.lbu[:, ch, ri, :], C.ps[pi][:, 0:128]),
                             r=[("ps", pi)], w=["lbu"])
                for ri in range(2):
                    for ch in range(NCH):
                        pi = (ri * NCH + ch) % 4
                        S.op("pe", lambda ri=ri, ch=ch, pi=pi: nc.tensor.transpose(
                            C.ps[pi][:, 0:128], Cin[:, ri, ch, :], C.ident[:]),
                            r=["Cin", "ident"], w=[("ps", pi)])
                        if ri == 0:
                            S.op("dve", lambda ri=ri, ch=ch, pi=pi: V.tensor_tensor(
                                C.cz[:, ch, ri, :], C.ps[pi][:, 0:128], C.mask[:], ALU.mult),
                                r=[("ps", pi), "mask"], w=["cz"])
                        else:
                            S.op("dve", lambda ri=ri, ch=ch, pi=pi: V.scalar_tensor_tensor(
                                C.cz[:, ch, ri, :], C.ps[pi][:, 0:128], -1.0, C.mask[:], ALU.mult, ALU.mult),
                                r=[("ps", pi), "mask"], w=["cz"])
                for ch in range(NCH):
                    S.op("dve", lambda ch=ch: V.tensor_scalar(C.dg[:, ch, :], C.ident[:], C.gn[:, 3, ch:ch + 1], None, ALU.mult),
                         r=["ident", "gains"], w=["dg"])
                S.barrier()

            ck(6)
            rmsnorm(S, nc, C, C.gn[:, 0, :])

            ck(7)
            with ExitStack() as st1:
                E1 = st1.enter_context
                OFF = 0 if state_only else 1
                Bf = E1(nc.sbuf_tensor("Bf", [128, 2, NPAIR, TBS + OFF], F32))
                if not state_only:
                    trs = E1(nc.sbuf_tensor("trs", [128, 2, NPAIR, TBS // 2], F32))
                t1 = E1(nc.sbuf_tensor("t1", [128, 2, NPAIR], F32))
                t2 = E1(nc.sbuf_tensor("t2", [128, 2, NPAIR], F32))
                if state_only:
                    tr1 = E1(nc.sbuf_tensor("tr1", [128, 2, NPAIR, TBS // 2], F32))
                    tr2 = E1(nc.sbuf_tensor("tr2", [128, 2, NPAIR, TBS // 2], F32))
                if not state_only:
                    Xb = E1(nc.sbuf_tensor("Xb", [128, 2, NPAIR, TBS], BF16))
                    ysb = E1(nc.sbuf_tensor("ysb", [128, 512], F32))
                    y2 = E1(nc.sbuf_tensor("y2", [128, 512], F32))
                    y3 = E1(nc.sbuf_tensor("y3", [128, 512], F32))
                V = nc.vector
                nblk = int(os.environ.get("L0_NBLK", T // TBS))
                for blk in range(nblk):
                    tsl = slice(blk * TBS, (blk + 1) * TBS)
                    if not state_only:
                        if blk == 0:
                            S.op("dve", lambda: V.tensor_copy(Bf[:, :, :, 0], C.Xc[:]), r=["Xc", "Bf"], w=["Bf"])
                        else:
                            S.op("dve", lambda: V.tensor_copy(Bf[:, :, :, 0], Bf[:, :, :, TBS]), r=["Bf"], w=["Bf"])
                    for ri in range(2):
                        for half in range(2):
                            fns = []
                            for cc in range(8):
                                ch = half * 8 + cc
                                for j in range(4):
                                    fns.append(lambda cc=cc, ch=ch, j=j, ri=ri: nc.tensor.matmul(
                                        C.ps[j][:, cc * TBS:(cc + 1) * TBS],
                                        C.lbu[32 * j:32 * j + 32, ch, ri, :], C.h[32 * j:32 * j + 32, ch, tsl],
                                        start=True, stop=True, tile_position=(32 * j, 0)))
                            S.group("pe", fns, r=["lbu", "h"], w=[("ps", 0), ("ps", 1), ("ps", 2), ("ps", 3)])
                            for j in range(4):
                                S.op("act", lambda j=j, ri=ri, half=half: nc.scalar.copy(
                                    Bf[:, ri, half * 32 + j:half * 32 + 32:4, OFF:OFF + TBS],
                                    C.ps[j][:].rearrange("p (a t) -> p a t", t=TBS)),
                                    r=[("ps", j)], w=["Bf"])
                    if state_only:
                        for lv in range(6):
                            sgl = 1 << lv
                            m = TBS >> (lv + 1)
                            Lv = Bf[:, :, :, sgl - 1::2 * sgl]
                            Rv = Bf[:, :, :, 2 * sgl - 1::2 * sgl]
                            a1 = C.TA1[:, lv].unsqueeze(3).broadcast_to([128, 2, NPAIR, m])
                            S.op("dve", lambda: V.tensor_tensor(tr1[:, :, :, 0:m], Lv, a1, ALU.mult), r=["Bf", "TA"], w=["tr1"])
                            for ri in range(2):
                                a2 = C.TA2[:, lv, ri, :].unsqueeze(2).broadcast_to([128, NPAIR, m])
                                S.op("dve", lambda ri=ri, a2=a2: V.tensor_tensor(
                                    tr2[:, ri, :, 0:m], Bf[:, 1 - ri, :, sgl - 1::2 * sgl], a2, ALU.mult),
                                    r=["Bf", "TA"], w=["tr2"])
                            S.op("dve", lambda: V.tensor_tensor(Rv, Rv, tr1[:, :, :, 0:m], ALU.add), r=["tr1", "Bf"], w=["Bf"])
                            S.op("dve", lambda: V.tensor_tensor(Rv, Rv, tr2[:, :, :, 0:m], ALU.add), r=["tr2", "Bf"], w=["Bf"])
                        S.op("dve", lambda: V.tensor_tensor(t1[:], C.TA1[:, 6], C.Xc[:], ALU.mult), r=["Xc", "TA"], w=["t1"])
                        for ri in range(2):
                            S.op("dve", lambda ri=ri: V.tensor_tensor(t2[:, ri, :], C.TA2[:, 6, ri, :], C.Xc[:, 1 - ri, :], ALU.mult),
                                 r=["Xc", "TA"], w=["t2"])
                        S.op("dve", lambda: V.tensor_tensor(C.Xc[:], t1[:], t2[:], ALU.add), r=["t1", "t2"], w=["Xc"])
                        S.op("dve", lambda: V.tensor_tensor(C.Xc[:], C.Xc[:], Bf[:, :, :, TBS - 1], ALU.add), r=["Bf", "Xc"], w=["Xc"])
                        continue
                    if not state_only:
                        def cmadd(lv, Lsl, Msl, m):
                            a1 = C.TA1[:, lv].unsqueeze(3).broadcast_to([128, 2, NPAIR, m])
                            S.op("dve", lambda: V.tensor_tensor(trs[:, :, :, 0:m], Bf[:, :, :, Lsl], a1, ALU.mult),
                                 r=["Bf", "TA"], w=["trs"])
                            S.op("dve", lambda: V.tensor_tensor(Bf[:, :, :, Msl], Bf[:, :, :, Msl], trs[:, :, :, 0:m], ALU.add),
                                 r=["trs", "Bf"], w=["Bf"])
                            for ri in range(2):
                                a2 = C.TA2[:, lv, ri, :].unsqueeze(2).broadcast_to([128, NPAIR, m])
                                S.op("dve", lambda ri=ri, a2=a2: V.tensor_tensor(
                                    trs[:, ri, :, 0:m], Bf[:, 1 - ri, :, Lsl], a2, ALU.mult), r=["Bf", "TA"], w=["trs"])
                            S.op("dve", lambda: V.tensor_tensor(Bf[:, :, :, Msl], Bf[:, :, :, Msl], trs[:, :, :, 0:m], ALU.add),
                                 r=["trs", "Bf"], w=["Bf"])
                        for lv in range(6):
                            sgl = 1 << lv
                            cmadd(lv, slice(sgl, TBS + 1, 2 * sgl), slice(2 * sgl, TBS + 1, 2 * sgl), TBS >> (lv + 1))
                        for lv in range(6, -1, -1):
                            sgl = 1 << lv
                            m = max(1, TBS >> (lv + 1))
                            cmadd(lv, slice(0, TBS, 2 * sgl), slice(sgl, TBS + 1, 2 * sgl), m)
                        S.op("dve", lambda: V.tensor_copy(C.Xc[:], Bf[:, :, :, TBS]), r=["Bf"], w=["Xc"])
                        S.op("act", lambda: nc.scalar.copy(Xb[:], Bf[:, :, :, 1:TBS + 1]), r=["Bf"], w=["Xb"])
                    for t in ([] if not state_only else []):
                        pass
                    for t in range(0):
                        prev = C.Xc[:] if t == 0 else Bf[:, :, :, t - 1]
                        cur = Bf[:, :, :, t]
                        S.op("dve", lambda prev=prev: V.tensor_tensor(t1[:], C.A1[:], prev, ALU.mult),
                             r=["Bf", "Xc", "A1"], w=["t1"])
                        for ri in range(2):
                            pv = C.Xc[:, 1 - ri, :] if t == 0 else Bf[:, 1 - ri, :, t - 1]
                            S.op("dve", lambda pv=pv, ri=ri: V.tensor_tensor(t2[:, ri, :], C.A2[:, ri, :], pv, ALU.mult),
                                 r=["Bf", "Xc", "A2"], w=["t2"])
                        S.op("dve", lambda cur=cur: V.tensor_tensor(cur, cur, t1[:], ALU.add), r=["t1", "Bf"], w=["Bf"])
                        S.op("dve", lambda cur=cur: V.tensor_tensor(cur, cur, t2[:], ALU.add), r=["t2", "Bf"], w=["Bf"])
                    for half in range(2):
                        pi = 4 + half
                        fns = []
                        for cc in range(8):
                            ch = half * 8 + cc
                            osl = slice(cc * TBS, (cc + 1) * TBS)
                            fns.append(lambda ch=ch, osl=osl, pi=pi: nc.tensor.matmul(
                                C.ps[pi][:, osl], C.dg[:, ch, :], C.h[:, ch, tsl], start=(ch % 8 == 0), stop=False))
                            for j in range(4):
                                pair = ch * 4 + j
                                for ri in range(2):
                                    fns.append(lambda ch=ch, osl=osl, pi=pi, j=j, ri=ri, pair=pair: nc.tensor.matmul(
                                        C.ps[pi][32 * j:32 * j + 32, osl], C.cz[:, ch, ri, 32 * j:32 * j + 32],
                                        Xb[:, ri, pair, :], start=False, stop=(ri == 1 and j == 3),
                                        tile_position=(0, 32 * j), skip_group_check=True))
                        S.group("pe", fns, r=["dg", "h", "cz", "Xb"], w=[("ps", pi)])
                        yp = C.ps[pi]
                        S.op("act", lambda yp=yp: nc.scalar.copy(ysb[:], yp[:]), r=[("ps", pi)], w=["ysb"])
                        S.op("dve", lambda: V.tensor_tensor(y2[:], ysb[:], ysb[:], ALU.mult), r=["ysb"], w=["y2"])
                        S.op("dve", lambda: V.tensor_scalar(y2[:], y2[:], 0.044715, 1.0, ALU.mult, ALU.add), r=["y2"], w=["y2"])
                        S.op("dve", lambda: V.tensor_tensor(y2[:], y2[:], ysb[:], ALU.mult), r=["y2", "ysb"], w=["y2"])
                        S.op("act", lambda: nc.scalar.activation(y3[:], y2[:], AF.Sigmoid, scale=1.5957691216057308),
                             r=["y2"], w=["y3"])
                        S.op("dve", lambda half=half: V.tensor_tensor(
                            C.h[:, half * 8:(half + 1) * 8, tsl], ysb[:].rearrange("p (a t) -> p a t", t=TBS),
                            y3[:].rearrange("p (a t) -> p a t", t=TBS), ALU.mult),
                            r=["ysb", "y3", "h"], w=["h"])
                S.dma("sp", xend, C.Xc[:], r=["Xc"], w=["xend_d"])
                S.barrier()
            st01.close()

            if not state_only:
                with ExitStack() as st2:
                    E2 = st2.enter_context
                    C.stg = [E2(nc.sbuf_tensor(f"stg{i}", [128, 2048], F32)) for i in range(3)]
                    C.stg_i = 0
                    C.wbuf = [E2(nc.sbuf_tensor(f"wb{i}", [128, NCH, 128], BF16)) for i in range(4)]
                    C.wb_i = 0
                    C.ps_i = 0
                    C.lin_ps = [0, 1, 2, 3]
                    C.out_ps = [4, 5]
                    wo = [E2(nc.sbuf_tensor(f"wo{i}", [128, 2048], BF16)) for i in range(4)]
                    hid = [E2(nc.sbuf_tensor(f"hid{i}", [128, T], BF16)) for i in range(4)]
                    sg = [E2(nc.sbuf_tensor(f"sg{i}", [128, 512], F32)) for i in range(2)]
                    val = [E2(nc.sbuf_tensor(f"val{i}", [128, 512], F32)) for i in range(2)]
                    V = nc.vector
                    st_ = {}

                    def glu_consume(oc, tb, pst, pk):
                        sl = slice(tb * 512, (tb + 1) * 512)
                        if oc % 2 == 0:
                            i = tb
                            S.op("act", lambda: nc.scalar.copy(val[i][:], pst[:]), r=[pk], w=[("val", i)])
                        else:
                            i = tb
                            c = oc // 2
                            S.op("act", lambda: nc.scalar.activation(sg[i][:], pst[:], AF.Sigmoid), r=[pk], w=[("sg", i)])
                            S.op("dve", lambda: V.tensor_tensor(sg[i][:], sg[i][:], val[i][:], ALU.mult),
                                 r=[("sg", i), ("val", i)], w=[("sg", i)])
                            S.op("dve", lambda: V.tensor_tensor(C.x[:, c, sl], C.x[:, c, sl], sg[i][:], ALU.add),
                                 r=[("sg", i), "x"], w=["x"])

                    for c in range(NCH):
                        for which in range(2):
                            linear_fm(S, nc, C, w_glu, 1, which * D + c * 128, C.h, "h",
                                      lambda oc, tb, pst, pk, which=which, c=c: glu_consume(2 * c + which, tb, pst, pk))
                    rmsnorm(S, nc, C, C.gn[:, 1, :])
                    ffn(S, nc, C, w_in, w_out, wo, hid, sg)
                    for c in range(NCH):
                        S.dma("sp", x1T[c * 128:(c + 1) * 128, :], C.x[:, c, :], r=["x"], w=["x1_d"])
                    rmsnorm(S, nc, C, C.gn[:, 2, :])
                    kvb = [E2(nc.sbuf_tensor(f"kvb{i}", [128, 512], BF16)) for i in range(2)]

                    def kv_consume(oc, tb, pst, pk):
                        sl = slice(tb * 512, (tb + 1) * 512)
                        i = (oc * 2 + tb) % 2
                        S.op("act", lambda: nc.scalar.copy(kvb[i][:], pst[:]), r=[pk], w=[("kvb", i)])
                        dst = kT if oc < 4 else vT
                        o4 = oc % 4
                        S.dma("sp", dst[o4 * 128:(o4 + 1) * 128, sl], kvb[i][:], r=[("kvb", i)], w=["kv_d"])
                    linear_fm(S, nc, C, w_kv, 8, 0, C.h, "h", kv_consume)
                    S.barrier()
        except _Stop:
            pass
        S.finish()
    return nc


def _key_sets():
    ks = []
    for kb in range(-1, 8):
        k0 = 128 * kb
        qs, qe = max(0, k0), min(T, k0 + 256)
        ks.append((k0 + 2048, 1, 128, qs, 1, qe - qs, qs - k0, 1 if kb < 0 else 0, 0))
    for r in range(4):
        for kc in (-128, 0, 128):
            qs, qe = max(0, kc), min(256, kc + 256)
            ks.append((r + 4 * kc + 2048, 4, 128, r + 4 * qs, 4, qe - qs, qs - kc, 1 if kc < 0 else 0, 1))
    for r in range(16):
        ks.append((r, 16, 128, r, 16, 64, 128, 2, 2))
        ks.append((r + 2048, 16, 64, r, 16, 64, 0, 0, 2))
    return ks


def build_l1():
    nc = bass.Bass("TRN2", target_bir_lowering=False)
    xT = _dram_in(nc, "xT", [D, T])
    kTa = _dram_in(nc, "kTa", [512, 3072], BF16)
    Va = _dram_in(nc, "Va", [3072, 512], BF16)
    gains = _dram_in(nc, "gains", [128, 4, NCH])
    bandm = _dram_in(nc, "bandm", [128, 256], BF16)
    kbias = _dram_in(nc, "kbias", [128, 4])
    w_q = _dram_in(nc, "w_q", [D, 3 * D])
    w_o = _dram_in(nc, "w_o", [D, D])
    w_in = _dram_in(nc, "w_in", [D, 2 * FF])
    w_out = _dram_in(nc, "w_out", [FF, D])
    outT = _dram_out(nc, "outT", [D, T])
    KS = _key_sets()
    with ExitStack() as st:
        E = st.enter_context
        S = Sched(nc, st)
        C = Ctx()
        V = nc.vector
        C.x = E(nc.sbuf_tensor("x", [128, NCH, T], F32))
        C.h = E(nc.sbuf_tensor("h", [128, NCH, T], BF16))
        C.ones = E(nc.sbuf_tensor("ones", [128, 128], BF16))
        C.gn = E(nc.sbuf_tensor("gn", [128, 4, NCH], F32))
        C.epsc = E(nc.sbuf_tensor("epsc", [128, 1], F32))
        C.rstd = E(nc.sbuf_tensor("rstd", [128, 512], F32))
        C.sq = [E(nc.sbuf_tensor(f"sq{i}", [128, 512], BF16)) for i in range(2)]
        C.ps = [E(nc.psum_tensor(f"ps{i}", [128, 512], F32)) for i in range(8)]
        S.op("dve", lambda: V.memset(C.ones[:], 1.0), w=["ones"])
        S.op("dve", lambda: V.memset(C.epsc[:], EPS), w=["epsc"])
        S.dma("sp", C.gn[:], gains, w=["gains"])
        for c in range(NCH):
            S.dma("sp", C.x[:, c, :], xT[c * 128:(c + 1) * 128, :], w=["x"])
        rmsnorm(S, nc, C, C.gn[:, 0, :])
        with ExitStack() as sa:
            EA = sa.enter_context
            attn = EA(nc.sbuf_tensor("attn", [128, NCH, T], BF16))
            with ExitStack() as sb:
                EB = sb.enter_context
                C.stg = [EB(nc.sbuf_tensor(f"stg{i}", [128, 2048], F32)) for i in range(3)]
                C.stg_i = 0
                C.wbuf = [EB(nc.sbuf_tensor(f"wb{i}", [128, NCH, 128], BF16)) for i in range(3)]
                C.wb_i = 0
                C.ps_i = 0
                C.lin_ps = [6, 7]
                mask = EB(nc.sbuf_tensor("bandS", [128, 256], BF16))
                kb_ = EB(nc.sbuf_tensor("kbS", [128, 4], F32))
                kts = EB(nc.sbuf_tensor("kts", [128, 3072], BF16))
                Vt = EB(nc.sbuf_tensor("Vt", [128, len(KS), 128], BF16))
                qg = [EB(nc.sbuf_tensor(f"qg{i}", [128, T], BF16)) for i in range(3)]
                pe_ = [EB(nc.sbuf_tensor(f"pe{i}", [128, 256], BF16)) for i in range(4)]
                pT = [EB(nc.sbuf_tensor(f"pT{i}", [128, 256], BF16)) for i in range(4)]
                rec = EB(nc.sbuf_tensor("rec", [128, 512], F32))
                S.dma("sp", mask[:], bandm, w=["mask"])
                S.dma("sp", kb_[:], kbias, w=["kb"])
                it = 0
                for hh in range(16):
                    kvh = hh // 4
                    if hh % 4 == 0:
                        S.dma("sp", kts[:], kTa[kvh * 128:(kvh + 1) * 128, :], w=["kts"])
                        for i, (k0, kst, m, q0, qst, n, off, bc, g) in enumerate(KS):
                            S.dma("sp", Vt[0:m, i, :], Va[k0:k0 + kst * (m - 1) + 1:kst, kvh * 128:(kvh + 1) * 128],
                                  w=["Vt"])
                    for g in range(3):
                        def q_consume(oc, tb, pst, pk, g=g):
                            sl = slice(tb * 512, (tb + 1) * 512)
                            S.op("act", lambda: nc.scalar.activation(qg[g][:, sl], pst[:], AF.Copy, scale=HD_SCALE),
                                 r=[pk], w=[("qg", g)])
                        linear_fm(S, nc, C, w_q, 1, g * D + hh * 128, C.h, "h", q_consume)
                    first = {0: True, 1: True}
                    LOOK = 1
                    nks = len(KS)

                    def stage_a(i):
                        (k0, kst, m, q0, qst, n, off, bc, g) = KS[i]
                        slot = (hh * nks + i) % 2
                        sps = C.ps[4 + slot]
                        S.group("pe", [lambda: nc.tensor.matmul(
                            sps[0:m, 0:n], kts[:, k0:k0 + kst * (m - 1) + 1:kst],
                            qg[g][:, q0:q0 + qst * (n - 1) + 1:qst], start=True, stop=True)],
                            r=["kts", ("qg", g)], w=[("s", slot)])
                        S.op("act", lambda: nc.scalar.activation(pe_[slot][0:m, 0:n], sps[0:m, 0:n], AF.Exp,
                                                                 bias=kb_[0:m, bc:bc + 1], scale=1.0),
                             r=[("s", slot), "kb"], w=[("pe", slot)])
                        S.op("dve", lambda: V.tensor_tensor(pT[slot][0:m, 0:n], pe_[slot][0:m, 0:n], mask[0:m, off:off + n], ALU.mult),
                             r=[("pe", slot), "mask"], w=[("pT", slot)])

                    def stage_b(i):
                        (k0, kst, m, q0, qst, n, off, bc, g) = KS[i]
                        slot = (hh * nks + i) % 2
                        segs = []
                        isplit = min(n, max(0, -(-(512 - q0) // qst)))
                        if isplit > 0:
                            segs.append((0, isplit, 0))
                        if isplit < n:
                            segs.append((isplit, n, 1))
                        fns = []
                        wk = []
                        for (i0, i1, bank) in segs:
                            c0 = q0 + qst * i0 - 512 * bank
                            cs = slice(c0, c0 + qst * (i1 - i0 - 1) + 1, qst)
                            st_ = first[bank]
                            first[bank] = False
                            fns.append(lambda cs=cs, bank=bank, i0=i0, i1=i1, st_=st_: nc.tensor.matmul(
                                C.ps[bank][:, cs], Vt[0:m, i, :], pT[slot][0:m, i0:i1], start=st_, stop=False,
                                skip_group_check=True))
                            fns.append(lambda cs=cs, bank=bank, i0=i0, i1=i1, st_=st_: nc.tensor.matmul(
                                C.ps[2 + bank][:, cs], C.ones[0:m, :], pT[slot][0:m, i0:i1], start=st_, stop=False,
                                skip_group_check=True))
                            wk += [("ps", bank), ("ps", 2 + bank)]
                        S.group("pe", fns, r=[("pT", slot), "Vt", "ones"], w=wk)

                    for step in range(nks + LOOK):
                        if step < nks:
                            stage_a(step)
                        if step - LOOK >= 0:
                            stage_b(step - LOOK)
                    for bank in range(2):
                        sl = slice(bank * 512, (bank + 1) * 512)
                        S.op("dve", lambda: V.reciprocal(rec[:], C.ps[2 + bank][:]), r=[("ps", 2 + bank)], w=["rec"])
                        S.op("dve", lambda: V.tensor_tensor(attn[:, hh, sl], C.ps[bank][:], rec[:], ALU.mult),
                             r=[("ps", bank), "rec"], w=["attn"])
                S.barrier()
            with ExitStack() as sc:
                EC = sc.enter_context
                C.stg = [EC(nc.sbuf_tensor(f"stgc{i}", [128, 2048], F32)) for i in range(3)]
                C.stg_i = 0
                C.wbuf = [EC(nc.sbuf_tensor(f"wbc{i}", [128, NCH, 128], BF16)) for i in range(4)]
                C.wb_i = 0
                C.lin_ps = [0, 1, 2, 3]

                def o_consume(oc, tb, pst, pk):
                    sl = slice(tb * 512, (tb + 1) * 512)
                    S.op("dve", lambda: V.tensor_tensor(C.x[:, oc, sl], C.x[:, oc, sl], pst[:], ALU.add),
                         r=[pk, "x"], w=["x"])
                linear_fm(S, nc, C, w_o, NCH, 0, attn, "attn", o_consume)
                S.barrier()
        with ExitStack() as sd:
            ED = sd.enter_context
            C.stg = [ED(nc.sbuf_tensor(f"stgd{i}", [128, 2048], F32)) for i in range(3)]
            C.stg_i = 0
            C.wbuf = [ED(nc.sbuf_tensor(f"wbd{i}", [128, NCH, 128], BF16)) for i in range(4)]
            C.wb_i = 0
            C.lin_ps = [0, 1, 2, 3]
            C.out_ps = [4, 5]
            wo = [ED(nc.sbuf_tensor(f"wo{i}", [128, 2048], BF16)) for i in range(4)]
            hid = [ED(nc.sbuf_tensor(f"hid{i}", [128, T], BF16)) for i in range(4)]
            sg = [ED(nc.sbuf_tensor(f"sg{i}", [128, 512], F32)) for i in range(2)]
            rmsnorm(S, nc, C, C.gn[:, 1, :])
            ffn(S, nc, C, w_in, w_out, wo, hid, sg)
            rmsnorm(S, nc, C, C.gn[:, 2, :], hkey="x", dst=C.x)
            for c in range(NCH):
                S.dma("sp", outT[c * 128:(c + 1) * 128, :], C.x[:, c, :], r=["x"], w=["out_d"])
            S.barrier()
        S.finish()
    return nc


HD_SCALE = 128 ** -0.5

_cache = {}


def _prog(name, fn):
    if name not in _cache:
        _cache[name] = fn()
    return _cache[name]


def _fm(v):
    return np.ascontiguousarray(np.asarray(v, np.float32).reshape(NCH, 128).T)


def layer0(inputs):
    x = np.asarray(inputs["x"], np.float32)
    n = 8
    gains = np.ascontiguousarray(np.stack([_fm(inputs["a_norm_mix"][0]), _fm(inputs["ffn_norm"][0]),
                                           _fm(inputs["kv_norm"]), _fm(inputs["s5_d"][0])], axis=1))
    ident = np.eye(128, dtype=np.float32)
    hp = np.arange(128) // 64
    g8 = (np.arange(128) // 16) % 2
    mask_sm = (hp[:, None] == g8[None, :]).astype(np.float32)
    base = {
        "lam_re": np.ascontiguousarray(inputs["s5_lam_re"][0]), "lam_im": np.ascontiguousarray(inputs["s5_lam_im"][0]),
        "log_dt": np.ascontiguousarray(inputs["s5_log_dt"][0].reshape(128, 1)),
        "b_re": np.ascontiguousarray(inputs["s5_b_re"][0]), "b_im": np.ascontiguousarray(inputs["s5_b_im"][0]),
        "c_re": np.ascontiguousarray(inputs["s5_c_re"][0].reshape(2048, 64)),
        "c_im": np.ascontiguousarray(inputs["s5_c_im"][0].reshape(2048, 64)),
        "gains": gains, "ident": ident, "mask_sm": mask_sm,
    }
    xTs = []
    for c in range(n):
        b, j = c // 4, c % 4
        xTs.append(np.ascontiguousarray(x[b, j * T:(j + 1) * T, :].T))
    zero_e = np.zeros((128, 3, 2, NPAIR), np.float32)
    nc_s = _prog("l0s", lambda: build_l0(True))
    maps = [dict(base, xT=xTs[c], eprev=zero_e) for c in range(n)]
    res = run_bass_kernel_spmd(nc_s, maps, core_ids=list(range(n)))
    eprev = []
    for c in range(n):
        b, j = c // 4, c % 4
        e = np.zeros((128, 3, 2, NPAIR), np.float32)
        for k in range(3):
            if j - 1 - k >= 0:
                e[:, k] = np.asarray(res.results[b * 4 + j - 1 - k]["xend"])
        eprev.append(e)
    nc_f = _prog("l0f", lambda: build_l0(False))
    full = dict(base, w_glu=np.ascontiguousarray(inputs["s5_w_glu"][0]),
                w_in=np.ascontiguousarray(inputs["ffn_w_in"][0]), w_out=np.ascontiguousarray(inputs["ffn_w_out"][0]),
                w_kv=np.ascontiguousarray(inputs["w_kv"]))
    maps = [dict(full, xT=xTs[c], eprev=eprev[c]) for c in range(n)]
    res = run_bass_kernel_spmd(nc_f, maps, core_ids=list(range(n)))
    return res.results


def layer1(inputs, r0):
    n = 8
    gains = np.ascontiguousarray(np.stack([_fm(inputs["b_norm_mix"][0]), _fm(inputs["ffn_norm"][1]),
                                           _fm(inputs["final_norm"]), _fm(inputs["final_norm"])], axis=1))
    bdt = np.asarray(r0[0]["kT"]).dtype
    sidx = np.arange(128)[:, None]
    qidx = np.arange(256)[None, :]
    bandm = ((sidx <= qidx) & (qidx <= sidx + 128)).astype(np.float32).astype(bdt)
    base = {"gains": gains, "bandm": bandm,
            "w_q": np.ascontiguousarray(inputs["attn_w_q"][0]), "w_o": np.ascontiguousarray(inputs["attn_w_o"][0]),
            "w_in": np.ascontiguousarray(inputs["ffn_w_in"][1]), "w_out": np.ascontiguousarray(inputs["ffn_w_out"][1])}
    maps = []
    for c in range(n):
        b, j = c // 4, c % 4
        kparts, vparts = [], []
        for jj in (j - 2, j - 1, j):
            if jj < 0:
                kparts.append(np.zeros((512, T), bdt))
                vparts.append(np.zeros((512, T), bdt))
            else:
                kparts.append(np.asarray(r0[b * 4 + jj]["kT"]))
                vparts.append(np.asarray(r0[b * 4 + jj]["vT"]))
        kTa = np.ascontiguousarray(np.concatenate(kparts, axis=1))
        Va = np.ascontiguousarray(np.concatenate(vparts, axis=1).T)
        kbias = np.zeros((128, 4), np.float32)
        if j == 0:
            kbias[:, 1] = -30000.0
            kbias[:, 2] = -30000.0
        elif j == 1:
            kbias[:64, 2] = -30000.0
        maps.append(dict(base, xT=np.ascontiguousarray(r0[c]["x1T"]), kTa=kTa, Va=Va, kbias=kbias))
    nc1 = _prog("l1", build_l1)
    res = run_bass_kernel_spmd(nc1, maps, core_ids=list(range(n)))
    return res.results


def kernel(**inputs):
    inputs = {k: np.asarray(v) for k, v in inputs.items()}
    r0 = layer0(inputs)
    r1 = layer1(inputs, r0)
    out = np.empty((2, 4096, D), np.float32)
    for c in range(8):
        b, j = c // 4, c % 4
        out[b, j * T:(j + 1) * T, :] = np.asarray(r1[c]["outT"]).T
    return out
```

```python
import math
from contextlib import ExitStack
import numpy as np
import concourse.bass as bass
import concourse.mybir as mybir
from concourse.bass_utils import run_bass_kernel_spmd

F32 = mybir.dt.float32
BF16 = mybir.dt.bfloat16
AF = mybir.ActivationFunctionType
ALU = mybir.AluOpType

D = 2048
NCH = 16
T = 1024
NTB = 2
FF = 5632
NF = 44
EPS = 1e-6
TBS = 64
NPAIR = 64


class Sched:
    def __init__(self, nc, st):
        self.nc = nc
        self.eng = {"pe": nc.tensor, "act": nc.scalar, "dve": nc.vector, "pool": nc.gpsimd, "sp": nc.sync}
        self.sem = {}
        self.cnt = {}
        for e in ["pe", "act", "dve", "pool"]:
            self.sem["c_" + e] = st.enter_context(nc.semaphore("c_" + e))
            self.cnt["c_" + e] = 0
        for q in ["sp", "act", "pool"]:
            self.sem["d_" + q] = st.enter_context(nc.semaphore("d_" + q))
            self.cnt["d_" + q] = 0
        self.seen = {e: {} for e in self.eng}
        self.last_w = {}
        self.readers = {}
        self.stopped = False
        _CUR[0] = self

    def _wait(self, e, deps):
        need = {}
        for (c, v) in deps:
            if c == "c_" + e and e == "pe":
                continue
            if v > need.get(c, 0):
                need[c] = v
        for c, v in need.items():
            if self.seen[e].get(c, 0) >= v:
                continue
            self.eng[e].wait_ge(self.sem[c], v)
            self.seen[e][c] = v

    def _deps(self, r, w):
        deps = []
        for k in r:
            if k in self.last_w:
                deps.append(self.last_w[k])
        for k in w:
            if k in self.last_w:
                deps.append(self.last_w[k])
            deps.extend(self.readers.get(k, []))
        return deps

    def _record(self, tok, r, w):
        for k in w:
            self.last_w[k] = tok
            self.readers[k] = []
        for k in r:
            self.readers.setdefault(k, []).append(tok)

    def op(self, e, fn, r=(), w=()):
        if self.stopped:
            return
        self._wait(e, self._deps(r, w))
        ins = fn()
        c = "c_" + e
        self.cnt[c] += 1
        ins.then_inc(self.sem[c], 1)
        self._record((c, self.cnt[c]), r, w)

    def group(self, e, fns, r=(), w=()):
        if self.stopped:
            return
        self._wait(e, self._deps(r, w))
        ins = None
        for f in fns:
            ins = f()
        c = "c_" + e
        self.cnt[c] += 1
        ins.then_inc(self.sem[c], 1)
        self._record((c, self.cnt[c]), r, w)

    def dma(self, q, out, in_, r=(), w=(), **kw):
        if self.stopped:
            return
        self._wait(q, self._deps(r, w))
        ins = self.eng[q].dma_start(out=out, in_=in_, **kw)
        c = "d_" + q
        self.cnt[c] += 16
        ins.then_inc(self.sem[c], 16)
        self._record((c, self.cnt[c]), r, w)

    def barrier(self):
        if self.stopped:
            return
        for e in self.eng:
            deps = [(c, v) for c, v in self.cnt.items() if v > 0]
            self._wait(e, deps)
        self.last_w = {}
        self.readers = {}

    def finish(self):
        deps = [(c, v) for c, v in self.cnt.items() if v > 0]
        self._wait("sp", deps)


def _dram_in(nc, name, shape, dt=F32):
    return nc.dram_tensor(name, list(shape), dt, kind="ExternalInput").ap()


def _dram_out(nc, name, shape, dt=F32):
    return nc.dram_tensor(name, list(shape), dt, kind="ExternalOutput").ap()


class Ctx:
    pass


class _Stop(Exception):
    pass


import os
STOP = int(os.environ.get("L0_STOP", "0"))


_CUR = [None]


def ck(k):
    if STOP == k:
        _CUR[0].stopped = True


def rmsnorm(S, nc, C, gain, hkey="h", dst=None):
    for tb in range(NTB):
        sl = slice(tb * 512, (tb + 1) * 512)
        ss = C.ps[6 + tb]
        fns = []
        for c in range(NCH):
            sq = C.sq[c % 2]
            S.op("act", lambda c=c, sq=sq: nc.scalar.activation(sq[:], C.x[:, c, sl], AF.Square),
                 r=["x"], w=[("sq", c % 2)])
            S.op("pe", lambda c=c, sq=sq: nc.tensor.matmul(ss[:], C.ones[:], sq[:], start=(c == 0), stop=(c == NCH - 1)),
                 r=[("sq", c % 2), "ones"], w=[("ps", 6 + tb)])
        S.op("act", lambda: nc.scalar.activation(C.rstd[:], ss[:], AF.Sqrt, bias=C.epsc[:, 0:1], scale=1.0 / D),
             r=[("ps", 6 + tb)], w=["rstd"])
        S.op("dve", lambda: nc.vector.reciprocal(C.rstd[:], C.rstd[:]), r=["rstd"], w=["rstd"])
        for c in range(NCH):
            S.op("dve", lambda c=c: nc.vector.scalar_tensor_tensor(
                (C.h if dst is None else dst)[:, c, sl], C.x[:, c, sl], gain[:, c:c + 1], C.rstd[:], ALU.mult, ALU.mult),
                r=["x", "rstd", "gains"], w=[hkey])


def load_cast(S, nc, C, src_ap, dst, dkey, shape3=None):
    i = C.stg_i
    C.stg_i += 1
    s = i % len(C.stg)
    stg = C.stg[s]
    sv = stg[:] if shape3 is None else stg[:].rearrange("p (k n) -> p k n", k=shape3[0])
    S.dma("sp", sv, src_ap, w=[("stg", s)])
    dv = dst
    S.op("act", lambda: nc.scalar.copy(dv, sv), r=[("stg", s)], w=[dkey])


def _slab_issue(S, nc, C, w_ap, c0):
    wb = C.wbuf[C.wb_i % len(C.wbuf)]
    wk = ("wb", C.wb_i % len(C.wbuf))
    C.wb_i += 1
    load_cast(S, nc, C, w_ap[:, c0:c0 + 128].rearrange("(k p) n -> p k n", p=128), wb[:], wk, shape3=(NCH, 128))
    return wb, wk


def _slab_prefetch(S, nc, C, w_ap, c0):
    key = (id(w_ap), c0)
    if key not in C.pre:
        C.pre[key] = _slab_issue(S, nc, C, w_ap, c0)


def _slab_get(S, nc, C, w_ap, c0):
    key = (id(w_ap), c0)
    if key in C.pre:
        return C.pre.pop(key)
    return _slab_issue(S, nc, C, w_ap, c0)


def linear_fm(S, nc, C, w_ap, n_out_chunks, col0, src, skey, consume, nxt=None):
    for oc in range(n_out_chunks):
        c0 = col0 + oc * 128
        wb, wk = _slab_get(S, nc, C, w_ap, c0)
        if oc + 1 < n_out_chunks:
            _slab_prefetch(S, nc, C, w_ap, c0 + 128)
        elif nxt is not None:
            _slab_prefetch(S, nc, C, nxt[0], nxt[1])
        for tb in range(NTB):
            pi = C.lin_ps[C.ps_i % len(C.lin_ps)]
            C.ps_i += 1
            pst = C.ps[pi]
            sl = slice(tb * 512, (tb + 1) * 512)
            S.group("pe", [lambda k=k: nc.tensor.matmul(pst[:], wb[:, k, :], src[:, k, sl],
                                                        start=(k == 0), stop=(k == NCH - 1))
                           for k in range(NCH)],
                    r=[wk, skey], w=[("ps", pi)])
            consume(oc, tb, pst, ("ps", pi))


def ffn(S, nc, C, w_in, w_out, wo, hid, sg):
    V = nc.vector
    GS = 2
    for g in range(NF // GS):
        for fi in range(GS):
            f = g * GS + fi
            hb = hid[f % 4]
            hk = ("hid", f % 4)
            wob = wo[f % 4]
            load_cast(S, nc, C, w_out[f * 128:(f + 1) * 128, :], wob[:], ("wo", f % 4))

            def ffn_consume(oc, tb, pst, pk, hb=hb, hk=hk):
                sl = slice(tb * 512, (tb + 1) * 512)
                if oc == 0:
                    S.op("act", lambda: nc.scalar.activation(sg[tb][:], pst[:], AF.Silu), r=[pk], w=[("sg", tb)])
                else:
                    S.op("dve", lambda: V.tensor_tensor(hb[:, sl], sg[tb][:], pst[:], ALU.mult),
                         r=[pk, ("sg", tb)], w=[hk])
            linear_fm(S, nc, C, w_in, 1, f * 128, C.h, "h",
                      lambda oc, tb, pst, pk: ffn_consume(0, tb, pst, pk), nxt=(w_in, FF + f * 128))
            linear_fm(S, nc, C, w_in, 1, FF + f * 128, C.h, "h",
                      lambda oc, tb, pst, pk: ffn_consume(1, tb, pst, pk),
                      nxt=((w_in, (f + 1) * 128) if f + 1 < NF else None))
        for c in range(NCH):
            for tb in range(NTB):
                sl = slice(tb * 512, (tb + 1) * 512)
                pi = C.out_ps[C.ps_i % 2]
                C.ps_i += 1
                fns = []
                rk = []
                for fi in range(GS):
                    f = g * GS + fi
                    rk += [("wo", f % 4), ("hid", f % 4)]
                    fns.append(lambda f=f, fi=fi, pi=pi, c=c, sl=sl: nc.tensor.matmul(
                        C.ps[pi][:], wo[f % 4][:, c * 128:(c + 1) * 128], hid[f % 4][:, sl],
                        start=(fi == 0), stop=(fi == GS - 1)))
                S.group("pe", fns, r=rk, w=[("ps", pi)])
                S.op("dve", lambda pi=pi, c=c, sl=sl: V.tensor_tensor(C.x[:, c, sl], C.x[:, c, sl], C.ps[pi][:], ALU.add),
                     r=[("ps", pi), "x"], w=["x"])


def build_l0(state_only):
    nc = bass.Bass("TRN2", target_bir_lowering=False)
    xT = _dram_in(nc, "xT", [D, T])
    lam_re = _dram_in(nc, "lam_re", [128, 64])
    lam_im = _dram_in(nc, "lam_im", [128, 64])
    log_dt = _dram_in(nc, "log_dt", [128, 1])
    b_re = _dram_in(nc, "b_re", [128, 64, 16])
    b_im = _dram_in(nc, "b_im", [128, 64, 16])
    c_re = _dram_in(nc, "c_re", [2048, 64])
    c_im = _dram_in(nc, "c_im", [2048, 64])
    gains = _dram_in(nc, "gains", [128, 4, NCH])
    eprev = _dram_in(nc, "eprev", [128, 3, 2, NPAIR])
    ident_d = _dram_in(nc, "ident", [128, 128])
    mask_d = _dram_in(nc, "mask_sm", [128, 128])
    xend = _dram_out(nc, "xend", [128, 2, NPAIR])
    if not state_only:
        w_glu = _dram_in(nc, "w_glu", [D, 2 * D])
        w_in = _dram_in(nc, "w_in", [D, 2 * FF])
        w_out = _dram_in(nc, "w_out", [FF, D])
        w_kv = _dram_in(nc, "w_kv", [D, 1024])
        x1T = _dram_out(nc, "x1T", [D, T])
        kT = _dram_out(nc, "kT", [512, T], BF16)
        vT = _dram_out(nc, "vT", [512, T], BF16)

    with ExitStack() as st:
        E = st.enter_context
        S = Sched(nc, st)
        C = Ctx()
        C.x = E(nc.sbuf_tensor("x", [128, NCH, T], F32))
        C.h = E(nc.sbuf_tensor("h", [128, NCH, T], BF16))
        C.ones = E(nc.sbuf_tensor("ones", [128, 128], BF16))
        C.ident = E(nc.sbuf_tensor("identS", [128, 128], F32))
        C.mask = E(nc.sbuf_tensor("maskS", [128, 128], F32))
        C.gn = E(nc.sbuf_tensor("gn", [128, 4, NCH], F32))
        C.epsc = E(nc.sbuf_tensor("epsc", [128, 1], F32))
        C.rstd = E(nc.sbuf_tensor("rstd", [128, 512], F32))
        C.sq = [E(nc.sbuf_tensor(f"sq{i}", [128, 512], BF16)) for i in range(2)]
        C.ps = [E(nc.psum_tensor(f"ps{i}", [128, 512], F32)) for i in range(8)]
        C.Xc = E(nc.sbuf_tensor("Xc", [128, 2, NPAIR], F32))
        st01 = ExitStack()
        E01 = st01.enter_context
        C.A1 = E01(nc.sbuf_tensor("A1", [128, 2, NPAIR], F32))
        C.A2 = E01(nc.sbuf_tensor("A2", [128, 2, NPAIR], F32))
        C.lbu = E01(nc.sbuf_tensor("lbu", [128, NCH, 2, 128], BF16))
        C.cz = E01(nc.sbuf_tensor("cz", [128, NCH, 2, 128], BF16))
        C.dg = E01(nc.sbuf_tensor("dg", [128, NCH, 128], BF16))
        C.TA1 = E01(nc.sbuf_tensor("TA1", [128, 7, 2, NPAIR], F32))
        C.TA2 = E01(nc.sbuf_tensor("TA2", [128, 7, 2, NPAIR], F32))

        try:
            S.op("dve", lambda: nc.vector.memset(C.ones[:], 1.0), w=["ones"])
            S.op("dve", lambda: nc.vector.memset(C.epsc[:], EPS), w=["epsc"])
            S.dma("sp", C.ident[:], ident_d, w=["ident"])
            S.dma("sp", C.mask[:], mask_d, w=["mask"])
            S.dma("sp", C.gn[:], gains, w=["gains"])
            for c in range(NCH):
                S.dma("sp", C.x[:, c, :], xT[c * 128:(c + 1) * 128, :], w=["x"])

            ck(1)
            with ExitStack() as st0:
                E0 = st0.enter_context
                lr = E0(nc.sbuf_tensor("lr", [128, 64], F32))
                li = E0(nc.sbuf_tensor("li", [128, 64], F32))
                ldt = E0(nc.sbuf_tensor("ldt", [128, 1], F32))
                dt = E0(nc.sbuf_tensor("dt", [128, 1], F32))
                mag = E0(nc.sbuf_tensor("mag", [128, 64], F32))
                ang = E0(nc.sbuf_tensor("ang", [128, 64], F32))
                angc = E0(nc.sbuf_tensor("angc", [128, 64], F32))
                tm = E0(nc.sbuf_tensor("tm", [128, 64], F32))
                tm2 = E0(nc.sbuf_tensor("tm2", [128, 64], F32))
                nr = E0(nc.sbuf_tensor("nr", [128, 64], F32))
                rden = E0(nc.sbuf_tensor("rden", [128, 64], F32))
                T4 = E0(nc.sbuf_tensor("T4", [128, 4, 2, 64], F32))
                S4 = E0(nc.sbuf_tensor("S4", [128, 4, 128], F32))
                Bz = E0(nc.sbuf_tensor("Bz", [128, 2, 128, 16], F32))
                Bb = E0(nc.sbuf_tensor("Bb", [128, 2, 128, 16], F32))
                tb1 = E0(nc.sbuf_tensor("tb1", [128, 128, 16], F32))
                Cin = E0(nc.sbuf_tensor("Cin", [128, 2, NCH, 128], F32))
                dgf = E0(nc.sbuf_tensor("dgf", [128, 128], F32))

                S.dma("sp", lr[:], lam_re, w=["lr"])
                S.dma("sp", li[:], lam_im, w=["li"])
                S.dma("sp", ldt[:], log_dt, w=["ldt"])
                for ri, src in enumerate([b_re, b_im]):
                    for hh in range(2):
                        for gh in range(2):
                            S.dma("sp", Bz[hh * 64:(hh + 1) * 64, ri, gh * 64:(gh + 1) * 64, :],
                                  src[gh * 64:(gh + 1) * 64].rearrange("g p c -> p g c"), w=["Bz"])
                for ri, src in enumerate([c_re, c_im]):
                    for dup in range(2):
                        S.dma("sp", Cin[:, ri, :, dup * 64:(dup + 1) * 64],
                              src.rearrange("(k q) p -> q k p", q=128), w=["Cin"])

                ck(2)
                V = nc.vector
                S.op("act", lambda: nc.scalar.activation(dt[:], ldt[:], AF.Exp), r=["ldt"], w=["dt"])
                S.op("act", lambda: nc.scalar.activation(mag[:], lr[:], AF.Exp, scale=dt[:, 0:1]), r=["lr", "dt"], w=["mag"])
                S.op("dve", lambda: V.tensor_scalar(ang[:], li[:], dt[:, 0:1], None, ALU.mult), r=["li", "dt"], w=["ang"])
                S.op("dve", lambda: V.tensor_scalar(angc[:], ang[:], math.pi / 2, None, ALU.add), r=["ang"], w=["angc"])
                for a_, key in ((ang, "ang"), (angc, "angc")):
                    for _ in range(4):
                        S.op("dve", lambda a_=a_: V.tensor_scalar(tm[:], a_[:], math.pi, 2 * math.pi, ALU.is_gt, ALU.mult),
                             r=[key], w=["tm"])
                        S.op("dve", lambda a_=a_: V.tensor_tensor(a_[:], a_[:], tm[:], ALU.subtract), r=["tm", key], w=[key])
                S.op("act", lambda: nc.scalar.activation(ang[:], ang[:], AF.Sin), r=["ang"], w=["ang"])
                S.op("act", lambda: nc.scalar.activation(angc[:], angc[:], AF.Sin), r=["angc"], w=["angc"])
                for dup in range(2):
                    S.op("dve", lambda dup=dup: V.tensor_tensor(T4[:, 0, dup, :], mag[:], angc[:], ALU.mult),
                         r=["mag", "angc"], w=["T4"])
                    S.op("dve", lambda dup=dup: V.tensor_tensor(T4[:, 1, dup, :], mag[:], ang[:], ALU.mult),
                         r=["mag", "ang"], w=["T4"])
                a_n = T4[:, 0, 0, :]
                b_n = T4[:, 1, 0, :]
                S.op("dve", lambda: V.tensor_scalar(nr[:], a_n, -1.0, None, ALU.add), r=["T4"], w=["nr"])
                S.op("dve", lambda: V.tensor_tensor(tm[:], lr[:], lr[:], ALU.mult), r=["lr"], w=["tm"])
                S.op("dve", lambda: V.tensor_tensor(tm2[:], li[:], li[:], ALU.mult), r=["li"], w=["tm2"])
                S.op("dve", lambda: V.tensor_tensor(tm[:], tm[:], tm2[:], ALU.add), r=["tm", "tm2"], w=["tm"])
                S.op("dve", lambda: V.reciprocal(rden[:], tm[:]), r=["tm"], w=["rden"])
                S.op("dve", lambda: V.tensor_tensor(tm[:], nr[:], lr[:], ALU.mult), r=["nr", "lr"], w=["tm"])
                S.op("dve", lambda: V.tensor_tensor(tm2[:], b_n, li[:], ALU.mult), r=["T4", "li"], w=["tm2"])
                S.op("dve", lambda: V.tensor_tensor(tm[:], tm[:], tm2[:], ALU.add), r=["tm", "tm2"], w=["tm"])
                for dup in range(2):
                    S.op("dve", lambda dup=dup: V.tensor_tensor(T4[:, 2, dup, :], tm[:], rden[:], ALU.mult),
                         r=["tm", "rden"], w=["T4"])
                S.op("dve", lambda: V.tensor_tensor(tm[:], b_n, lr[:], ALU.mult), r=["T4", "lr"], w=["tm"])
                S.op("dve", lambda: V.tensor_tensor(tm2[:], nr[:], li[:], ALU.mult), r=["nr", "li"], w=["tm2"])
                S.op("dve", lambda: V.tensor_tensor(tm[:], tm[:], tm2[:], ALU.subtract), r=["tm", "tm2"], w=["tm"])
                for dup in range(2):
                    S.op("dve", lambda dup=dup: V.tensor_tensor(T4[:, 3, dup, :], tm[:], rden[:], ALU.mult),
                         r=["tm", "rden"], w=["T4"])
                ck(3)
                for q in range(4):
                    S.op("pe", lambda q=q: nc.tensor.transpose(C.ps[0][:, 0:128], T4[:, q, :, :].rearrange("g a p -> g (a p)"), C.ident[:]),
                         r=["T4", "ident"], w=[("ps", 0)])
                    S.op("dve", lambda q=q: V.tensor_copy(S4[:, q, :], C.ps[0][:, 0:128]), r=[("ps", 0)], w=["S4"])
                for hh in range(2):
                    ps_ = slice(hh * 64, (hh + 1) * 64)
                    for ri in range(2):
                        S.op("dve", lambda hh=hh, ps_=ps_, ri=ri: V.tensor_copy(C.A1[ps_, ri, :], S4[ps_, 0, hh::2]),
                             r=["S4"], w=["A1"])
                    S.op("dve", lambda hh=hh, ps_=ps_: V.tensor_scalar(C.A2[ps_, 0, :], S4[ps_, 1, hh::2], -1.0, None, ALU.mult),
                         r=["S4"], w=["A2"])
                    S.op("dve", lambda hh=hh, ps_=ps_: V.tensor_copy(C.A2[ps_, 1, :], S4[ps_, 1, hh::2]),
                         r=["S4"], w=["A2"])
                Ep = E0(nc.sbuf_tensor("Ep", [128, 3, 2, NPAIR], F32))
                mr = E0(nc.sbuf_tensor("mr", [128, NPAIR], F32))
                mi = E0(nc.sbuf_tensor("mi", [128, NPAIR], F32))
                q1 = E0(nc.sbuf_tensor("q1", [128, NPAIR], F32))
                q2 = E0(nc.sbuf_tensor("q2", [128, NPAIR], F32))
                q3 = E0(nc.sbuf_tensor("q3", [128, NPAIR], F32))
                S.dma("sp", Ep[:], eprev, w=["Ep"])
                S.op("dve", lambda: V.tensor_copy(mr[:], C.A1[:, 0, :]), r=["A1"], w=["mr"])
                S.op("dve", lambda: V.tensor_copy(mi[:], C.A2[:, 1, :]), r=["A2"], w=["mi"])
                for lv in range(10):
                    if lv < 7:
                        S.op("dve", lambda lv=lv: V.tensor_copy(C.TA1[:, lv, 0, :], mr[:]), r=["mr"], w=["TA"])
                        S.op("dve", lambda lv=lv: V.tensor_copy(C.TA1[:, lv, 1, :], mr[:]), r=["mr"], w=["TA"])
                        S.op("dve", lambda lv=lv: V.tensor_scalar(C.TA2[:, lv, 0, :], mi[:], -1.0, None, ALU.mult), r=["mi"], w=["TA"])
                        S.op("dve", lambda lv=lv: V.tensor_copy(C.TA2[:, lv, 1, :], mi[:]), r=["mi"], w=["TA"])
                    S.op("dve", lambda: V.tensor_tensor(q1[:], mr[:], mr[:], ALU.mult), r=["mr"], w=["q1"])
                    S.op("dve", lambda: V.tensor_tensor(q2[:], mi[:], mi[:], ALU.mult), r=["mi"], w=["q2"])
                    S.op("dve", lambda: V.tensor_tensor(q3[:], mr[:], mi[:], ALU.mult), r=["mr", "mi"], w=["q3"])
                    S.op("dve", lambda: V.tensor_tensor(mr[:], q1[:], q2[:], ALU.subtract), r=["q1", "q2"], w=["mr"])
                    S.op("dve", lambda: V.tensor_scalar(mi[:], q3[:], 2.0, None, ALU.mult), r=["q3"], w=["mi"])
                S.op("dve", lambda: V.tensor_copy(C.Xc[:], Ep[:, 2]), r=["Ep"], w=["Xc"])
                for k in (1, 0):
                    xr, xi = C.Xc[:, 0, :], C.Xc[:, 1, :]
                    S.op("dve", lambda: V.tensor_tensor(q1[:], mr[:], xr, ALU.mult), r=["mr", "Xc"], w=["q1"])
                    S.op("dve", lambda: V.tensor_tensor(q2[:], mi[:], xi, ALU.mult), r=["mi", "Xc"], w=["q2"])
                    S.op("dve", lambda: V.tensor_tensor(q1[:], q1[:], q2[:], ALU.subtract), r=["q1", "q2"], w=["q1"])
                    S.op("dve", lambda: V.tensor_tensor(q2[:], mr[:], xi, ALU.mult), r=["mr", "Xc"], w=["q2"])
                    S.op("dve", lambda: V.tensor_tensor(q3[:], mi[:], xr, ALU.mult), r=["mi", "Xc"], w=["q3"])
                    S.op("dve", lambda: V.tensor_tensor(q2[:], q2[:], q3[:], ALU.add), r=["q2", "q3"], w=["q2"])
                    S.op("dve", lambda k=k: V.tensor_tensor(C.Xc[:, 0, :], q1[:], Ep[:, k, 0, :], ALU.add), r=["q1", "Ep"], w=["Xc"])
                    S.op("dve", lambda k=k: V.tensor_tensor(C.Xc[:, 1, :], q2[:], Ep[:, k, 1, :], ALU.add), r=["q2", "Ep"], w=["Xc"])
                ck(4)
                fre = S4[:, 2, :].unsqueeze(2).broadcast_to([128, 128, 16])
                fim = S4[:, 3, :].unsqueeze(2).broadcast_to([128, 128, 16])
                S.op("dve", lambda: V.tensor_tensor(Bb[:, 0], Bz[:, 0], fre, ALU.mult), r=["Bz", "S4"], w=["Bb"])
                S.op("dve", lambda: V.tensor_tensor(tb1[:], Bz[:, 1], fim, ALU.mult), r=["Bz", "S4"], w=["tb1"])
                S.op("dve", lambda: V.tensor_tensor(Bb[:, 0], Bb[:, 0], tb1[:], ALU.subtract), r=["tb1", "Bb"], w=["Bb"])
                S.op("dve", lambda: V.tensor_tensor(Bb[:, 1], Bz[:, 1], fre, ALU.mult), r=["Bz", "S4"], w=["Bb"])
                S.op("dve", lambda: V.tensor_tensor(tb1[:], Bz[:, 0], fim, ALU.mult), r=["Bz", "S4"], w=["tb1"])
                S.op("dve", lambda: V.tensor_tensor(Bb[:, 1], Bb[:, 1], tb1[:], ALU.add), r=["tb1", "Bb"], w=["Bb"])
                mk = C.mask[:].rearrange("p (g c) -> p g c", c=16)
                for ri in range(2):
                    for ch in range(NCH):
                        S.op("dve", lambda ri=ri, ch=ch: V.tensor_tensor(
                            Bb[:, ri, ch * 8:(ch + 1) * 8, :], Bb[:, ri, ch * 8:(ch + 1) * 8, :], mk, ALU.mult),
                            r=["Bb", "mask"], w=["Bb"])
                ck(5)
                for ri in range(2):
                    for ch in range(NCH):
                        pi = (ri * NCH + ch) % 4
                        S.op("pe", lambda ri=ri, ch=ch, pi=pi: nc.tensor.transpose(
                            C.ps[pi][:, 0:128], Bb[:, ri, ch * 8:(ch + 1) * 8, :].rearrange("p g c -> p (g c)"), C.ident[:]),
                            r=["Bb", "ident"], w=[("ps", pi)])
                        S.op("act", lambda ri=ri, ch=ch, pi=pi: nc.scalar.copy(C.lbu[:, ch, ri, :], C.ps[pi][:, 0:128]),
                             r=[("ps", pi)], w=["lbu"])
                for ri in range(2):
                    for ch in range(NCH):
                        pi = (ri * NCH + ch) % 4
                        S.op("pe", lambda ri=ri, ch=ch, pi=pi: nc.tensor.transpose(
                            C.ps[pi][:, 0:128], Cin[:, ri, ch, :], C.ident[:]),
                            r=["Cin", "ident"], w=[("ps", pi)])
                        if ri == 0:
                            S.op("dve", lambda ri=ri, ch=ch, pi=pi: V.tensor_tensor(
                                C.cz[:, ch, ri, :], C.ps[pi][:, 0:128], C.mask[:], ALU.mult),
                                r=[("ps", pi), "mask"], w=["cz"])
                        else:
                            S.op("dve", lambda ri=ri, ch=ch, pi=pi: V.scalar_tensor_tensor(
                                C.cz[:, ch, ri, :], C.ps[pi][:, 0:128], -1.0, C.mask[:], ALU.mult, ALU.mult),
                                r=[("ps", pi), "mask"], w=["cz"])
                for ch in range(NCH):
                    S.op("dve", lambda ch=ch: V.tensor_scalar(C.dg[:, ch, :], C.ident[:], C.gn[:, 3, ch:ch + 1], None, ALU.mult),
                         r=["ident", "gains"], w=["dg"])
                S.barrier()

            ck(6)
            rmsnorm(S, nc, C, C.gn[:, 0, :])

            ck(7)
            with ExitStack() as st1:
                E1 = st1.enter_context
                OFF = 0 if state_only else 1
                Bf = E1(nc.sbuf_tensor("Bf", [128, 2, NPAIR, TBS + OFF], F32))
                if not state_only:
                    trs = E1(nc.sbuf_tensor("trs", [128, 2, NPAIR, TBS // 2], F32))
                t1 = E1(nc.sbuf_tensor("t1", [128, 2, NPAIR], F32))
                t2 = E1(nc.sbuf_tensor("t2", [128, 2, NPAIR], F32))
                if state_only:
                    tr1 = E1(nc.sbuf_tensor("tr1", [128, 2, NPAIR, TBS // 2], F32))
                    tr2 = E1(nc.sbuf_tensor("tr2", [128, 2, NPAIR, TBS // 2], F32))
                if not state_only:
                    Xb = E1(nc.sbuf_tensor("Xb", [128, 2, NPAIR, TBS], BF16))
                    ysb = E1(nc.sbuf_tensor("ysb", [128, 512], F32))
                    y2 = E1(nc.sbuf_tensor("y2", [128, 512], F32))
                    y3 = E1(nc.sbuf_tensor("y3", [128, 512], F32))
                V = nc.vector
                nblk = int(os.environ.get("L0_NBLK", T // TBS))
                for blk in range(nblk):
                    tsl = slice(blk * TBS, (blk + 1) * TBS)
                    if not state_only:
                        if blk == 0:
                            S.op("dve", lambda: V.tensor_copy(Bf[:, :, :, 0], C.Xc[:]), r=["Xc", "Bf"], w=["Bf"])
                        else:
                            S.op("dve", lambda: V.tensor_copy(Bf[:, :, :, 0], Bf[:, :, :, TBS]), r=["Bf"], w=["Bf"])
                    for ri in range(2):
                        for half in range(2):
                            fns = []
                            for cc in range(8):
                                ch = half * 8 + cc
                                for j in range(4):
                                    fns.append(lambda cc=cc, ch=ch, j=j, ri=ri: nc.tensor.matmul(
                                        C.ps[j][:, cc * TBS:(cc + 1) * TBS],
                                        C.lbu[32 * j:32 * j + 32, ch, ri, :], C.h[32 * j:32 * j + 32, ch, tsl],
                                        start=True, stop=True, tile_position=(32 * j, 0)))
                            S.group("pe", fns, r=["lbu", "h"], w=[("ps", 0), ("ps", 1), ("ps", 2), ("ps", 3)])
                            for j in range(4):
                                S.op("act", lambda j=j, ri=ri, half=half: nc.scalar.copy(
                                    Bf[:, ri, half * 32 + j:half * 32 + 32:4, OFF:OFF + TBS],
                                    C.ps[j][:].rearrange("p (a t) -> p a t", t=TBS)),
                                    r=[("ps", j)], w=["Bf"])
                    if state_only:
                        for lv in range(6):
                            sgl = 1 << lv
                            m = TBS >> (lv + 1)
                            Lv = Bf[:, :, :, sgl - 1::2 * sgl]
                            Rv = Bf[:, :, :, 2 * sgl - 1::2 * sgl]
                            a1 = C.TA1[:, lv].unsqueeze(3).broadcast_to([128, 2, NPAIR, m])
                            S.op("dve", lambda: V.tensor_tensor(tr1[:, :, :, 0:m], Lv, a1, ALU.mult), r=["Bf", "TA"], w=["tr1"])
                            for ri in range(2):
                                a2 = C.TA2[:, lv, ri, :].unsqueeze(2).broadcast_to([128, NPAIR, m])
                                S.op("dve", lambda ri=ri, a2=a2: V.tensor_tensor(
                                    tr2[:, ri, :, 0:m], Bf[:, 1 - ri, :, sgl - 1::2 * sgl], a2, ALU.mult),
                                    r=["Bf", "TA"], w=["tr2"])
                            S.op("dve", lambda: V.tensor_tensor(Rv, Rv, tr1[:, :, :, 0:m], ALU.add), r=["tr1", "Bf"], w=["Bf"])
                            S.op("dve", lambda: V.tensor_tensor(Rv, Rv, tr2[:, :, :, 0:m], ALU.add), r=["tr2", "Bf"], w=["Bf"])
                        S.op("dve", lambda: V.tensor_tensor(t1[:], C.TA1[:, 6], C.Xc[:], ALU.mult), r=["Xc", "TA"], w=["t1"])
                        for ri in range(2):
                            S.op("dve", lambda ri=ri: V.tensor_tensor(t2[:, ri, :], C.TA2[:, 6, ri, :], C.Xc[:, 1 - ri, :], ALU.mult),
                                 r=["Xc", "TA"], w=["t2"])
                        S.op("dve", lambda: V.tensor_tensor(C.Xc[:], t1[:], t2[:], ALU.add), r=["t1", "t2"], w=["Xc"])
                        S.op("dve", lambda: V.tensor_tensor(C.Xc[:], C.Xc[:], Bf[:, :, :, TBS - 1], ALU.add), r=["Bf", "Xc"], w=["Xc"])
                        continue
                    if not state_only:
                        def cmadd(lv, Lsl, Msl, m):
                            a1 = C.TA1[:, lv].unsqueeze(3).broadcast_to([128, 2, NPAIR, m])
                            S.op("dve", lambda: V.tensor_tensor(trs[:, :, :, 0:m], Bf[:, :, :, Lsl], a1, ALU.mult),
                                 r=["Bf", "TA"], w=["trs"])
                            S.op("dve", lambda: V.tensor_tensor(Bf[:, :, :, Msl], Bf[:, :, :, Msl], trs[:, :, :, 0:m], ALU.add),
                                 r=["trs", "Bf"], w=["Bf"])
                            for ri in range(2):
                                a2 = C.TA2[:, lv, ri, :].unsqueeze(2).broadcast_to([128, NPAIR, m])
                                S.op("dve", lambda ri=ri, a2=a2: V.tensor_tensor(
                                    trs[:, ri, :, 0:m], Bf[:, 1 - ri, :, Lsl], a2, ALU.mult), r=["Bf", "TA"], w=["trs"])
                            S.op("dve", lambda: V.tensor_tensor(Bf[:, :, :, Msl], Bf[:, :, :, Msl], trs[:, :, :, 0:m], ALU.add),
                                 r=["trs", "Bf"], w=["Bf"])
                        for lv in range(6):
                            sgl = 1 << lv
                            cmadd(lv, slice(sgl, TBS + 1, 2 * sgl), slice(2 * sgl, TBS + 1, 2 * sgl), TBS >> (lv + 1))
                        for lv in range(6, -1, -1):
                            sgl = 1 << lv
                            m = max(1, TBS >> (lv + 1))
                            cmadd(lv, slice(0, TBS, 2 * sgl), slice(sgl, TBS + 1, 2 * sgl), m)
                        S.op("dve", lambda: V.tensor_copy(C.Xc[:], Bf[:, :, :, TBS]), r=["Bf"], w=["Xc"])
                        S.op("act", lambda: nc.scalar.copy(Xb[:], Bf[:, :, :, 1:TBS + 1]), r=["Bf"], w=["Xb"])
                    for t in ([] if not state_only else []):
                        pass
                    for t in range(0):
                        prev = C.Xc[:] if t == 0 else Bf[:, :, :, t - 1]
                        cur = Bf[:, :, :, t]
                        S.op("dve", lambda prev=prev: V.tensor_tensor(t1[:], C.A1[:], prev, ALU.mult),
                             r=["Bf", "Xc", "A1"], w=["t1"])
                        for ri in range(2):
                            pv = C.Xc[:, 1 - ri, :] if t == 0 else Bf[:, 1 - ri, :, t - 1]
                            S.op("dve", lambda pv=pv, ri=ri: V.tensor_tensor(t2[:, ri, :], C.A2[:, ri, :], pv, ALU.mult),
                                 r=["Bf", "Xc", "A2"], w=["t2"])
                        S.op("dve", lambda cur=cur: V.tensor_tensor(cur, cur, t1[:], ALU.add), r=["t1", "Bf"], w=["Bf"])
                        S.op("dve", lambda cur=cur: V.tensor_tensor(cur, cur, t2[:], ALU.add), r=["t2", "Bf"], w=["Bf"])
                    for half in range(2):
                        pi = 4 + half
                        fns = []
                        for cc in range(8):
                            ch = half * 8 + cc
                            osl = slice(cc * TBS, (cc + 1) * TBS)
                            fns.append(lambda ch=ch, osl=osl, pi=pi: nc.tensor.matmul(
                                C.ps[pi][:, osl], C.dg[:, ch, :], C.h[:, ch, tsl], start=(ch % 8 == 0), stop=False))
                            for j in range(4):
                                pair = ch * 4 + j
                                for ri in range(2):
                                    fns.append(lambda ch=ch, osl=osl, pi=pi, j=j, ri=ri, pair=pair: nc.tensor.matmul(
                                        C.ps[pi][32 * j:32 * j + 32, osl], C.cz[:, ch, ri, 32 * j:32 * j + 32],
                                        Xb[:, ri, pair, :], start=False, stop=(ri == 1 and j == 3),
                                        tile_position=(0, 32 * j), skip_group_check=True))
                        S.group("pe", fns, r=["dg", "h", "cz", "Xb"], w=[("ps", pi)])
                        yp = C.ps[pi]
                        S.op("act", lambda yp=yp: nc.scalar.copy(ysb[:], yp[:]), r=[("ps", pi)], w=["ysb"])
                        S.op("dve", lambda: V.tensor_tensor(y2[:], ysb[:], ysb[:], ALU.mult), r=["ysb"], w=["y2"])
                        S.op("dve", lambda: V.tensor_scalar(y2[:], y2[:], 0.044715, 1.0, ALU.mult, ALU.add), r=["y2"], w=["y2"])
                        S.op("dve", lambda: V.tensor_tensor(y2[:], y2[:], ysb[:], ALU.mult), r=["y2", "ysb"], w=["y2"])
                        S.op("act", lambda: nc.scalar.activation(y3[:], y2[:], AF.Sigmoid, scale=1.5957691216057308),
                             r=["y2"], w=["y3"])
                        S.op("dve", lambda half=half: V.tensor_tensor(
                            C.h[:, half * 8:(half + 1) * 8, tsl], ysb[:].rearrange("p (a t) -> p a t", t=TBS),
                            y3[:].rearrange("p (a t) -> p a t", t=TBS), ALU.mult),
                            r=["ysb", "y3", "h"], w=["h"])
                S.dma("sp", xend, C.Xc[:], r=["Xc"], w=["xend_d"])
                S.barrier()
            st01.close()

            if not state_only:
                with ExitStack() as st2:
                    E2 = st2.enter_context
                    C.stg = [E2(nc.sbuf_tensor(f"stg{i}", [128, 2048], F32)) for i in range(3)]
                    C.stg_i = 0
                    C.wbuf = [E2(nc.sbuf_tensor(f"wb{i}", [128, NCH, 128], BF16)) for i in range(4)]
                    C.wb_i = 0
                    C.pre = {}
                    C.ps_i = 0
                    C.lin_ps = [0, 1, 2, 3]
                    C.out_ps = [4, 5]
                    wo = [E2(nc.sbuf_tensor(f"wo{i}", [128, 2048], BF16)) for i in range(4)]
                    hid = [E2(nc.sbuf_tensor(f"hid{i}", [128, T], BF16)) for i in range(4)]
                    sg = [E2(nc.sbuf_tensor(f"sg{i}", [128, 512], F32)) for i in range(2)]
                    val = [E2(nc.sbuf_tensor(f"val{i}", [128, 512], F32)) for i in range(2)]
                    V = nc.vector
                    st_ = {}

                    def glu_consume(oc, tb, pst, pk):
                        sl = slice(tb * 512, (tb + 1) * 512)
                        if oc % 2 == 0:
                            i = tb
                            S.op("act", lambda: nc.scalar.copy(val[i][:], pst[:]), r=[pk], w=[("val", i)])
                        else:
                            i = tb
                            c = oc // 2
                            S.op("act", lambda: nc.scalar.activation(sg[i][:], pst[:], AF.Sigmoid), r=[pk], w=[("sg", i)])
                            S.op("dve", lambda: V.tensor_tensor(sg[i][:], sg[i][:], val[i][:], ALU.mult),
                                 r=[("sg", i), ("val", i)], w=[("sg", i)])
                            S.op("dve", lambda: V.tensor_tensor(C.x[:, c, sl], C.x[:, c, sl], sg[i][:], ALU.add),
                                 r=[("sg", i), "x"], w=["x"])

                    for c in range(NCH):
                        for which in range(2):
                            nx = (w_glu, D + c * 128) if which == 0 else ((w_glu, (c + 1) * 128) if c + 1 < NCH else None)
                            linear_fm(S, nc, C, w_glu, 1, which * D + c * 128, C.h, "h",
                                      lambda oc, tb, pst, pk, which=which, c=c: glu_consume(2 * c + which, tb, pst, pk), nxt=nx)
                    rmsnorm(S, nc, C, C.gn[:, 1, :])
                    ffn(S, nc, C, w_in, w_out, wo, hid, sg)
                    for c in range(NCH):
                        S.dma("sp", x1T[c * 128:(c + 1) * 128, :], C.x[:, c, :], r=["x"], w=["x1_d"])
                    rmsnorm(S, nc, C, C.gn[:, 2, :])
                    kvb = [E2(nc.sbuf_tensor(f"kvb{i}", [128, 512], BF16)) for i in range(2)]

                    def kv_consume(oc, tb, pst, pk):
                        sl = slice(tb * 512, (tb + 1) * 512)
                        i = (oc * 2 + tb) % 2
                        S.op("act", lambda: nc.scalar.copy(kvb[i][:], pst[:]), r=[pk], w=[("kvb", i)])
                        dst = kT if oc < 4 else vT
                        o4 = oc % 4
                        S.dma("sp", dst[o4 * 128:(o4 + 1) * 128, sl], kvb[i][:], r=[("kvb", i)], w=["kv_d"])
                    linear_fm(S, nc, C, w_kv, 8, 0, C.h, "h", kv_consume)
                    S.barrier()
        except _Stop:
            pass
        S.finish()
    return nc


def _key_sets():
    ks = []
    for kb in range(-1, 8):
        k0 = 128 * kb
        qs, qe = max(0, k0), min(T, k0 + 256)
        ks.append((k0 + 2048, 1, 128, qs, 1, qe - qs, qs - k0, 1 if kb < 0 else 0, 0))
    for r in range(4):
        for kc in (-128, 0, 128):
            qs, qe = max(0, kc), min(256, kc + 256)
            ks.append((r + 4 * kc + 2048, 4, 128, r + 4 * qs, 4, qe - qs, qs - kc, 1 if kc < 0 else 0, 1))
    for r in range(16):
        ks.append((r, 16, 128, r, 16, 64, 128, 2, 2))
        ks.append((r + 2048, 16, 64, r, 16, 64, 0, 0, 2))
    return ks


def build_l1():
    nc = bass.Bass("TRN2", target_bir_lowering=False)
    xT = _dram_in(nc, "xT", [D, T])
    kTa = _dram_in(nc, "kTa", [512, 3072], BF16)
    Va = _dram_in(nc, "Va", [3072, 512], BF16)
    gains = _dram_in(nc, "gains", [128, 4, NCH])
    bandm = _dram_in(nc, "bandm", [128, 256], BF16)
    kbias = _dram_in(nc, "kbias", [128, 4])
    w_q = _dram_in(nc, "w_q", [D, 3 * D])
    w_o = _dram_in(nc, "w_o", [D, D])
    w_in = _dram_in(nc, "w_in", [D, 2 * FF])
    w_out = _dram_in(nc, "w_out", [FF, D])
    outT = _dram_out(nc, "outT", [D, T])
    KS = _key_sets()
    with ExitStack() as st:
        E = st.enter_context
        S = Sched(nc, st)
        C = Ctx()
        V = nc.vector
        C.x = E(nc.sbuf_tensor("x", [128, NCH, T], F32))
        C.h = E(nc.sbuf_tensor("h", [128, NCH, T], BF16))
        C.ones = E(nc.sbuf_tensor("ones", [128, 128], BF16))
        C.gn = E(nc.sbuf_tensor("gn", [128, 4, NCH], F32))
        C.epsc = E(nc.sbuf_tensor("epsc", [128, 1], F32))
        C.rstd = E(nc.sbuf_tensor("rstd", [128, 512], F32))
        C.sq = [E(nc.sbuf_tensor(f"sq{i}", [128, 512], BF16)) for i in range(2)]
        C.ps = [E(nc.psum_tensor(f"ps{i}", [128, 512], F32)) for i in range(8)]
        S.op("dve", lambda: V.memset(C.ones[:], 1.0), w=["ones"])
        S.op("dve", lambda: V.memset(C.epsc[:], EPS), w=["epsc"])
        S.dma("sp", C.gn[:], gains, w=["gains"])
        for c in range(NCH):
            S.dma("sp", C.x[:, c, :], xT[c * 128:(c + 1) * 128, :], w=["x"])
        rmsnorm(S, nc, C, C.gn[:, 0, :])
        with ExitStack() as sa:
            EA = sa.enter_context
            attn = EA(nc.sbuf_tensor("attn", [128, NCH, T], BF16))
            with ExitStack() as sb:
                EB = sb.enter_context
                C.stg = [EB(nc.sbuf_tensor(f"stg{i}", [128, 2048], F32)) for i in range(3)]
                C.stg_i = 0
                C.wbuf = [EB(nc.sbuf_tensor(f"wb{i}", [128, NCH, 128], BF16)) for i in range(3)]
                C.wb_i = 0
                C.pre = {}
                C.ps_i = 0
                C.lin_ps = [6, 7]
                mask = EB(nc.sbuf_tensor("bandS", [128, 256], BF16))
                kb_ = EB(nc.sbuf_tensor("kbS", [128, 4], F32))
                kts = EB(nc.sbuf_tensor("kts", [128, 3072], BF16))
                Vt = EB(nc.sbuf_tensor("Vt", [128, len(KS), 128], BF16))
                qg = [EB(nc.sbuf_tensor(f"qg{i}", [128, T], BF16)) for i in range(3)]
                pe_ = [EB(nc.sbuf_tensor(f"pe{i}", [128, 256], BF16)) for i in range(4)]
                pT = [EB(nc.sbuf_tensor(f"pT{i}", [128, 256], BF16)) for i in range(4)]
                rec = EB(nc.sbuf_tensor("rec", [128, 512], F32))
                S.dma("sp", mask[:], bandm, w=["mask"])
                S.dma("sp", kb_[:], kbias, w=["kb"])
                it = 0
                for hh in range(16):
                    kvh = hh // 4
                    if hh % 4 == 0:
                        S.dma("sp", kts[:], kTa[kvh * 128:(kvh + 1) * 128, :], w=["kts"])
                        for i, (k0, kst, m, q0, qst, n, off, bc, g) in enumerate(KS):
                            S.dma("sp", Vt[0:m, i, :], Va[k0:k0 + kst * (m - 1) + 1:kst, kvh * 128:(kvh + 1) * 128],
                                  w=["Vt"])
                    for g in range(3):
                        def q_consume(oc, tb, pst, pk, g=g):
                            sl = slice(tb * 512, (tb + 1) * 512)
                            S.op("act", lambda: nc.scalar.activation(qg[g][:, sl], pst[:], AF.Copy, scale=HD_SCALE),
                                 r=[pk], w=[("qg", g)])
                        nx = (w_q, (g + 1) * D + hh * 128) if g < 2 else ((w_q, (hh + 1) * 128) if hh < 15 else None)
                        linear_fm(S, nc, C, w_q, 1, g * D + hh * 128, C.h, "h", q_consume, nxt=nx)
                    first = {0: True, 1: True}
                    LOOK = 1
                    nks = len(KS)

                    def stage_a(i):
                        (k0, kst, m, q0, qst, n, off, bc, g) = KS[i]
                        slot = (hh * nks + i) % 2
                        sps = C.ps[4 + slot]
                        S.group("pe", [lambda: nc.tensor.matmul(
                            sps[0:m, 0:n], kts[:, k0:k0 + kst * (m - 1) + 1:kst],
                            qg[g][:, q0:q0 + qst * (n - 1) + 1:qst], start=True, stop=True)],
                            r=["kts", ("qg", g)], w=[("s", slot)])
                        S.op("act", lambda: nc.scalar.activation(pe_[slot][0:m, 0:n], sps[0:m, 0:n], AF.Exp,
                                                                 bias=kb_[0:m, bc:bc + 1], scale=1.0),
                             r=[("s", slot), "kb"], w=[("pe", slot)])
                        S.op("dve", lambda: V.tensor_tensor(pT[slot][0:m, 0:n], pe_[slot][0:m, 0:n], mask[0:m, off:off + n], ALU.mult),
                             r=[("pe", slot), "mask"], w=[("pT", slot)])

                    def stage_b(i):
                        (k0, kst, m, q0, qst, n, off, bc, g) = KS[i]
                        slot = (hh * nks + i) % 2
                        segs = []
                        isplit = min(n, max(0, -(-(512 - q0) // qst)))
                        if isplit > 0:
                            segs.append((0, isplit, 0))
                        if isplit < n:
                            segs.append((isplit, n, 1))
                        fns = []
                        wk = []
                        for (i0, i1, bank) in segs:
                            c0 = q0 + qst * i0 - 512 * bank
                            cs = slice(c0, c0 + qst * (i1 - i0 - 1) + 1, qst)
                            st_ = first[bank]
                            first[bank] = False
                            fns.append(lambda cs=cs, bank=bank, i0=i0, i1=i1, st_=st_: nc.tensor.matmul(
                                C.ps[bank][:, cs], Vt[0:m, i, :], pT[slot][0:m, i0:i1], start=st_, stop=False,
                                skip_group_check=True))
                            fns.append(lambda cs=cs, bank=bank, i0=i0, i1=i1, st_=st_: nc.tensor.matmul(
                                C.ps[2 + bank][:, cs], C.ones[0:m, :], pT[slot][0:m, i0:i1], start=st_, stop=False,
                                skip_group_check=True))
                            wk += [("ps", bank), ("ps", 2 + bank)]
                        S.group("pe", fns, r=[("pT", slot), "Vt", "ones"], w=wk)

                    for step in range(nks + LOOK):
                        if step < nks:
                            stage_a(step)
                        if step - LOOK >= 0:
                            stage_b(step - LOOK)
                    for bank in range(2):
                        sl = slice(bank * 512, (bank + 1) * 512)
                        S.op("dve", lambda: V.reciprocal(rec[:], C.ps[2 + bank][:]), r=[("ps", 2 + bank)], w=["rec"])
                        S.op("dve", lambda: V.tensor_tensor(attn[:, hh, sl], C.ps[bank][:], rec[:], ALU.mult),
                             r=[("ps", bank), "rec"], w=["attn"])
                S.barrier()
            with ExitStack() as sc:
                EC = sc.enter_context
                C.stg = [EC(nc.sbuf_tensor(f"stgc{i}", [128, 2048], F32)) for i in range(3)]
                C.stg_i = 0
                C.wbuf = [EC(nc.sbuf_tensor(f"wbc{i}", [128, NCH, 128], BF16)) for i in range(4)]
                C.wb_i = 0
                C.pre = {}
                C.lin_ps = [0, 1, 2, 3]

                def o_consume(oc, tb, pst, pk):
                    sl = slice(tb * 512, (tb + 1) * 512)
                    S.op("dve", lambda: V.tensor_tensor(C.x[:, oc, sl], C.x[:, oc, sl], pst[:], ALU.add),
                         r=[pk, "x"], w=["x"])
                linear_fm(S, nc, C, w_o, NCH, 0, attn, "attn", o_consume)
                S.barrier()
        with ExitStack() as sd:
            ED = sd.enter_context
            C.stg = [ED(nc.sbuf_tensor(f"stgd{i}", [128, 2048], F32)) for i in range(3)]
            C.stg_i = 0
            C.wbuf = [ED(nc.sbuf_tensor(f"wbd{i}", [128, NCH, 128], BF16)) for i in range(4)]
            C.wb_i = 0
            C.pre = {}
            C.lin_ps = [0, 1, 2, 3]
            C.out_ps = [4, 5]
            wo = [ED(nc.sbuf_tensor(f"wo{i}", [128, 2048], BF16)) for i in range(4)]
            hid = [ED(nc.sbuf_tensor(f"hid{i}", [128, T], BF16)) for i in range(4)]
            sg = [ED(nc.sbuf_tensor(f"sg{i}", [128, 512], F32)) for i in range(2)]
            rmsnorm(S, nc, C, C.gn[:, 1, :])
            ffn(S, nc, C, w_in, w_out, wo, hid, sg)
            rmsnorm(S, nc, C, C.gn[:, 2, :], hkey="x", dst=C.x)
            for c in range(NCH):
                S.dma("sp", outT[c * 128:(c + 1) * 128, :], C.x[:, c, :], r=["x"], w=["out_d"])
            S.barrier()
        S.finish()
    return nc


HD_SCALE = 128 ** -0.5

_cache = {}


def _prog(name, fn):
    if name not in _cache:
        _cache[name] = fn()
    return _cache[name]


def _fm(v):
    return np.ascontiguousarray(np.asarray(v, np.float32).reshape(NCH, 128).T)


def layer0(inputs):
    x = np.asarray(inputs["x"], np.float32)
    n = 8
    gains = np.ascontiguousarray(np.stack([_fm(inputs["a_norm_mix"][0]), _fm(inputs["ffn_norm"][0]),
                                           _fm(inputs["kv_norm"]), _fm(inputs["s5_d"][0])], axis=1))
    ident = np.eye(128, dtype=np.float32)
    hp = np.arange(128) // 64
    g8 = (np.arange(128) // 16) % 2
    mask_sm = (hp[:, None] == g8[None, :]).astype(np.float32)
    base = {
        "lam_re": np.ascontiguousarray(inputs["s5_lam_re"][0]), "lam_im": np.ascontiguousarray(inputs["s5_lam_im"][0]),
        "log_dt": np.ascontiguousarray(inputs["s5_log_dt"][0].reshape(128, 1)),
        "b_re": np.ascontiguousarray(inputs["s5_b_re"][0]), "b_im": np.ascontiguousarray(inputs["s5_b_im"][0]),
        "c_re": np.ascontiguousarray(inputs["s5_c_re"][0].reshape(2048, 64)),
        "c_im": np.ascontiguousarray(inputs["s5_c_im"][0].reshape(2048, 64)),
        "gains": gains, "ident": ident, "mask_sm": mask_sm,
    }
    xTs = []
    for c in range(n):
        b, j = c // 4, c % 4
        xTs.append(np.ascontiguousarray(x[b, j * T:(j + 1) * T, :].T))
    zero_e = np.zeros((128, 3, 2, NPAIR), np.float32)
    nc_s = _prog("l0s", lambda: build_l0(True))
    maps = [dict(base, xT=xTs[c], eprev=zero_e) for c in range(n)]
    res = run_bass_kernel_spmd(nc_s, maps, core_ids=list(range(n)))
    eprev = []
    for c in range(n):
        b, j = c // 4, c % 4
        e = np.zeros((128, 3, 2, NPAIR), np.float32)
        for k in range(3):
            if j - 1 - k >= 0:
                e[:, k] = np.asarray(res.results[b * 4 + j - 1 - k]["xend"])
        eprev.append(e)
    nc_f = _prog("l0f", lambda: build_l0(False))
    full = dict(base, w_glu=np.ascontiguousarray(inputs["s5_w_glu"][0]),
                w_in=np.ascontiguousarray(inputs["ffn_w_in"][0]), w_out=np.ascontiguousarray(inputs["ffn_w_out"][0]),
                w_kv=np.ascontiguousarray(inputs["w_kv"]))
    maps = [dict(full, xT=xTs[c], eprev=eprev[c]) for c in range(n)]
    res = run_bass_kernel_spmd(nc_f, maps, core_ids=list(range(n)))
    return res.results


def layer1(inputs, r0):
    n = 8
    gains = np.ascontiguousarray(np.stack([_fm(inputs["b_norm_mix"][0]), _fm(inputs["ffn_norm"][1]),
                                           _fm(inputs["final_norm"]), _fm(inputs["final_norm"])], axis=1))
    bdt = np.asarray(r0[0]["kT"]).dtype
    sidx = np.arange(128)[:, None]
    qidx = np.arange(256)[None, :]
    bandm = ((sidx <= qidx) & (qidx <= sidx + 128)).astype(np.float32).astype(bdt)
    base = {"gains": gains, "bandm": bandm,
            "w_q": np.ascontiguousarray(inputs["attn_w_q"][0]), "w_o": np.ascontiguousarray(inputs["attn_w_o"][0]),
            "w_in": np.ascontiguousarray(inputs["ffn_w_in"][1]), "w_out": np.ascontiguousarray(inputs["ffn_w_out"][1])}
    maps = []
    for c in range(n):
        b, j = c // 4, c % 4
        kparts, vparts = [], []
        for jj in (j - 2, j - 1, j):
            if jj < 0:
                kparts.append(np.zeros((512, T), bdt))
                vparts.append(np.zeros((512, T), bdt))
            else:
                kparts.append(np.asarray(r0[b * 4 + jj]["kT"]))
                vparts.append(np.asarray(r0[b * 4 + jj]["vT"]))
        kTa = np.ascontiguousarray(np.concatenate(kparts, axis=1))
        Va = np.ascontiguousarray(np.concatenate(vparts, axis=1).T)
        kbias = np.zeros((128, 4), np.float32)
        if j == 0:
            kbias[:, 1] = -30000.0
            kbias[:, 2] = -30000.0
        elif j == 1:
            kbias[:64, 2] = -30000.0
        maps.append(dict(base, xT=np.ascontiguousarray(r0[c]["x1T"]), kTa=kTa, Va=Va, kbias=kbias))
    nc1 = _prog("l1", build_l1)
    res = run_bass_kernel_spmd(nc1, maps, core_ids=list(range(n)))
    return res.results


def kernel(**inputs):
    inputs = {k: np.asarray(v) for k, v in inputs.items()}
    r0 = layer0(inputs)
    r1 = layer1(inputs, r0)
    out = np.empty((2, 4096, D), np.float32)
    for c in range(8):
        b, j = c // 4, c % 4
        out[b, j * T:(j + 1) * T, :] = np.asarray(r1[c]["outT"]).T
    return out
```
